# Optimizing a Trainium2 kernel written in Bass

```python
import math
import jax, jax.numpy as jnp
from jax import lax
import numpy as np

D_MODEL = 1024
BATCH = 2
SEQ = 8192
DEPTH = 2
DEC_BATCH = 16
DEC_SEQ = 64
PAST_LEN = 4096

CHUNK = 64
N_META = 16
N_EVEN = (DEPTH + 1) // 2
N_ODD = DEPTH // 2
NORM_EPS = 1e-6

HA_HEADS = 4
HA_DK = 128
HA_DV = 128
HB_HEADS = 4
HB_DQK = 64
HB_DV = 128
GATE_CAP = 15.0
HC_QK_HEADS = 8
HC_V_HEADS = 16
HC_DK = 128
HC_DV = 128
CONV_W = 4
GDN_CONV_DIM = 2 * HC_QK_HEADS * HC_DK + HC_V_HEADS * HC_DV
FF_HIDDEN = -(-8 * D_MODEL // (3 * 256)) * 256

EVEN_SIZES = (HA_HEADS * HA_DK, HA_HEADS * HA_DK, HA_HEADS * HA_DV, HA_HEADS * HA_DV,
              HB_HEADS * HB_DQK, HB_HEADS * HB_DQK, HB_HEADS * HB_DV, HB_HEADS * HB_DV,
              HB_HEADS, HB_HEADS)
EVEN_COLS = sum(EVEN_SIZES)
EVEN_OUT = HA_HEADS * HA_DV + HB_HEADS * HB_DV
ODD_SIZES = (GDN_CONV_DIM, HC_V_HEADS * HC_DV, HC_V_HEADS, HC_V_HEADS)
ODD_COLS = sum(ODD_SIZES)

kernel_name = "hgrn2_mlstm_gdn_streaming_step"


def _rmsnorm(x, w):
    xf = x.astype(jnp.float32)
    y = xf * lax.rsqrt(jnp.mean(xf * xf, axis=-1, keepdims=True) + NORM_EPS)
    return (y * w.astype(jnp.float32)).astype(x.dtype)


def _l2norm(x):
    return x * lax.rsqrt(jnp.sum(x * x, axis=-1, keepdims=True) + NORM_EPS)


def _split(a, sizes):
    offs, acc = [], 0
    for s in sizes[:-1]:
        acc += s
        offs.append(acc)
    return jnp.split(a, offs, axis=-1)


def _causal_conv(x, buf, w):
    L = x.shape[1]
    xp = jnp.concatenate([buf, x], axis=1)
    y = sum(xp[:, j:j + L] * w[j] for j in range(CONV_W))
    return y, xp[:, xp.shape[1] - (CONV_W - 1):]


def _chunked_scan(step, state, seqs, block):
    n = seqs[0].shape[1] // block
    xs = tuple(jnp.swapaxes(a.reshape(a.shape[0], n, block, *a.shape[2:]), 0, 1) for a in seqs)
    state, ys = lax.scan(step, state, xs)
    ys = jnp.swapaxes(ys, 0, 1)
    return state, ys.reshape(ys.shape[0], n * block, *ys.shape[3:])


def _run_segments(step, state, seqs, segments):
    outs = []
    for start, stop, block in segments:
        state, y = _chunked_scan(step, state, tuple(a[:, start:stop] for a in seqs), block)
        outs.append(y)
    return state, jnp.concatenate(outs, axis=1)


def _hgrn_step(S, inp):
    q, k, log_f, v = inp
    c = q.shape[1]
    b = jnp.cumsum(log_f, axis=1)
    causal = jnp.tril(jnp.ones((c, c), bool))
    diff = b[:, :, None] - b[:, None, :]
    dec = jnp.exp(jnp.where(causal[None, :, :, None, None], diff, -jnp.inf))
    scores = jnp.einsum('bthk,btshk,bshk->bhts', q, dec, k)
    o = jnp.einsum('bhts,bshv->bthv', scores, v) + jnp.einsum('bthk,bhkv->bthv', q * jnp.exp(b), S)
    b_last = b[:, -1]
    S = jnp.exp(b_last)[..., None] * S + jnp.einsum('bshk,bshv->bhkv', k * jnp.exp(b_last[:, None] - b), v)
    return S, o


def _mlstm_step(state, inp):
    C, n, m = state
    q, k, v, ig, log_f = inp
    c = q.shape[1]
    F = jnp.cumsum(log_f, axis=1)
    causal = jnp.tril(jnp.ones((c, c), bool))
    logD = jnp.where(causal[None, :, :, None], F[:, :, None] - F[:, None, :] + ig[:, None, :], -jnp.inf)
    log_inter = F + m[:, None]
    m_t = jnp.maximum(log_inter, jnp.max(logD, axis=2))
    D = jnp.exp(logD - m_t[:, :, None])
    w_inter = jnp.exp(log_inter - m_t)
    qk = jnp.einsum('bthd,bshd->btsh', q, k) * D
    num = jnp.einsum('btsh,bshv->bthv', qk, v) + w_inter[..., None] * jnp.einsum('bthd,bhdv->bthv', q, C)
    den = jnp.sum(qk, axis=2) + w_inter * jnp.einsum('bthd,bhd->bth', q, n)
    h = num / jnp.maximum(jnp.abs(den), jnp.exp(-m_t))[..., None]
    F_last = F[:, -1]
    m_new = m_t[:, -1]
    w_s = jnp.exp(F_last[:, None] - F + ig - m_new[:, None])
    decay = jnp.exp(F_last + m - m_new)
    C = decay[..., None, None] * C + jnp.einsum('bsh,bshd,bshv->bhdv', w_s, k, v)
    n = decay[..., None] * n + jnp.einsum('bsh,bshd->bhd', w_s, k)
    return (C, n, m_new), h


def _gdn_step(S, inp):
    q, k, v, beta, g = inp
    c = q.shape[1]
    G = jnp.cumsum(g, axis=1)
    Gh = jnp.swapaxes(G, 1, 2)
    causal = jnp.tril(jnp.ones((c, c), bool))
    strict = jnp.tril(jnp.ones((c, c), bool), -1)
    decay = jnp.exp(jnp.where(causal, Gh[..., :, None] - Gh[..., None, :], -jnp.inf))
    kb = k * beta[..., None]
    lower = jnp.where(strict, jnp.einsum('bthd,bshd->bhts', kb, k) * decay, 0.0)
    a_mat = lower + jnp.eye(c, dtype=lower.dtype)
    rhs = jnp.concatenate([jnp.swapaxes(v * beta[..., None], 1, 2),
                           jnp.swapaxes(kb * jnp.exp(G)[..., None], 1, 2)], axis=-1)
    sol = lax.linalg.triangular_solve(a_mat, rhs, left_side=True, lower=True, unit_diagonal=True)
    u, w = sol[..., :HC_DV], sol[..., HC_DV:]
    v_new = u - jnp.einsum('bhtk,bhkv->bhtv', w, S)
    attn = jnp.einsum('bthd,bshd->bhts', q, k) * decay
    o = jnp.einsum('bthk,bhkv->bhtv', q * jnp.exp(G)[..., None], S) + jnp.einsum('bhts,bhsv->bhtv', attn, v_new)
    G_last = G[:, -1]
    S = jnp.exp(G_last)[..., None, None] * S + jnp.einsum('bshk,bsh,bhsv->bhkv', k, jnp.exp(G_last[:, None] - G), v_new)
    return S, jnp.swapaxes(o, 1, 2)


def _even_mixer(h, e, state, segments, w_in, w_out, lb_logits, hgrn_norm, b_i, b_f, mlstm_norm):
    B, L, _ = h.shape
    f32 = jnp.float32
    p = (h @ w_in).astype(f32)
    qa, fa, ia, ga, qb, kb, vb, ob, ib, fb = _split(p, EVEN_SIZES)
    lb = jnp.cumsum(jax.nn.softmax(lb_logits.astype(f32), axis=0), axis=0)[e].reshape(HA_HEADS, HA_DK)
    fa = fa.reshape(B, L, HA_HEADS, HA_DK)
    log_f = jnp.log(lb + (1.0 - lb) * jax.nn.sigmoid(fa))
    k_a = (1.0 - lb) * jax.nn.sigmoid(-fa)
    q_a = jax.nn.silu(qa.reshape(B, L, HA_HEADS, HA_DK))
    S_a, o_a = _run_segments(_hgrn_step, state[0].astype(f32),
                             (q_a, k_a, log_f, ia.reshape(B, L, HA_HEADS, HA_DV)), segments)
    o_a = _rmsnorm(o_a, hgrn_norm.reshape(HA_HEADS, HA_DV)) * jax.nn.silu(ga.reshape(B, L, HA_HEADS, HA_DV))
    ig = GATE_CAP * jnp.tanh((ib + b_i.astype(f32)) / GATE_CAP)
    fg = GATE_CAP * jnp.tanh((fb + b_f.astype(f32)) / GATE_CAP)
    mstate = tuple(s.astype(f32) for s in state[1:])
    mstate, h_b = _run_segments(_mlstm_step, mstate,
                                (qb.reshape(B, L, HB_HEADS, HB_DQK),
                                 kb.reshape(B, L, HB_HEADS, HB_DQK) * HB_DQK ** -0.5,
                                 vb.reshape(B, L, HB_HEADS, HB_DV), ig, jax.nn.log_sigmoid(fg)), segments)
    o_b = _rmsnorm(h_b, mlstm_norm.reshape(HB_HEADS, HB_DV)) * jax.nn.sigmoid(ob.reshape(B, L, HB_HEADS, HB_DV))
    merged = jnp.concatenate([o_a.reshape(B, L, -1), o_b.reshape(B, L, -1)], axis=-1).astype(h.dtype)
    return merged @ w_out, (S_a, mstate[0], mstate[1], mstate[2])


def _odd_mixer(h, S0, conv0, segments, w_in, conv_w, a_log, dt_bias, norm_w, w_out):
    B, L, _ = h.shape
    f32 = jnp.float32
    p = (h @ w_in).astype(f32)
    qkv, z, b, a = _split(p, ODD_SIZES)
    qkv, conv_new = _causal_conv(qkv, conv0.astype(f32), conv_w.astype(f32))
    q, k, v = _split(jax.nn.silu(qkv), (HC_QK_HEADS * HC_DK, HC_QK_HEADS * HC_DK, HC_V_HEADS * HC_DV))
    rep = HC_V_HEADS // HC_QK_HEADS
    q = jnp.repeat(_l2norm(q.reshape(B, L, HC_QK_HEADS, HC_DK)), rep, axis=2) * HC_DK ** -0.5
    k = jnp.repeat(_l2norm(k.reshape(B, L, HC_QK_HEADS, HC_DK)), rep, axis=2)
    v = v.reshape(B, L, HC_V_HEADS, HC_DV)
    beta = jax.nn.sigmoid(b)
    g = -jnp.exp(a_log.astype(f32)) * jax.nn.softplus(a + dt_bias.astype(f32))
    S, o = _run_segments(_gdn_step, S0.astype(f32), (q, k, v, beta, g), segments)
    o = _rmsnorm(o, norm_w) * jax.nn.silu(z.reshape(B, L, HC_V_HEADS, HC_DV))
    return o.reshape(B, L, -1).astype(h.dtype) @ w_out, S, conv_new


def _swiglu(h, w_in, w_out):
    gate, up = jnp.split(h @ w_in, 2, axis=-1)
    return (jax.nn.silu(gate) * up) @ w_out


def _trunk(x, states, segments, norm_mix, norm_ffn, norm_final, even_w_in, even_w_out, hgrn_lb_logits,
           hgrn_norm, mlstm_b_i, mlstm_b_f, mlstm_norm, odd_w_in, odd_conv_w, gdn_a_log, gdn_dt_bias,
           gdn_norm, odd_w_out, ffn_w_in, ffn_w_out):
    S_h, C_m, n_m, m_m, S_g, conv_g = states
    nh, nC, nn_, nm, ng, nconv = [], [], [], [], [], []
    for layer in range(DEPTH):
        h = _rmsnorm(x, norm_mix[layer])
        if layer % 2 == 0:
            e = layer // 2
            mix, st = _even_mixer(h, e, (S_h[e], C_m[e], n_m[e], m_m[e]), segments, even_w_in[e], even_w_out[e],
                                  hgrn_lb_logits, hgrn_norm[e], mlstm_b_i[e], mlstm_b_f[e], mlstm_norm[e])
            nh.append(st[0]); nC.append(st[1]); nn_.append(st[2]); nm.append(st[3])
        else:
            o = layer // 2
            mix, s_new, c_new = _odd_mixer(h, S_g[o], conv_g[o], segments, odd_w_in[o], odd_conv_w[o],
                                           gdn_a_log[o], gdn_dt_bias[o], gdn_norm[o], odd_w_out[o])
            ng.append(s_new); nconv.append(c_new)
        x = x + mix
        x = x + _swiglu(_rmsnorm(x, norm_ffn[layer]), ffn_w_in[layer], ffn_w_out[layer])
    y = _rmsnorm(x, norm_final)
    dt = x.dtype
    stk = lambda xs: jnp.stack(xs, axis=0).astype(dt)
    return y, (stk(nh), stk(nC), stk(nn_), stk(nm), stk(ng), stk(nconv))


def setup_inputs(seed: int = 0) -> dict:
    key = jax.random.key(seed)
    ks = iter(jax.random.split(key, 32))

    def nrm(shape, scale=1.0):
        return scale * jax.random.normal(next(ks), shape, jnp.float32)

    a_log = jnp.log(jax.random.uniform(next(ks), (N_ODD, HC_V_HEADS), jnp.float32, 1.0, 16.0))
    dt = jnp.exp(jax.random.uniform(next(ks), (N_ODD, HC_V_HEADS), jnp.float32, math.log(1e-3), math.log(1e-1)))
    dt_bias = dt + jnp.log(-jnp.expm1(-dt))
    return {
        "x_prompt": nrm((BATCH, SEQ, D_MODEL)),
        "x_sample": nrm((DEC_BATCH, DEC_SEQ, D_MODEL)),
        "state_hgrn_S": nrm((N_EVEN, DEC_BATCH, HA_HEADS, HA_DK, HA_DV), 0.3),
        "state_mlstm_C": nrm((N_EVEN, DEC_BATCH, HB_HEADS, HB_DQK, HB_DV), 0.3),
        "state_mlstm_n": nrm((N_EVEN, DEC_BATCH, HB_HEADS, HB_DQK), 0.3),
        "state_mlstm_m": nrm((N_EVEN, DEC_BATCH, HB_HEADS)),
        "state_gdn_S": nrm((N_ODD, DEC_BATCH, HC_V_HEADS, HC_DK, HC_DV), 0.1),
        "state_gdn_conv": nrm((N_ODD, DEC_BATCH, CONV_W - 1, GDN_CONV_DIM)),
        "meta_tokens": nrm((N_META, D_MODEL)),
        "norm_mix": 1.0 + nrm((DEPTH, D_MODEL), 0.02),
        "norm_ffn": 1.0 + nrm((DEPTH, D_MODEL), 0.02),
        "norm_final": 1.0 + nrm((D_MODEL,), 0.02),
        "even_w_in": nrm((N_EVEN, D_MODEL, EVEN_COLS), D_MODEL ** -0.5),
        "even_w_out": nrm((N_EVEN, EVEN_OUT, D_MODEL), EVEN_OUT ** -0.5),
        "hgrn_lb_logits": nrm((N_EVEN + 1, HA_HEADS * HA_DK), 0.5),
        "hgrn_norm": 1.0 + nrm((N_EVEN, HA_HEADS * HA_DV), 0.02),
        "mlstm_b_i": nrm((N_EVEN, HB_HEADS), 0.1),
        "mlstm_b_f": 3.0 + nrm((N_EVEN, HB_HEADS), 0.5),
        "mlstm_norm": 1.0 + nrm((N_EVEN, HB_HEADS * HB_DV), 0.02),
        "odd_w_in": nrm((N_ODD, D_MODEL, ODD_COLS), D_MODEL ** -0.5),
        "odd_conv_w": nrm((N_ODD, CONV_W, GDN_CONV_DIM), CONV_W ** -0.5),
        "gdn_a_log": a_log,
        "gdn_dt_bias": dt_bias,
        "gdn_norm": 1.0 + nrm((N_ODD, HC_DV), 0.02),
        "odd_w_out": nrm((N_ODD, HC_V_HEADS * HC_DV, D_MODEL), (HC_V_HEADS * HC_DV) ** -0.5),
        "ffn_w_in": nrm((DEPTH, D_MODEL, 2 * FF_HIDDEN), D_MODEL ** -0.5),
        "ffn_w_out": nrm((DEPTH, FF_HIDDEN, D_MODEL), FF_HIDDEN ** -0.5),
    }


def reference(x_prompt, x_sample, state_hgrn_S, state_mlstm_C, state_mlstm_n, state_mlstm_m, state_gdn_S,
              state_gdn_conv, meta_tokens, norm_mix, norm_ffn, norm_final, even_w_in, even_w_out, hgrn_lb_logits,
              hgrn_norm, mlstm_b_i, mlstm_b_f, mlstm_norm, odd_w_in, odd_conv_w, gdn_a_log, gdn_dt_bias, gdn_norm,
              odd_w_out, ffn_w_in, ffn_w_out):
    weights = (norm_mix, norm_ffn, norm_final, even_w_in, even_w_out, hgrn_lb_logits, hgrn_norm, mlstm_b_i,
               mlstm_b_f, mlstm_norm, odd_w_in, odd_conv_w, gdn_a_log, gdn_dt_bias, gdn_norm, odd_w_out,
               ffn_w_in, ffn_w_out)
    f32 = jnp.float32
    B, T, _ = x_prompt.shape
    meta = jnp.broadcast_to(meta_tokens.astype(x_prompt.dtype)[None], (B, N_META, D_MODEL))
    x0 = jnp.concatenate([meta, x_prompt], axis=1)
    zero_states = (
        jnp.zeros((N_EVEN, B, HA_HEADS, HA_DK, HA_DV), f32),
        jnp.zeros((N_EVEN, B, HB_HEADS, HB_DQK, HB_DV), f32),
        jnp.zeros((N_EVEN, B, HB_HEADS, HB_DQK), f32),
        jnp.zeros((N_EVEN, B, HB_HEADS), f32),
        jnp.zeros((N_ODD, B, HC_V_HEADS, HC_DK, HC_DV), f32),
        jnp.zeros((N_ODD, B, CONV_W - 1, GDN_CONV_DIM), x_prompt.dtype),
    )
    y0, p_states = _trunk(x0, zero_states, ((0, N_META, N_META), (N_META, N_META + T, CHUNK)), *weights)
    L = x_sample.shape[1]
    s_in = (state_hgrn_S, state_mlstm_C, state_mlstm_n, state_mlstm_m, state_gdn_S, state_gdn_conv)
    y1, s_states = _trunk(x_sample, s_in, ((0, L, L),), *weights)
    p_hgrn_S, p_mlstm_C, p_mlstm_n, p_mlstm_m, p_gdn_S, p_gdn_conv = p_states
    s_hgrn_S, s_mlstm_C, s_mlstm_n, s_mlstm_m, s_gdn_S, s_gdn_conv = s_states
    return (y0[:, N_META:], y1, p_hgrn_S, p_mlstm_C, p_mlstm_n, p_mlstm_m, p_gdn_S, p_gdn_conv,
            s_hgrn_S, s_mlstm_C, s_mlstm_n, s_mlstm_m, s_gdn_S, s_gdn_conv)
```

```python
import numpy as np
from contextlib import ExitStack
import concourse.bass as bass
import concourse.mybir as mybir
from concourse.bass_utils import run_bass_kernel_spmd

F32 = mybir.dt.float32
BF16 = mybir.dt.bfloat16
AF = mybir.ActivationFunctionType
ALU = mybir.AluOpType
AX = mybir.AxisListType

D = 1024
FF = 2816
EVEN_COLS = 3592
ODD_COLS = 6176
EPS = 1e-6
N_CORES = 8
T_FULL = 8192
NS_FULL = 2
DBL_BF16 = True


class TB:
    def __init__(self, t, name):
        self.t = t
        self.name = name
        self.lw = None
        self.rd = {}

    def ap(self):
        return self.t[:]

    def __getitem__(self, idx):
        return self.t[idx]


class Sched:
    def __init__(self, nc, n_dma_slots=8, same_engine_sync=True):
        self.nc = nc
        self.es0 = ExitStack()
        self.stack = [self.es0]
        self.eng = {"pe": nc.tensor, "act": nc.scalar, "dve": nc.vector, "pool": nc.gpsimd, "sp": nc.sync}
        self.sems = {}
        self.cnt = {}
        for e in ("pe", "act", "dve", "pool"):
            self.sems[e] = self.es0.enter_context(nc.semaphore("s_" + e))
            self.cnt[e] = 0
        self.nslots = n_dma_slots
        self.dma_q = {}
        for q in ("sp", "pool"):
            slots = []
            for i in range(n_dma_slots):
                key = ("dma", q, i)
                self.sems[key] = self.es0.enter_context(nc.semaphore("d_%s%d" % (q, i)))
                self.cnt[key] = 0
                slots.append(key)
            self.dma_q[q] = [slots, 0]
        self.waited = {e: {} for e in self.eng}
        self.same = same_engine_sync
        self.noself = {"pe"}
        self.ninst = 0

    def push(self):
        self.phase = getattr(self, "phase", 0) + 1
        self.sb_bytes = getattr(self, "sb_base", 0)
        self.stack.append(ExitStack())

    def pop(self):
        self.barrier()
        self.stack.pop().close()

    def sb(self, name, shape, dtype):
        nb = int(np.prod(shape[1:])) * (2 if dtype == BF16 else 4)
        nb = -(-nb // 32) * 32
        if len(self.stack) == 1:
            self.sb_base = getattr(self, "sb_base", 0) + nb
        self.sb_bytes = getattr(self, "sb_bytes", 0) + nb
        assert self.sb_bytes <= 200 * 1024, ("SBUF budget exceeded", name, self.sb_bytes)
        t = self.stack[-1].enter_context(self.nc.sbuf_tensor("p%d_%s" % (getattr(self, "phase", 0), name), list(shape), dtype))
        return TB(t, name)

    def ps(self, name, shape, dtype):
        t = self.stack[-1].enter_context(self.nc.psum_tensor("p%d_%s" % (getattr(self, "phase", 0), name), list(shape), dtype))
        return TB(t, name)

    def _wait(self, e, k, v):
        if v <= 0 or self.waited[e].get(k, 0) >= v:
            return
        self.eng[e].wait_ge(self.sems[k], v)
        self.waited[e][k] = v
        self.ninst += 1

    def barrier(self):
        for e in self.eng:
            for k, v in self.cnt.items():
                if k == e:
                    continue
                self._wait(e, k, v)

    def _deps(self, e, reads, writes):
        need = {}

        def add(ev):
            if ev is None:
                return
            k, v = ev
            if need.get(k, 0) < v:
                need[k] = v

        for b in reads:
            add(b.lw)
        for b in writes:
            add(b.lw)
            for k, v in b.rd.items():
                add((k, v))
        for k, v in need.items():
            if k == e and (e in self.noself or not self.same):
                continue
            self._wait(e, k, v)

    def _commit(self, ev, reads, writes):
        k, v = ev
        for b in reads:
            if b.rd.get(k, 0) < v:
                b.rd[k] = v
        for b in writes:
            b.lw = ev
            b.rd = {}

    def op(self, e, fn, reads, writes):
        self._deps(e, reads, writes)
        inst = fn()
        self.cnt[e] += 1
        inst.then_inc(self.sems[e], 1)
        self._commit((e, self.cnt[e]), reads, writes)
        self.ninst += 1
        return inst

    def dma(self, out, in_, reads, writes, q="sp", **kw):
        slots, n = self.dma_q[q]
        key = slots[n % self.nslots]
        self.dma_q[q][1] = n + 1
        self._wait(q, key, self.cnt[key])
        self._deps(q, reads, writes)
        o = out.ap() if isinstance(out, TB) else out
        i = in_.ap() if isinstance(in_, TB) else in_
        inst = self.eng[q].dma_start(out=o, in_=i, **kw)
        self.cnt[key] += 16
        inst.then_inc(self.sems[key], 16)
        self._commit((key, self.cnt[key]), reads, writes)
        self.ninst += 1
        return inst

    def finish(self):
        self.barrier()
        while self.stack:
            self.stack.pop().close()

    def mm(self, out, lhsT, rhs, R, W, start=True, stop=True):
        nc = self.nc
        return self.op("pe", lambda: nc.tensor.matmul(out, lhsT=lhsT, rhs=rhs, start=start, stop=stop), R, W)

    def tr(self, out, in_, ident, R, W):
        nc = self.nc
        return self.op("pe", lambda: nc.tensor.transpose(out, in_, ident), R, W)

    def tt(self, out, a, b, op, R, W, e="dve"):
        eo = self.nc.vector if e == "dve" else self.nc.gpsimd
        return self.op(e, lambda: eo.tensor_tensor(out=out, in0=a, in1=b, op=op), R, W)

    def ts(self, out, a, s1, op0, R, W, s2=None, op1=None, e="dve"):
        eo = self.nc.vector if e == "dve" else self.nc.gpsimd
        if op1 is None:
            return self.op(e, lambda: eo.tensor_scalar(out=out, in0=a, scalar1=s1, scalar2=None, op0=op0), R, W)
        return self.op(e, lambda: eo.tensor_scalar(out=out, in0=a, scalar1=s1, scalar2=s2, op0=op0, op1=op1), R, W)

    def stt(self, out, a, s, b, op0, op1, R, W):
        nc = self.nc
        return self.op("dve", lambda: nc.vector.scalar_tensor_tensor(out=out, in0=a, scalar=s, in1=b, op0=op0, op1=op1), R, W)

    def af(self, out, in_, func, R, W, scale=None, bias=None, accum=None):
        nc = self.nc
        kw = {}
        if scale is not None:
            kw["scale"] = scale
        if bias is not None:
            kw["bias"] = bias
        if accum is not None:
            kw["accum_out"] = accum
        return self.op("act", lambda: nc.scalar.activation(out=out, in_=in_, func=func, **kw), R, W)

    def cp(self, out, in_, R, W, e="dve"):
        nc = self.nc
        if e == "act":
            return self.op("act", lambda: nc.scalar.copy(out=out, in_=in_), R, W)
        eo = nc.vector if e == "dve" else nc.gpsimd
        return self.op(e, lambda: eo.tensor_copy(out=out, in_=in_), R, W)

    def red(self, out, in_, op, R, W):
        nc = self.nc
        return self.op("dve", lambda: nc.vector.tensor_reduce(out=out, in_=in_, axis=AX.X, op=op), R, W)

    def recip(self, out, in_, R, W):
        nc = self.nc
        return self.op("dve", lambda: nc.vector.reciprocal(out=out, in_=in_), R, W)

    def memset(self, tb, val, e="dve"):
        eo = self.nc.vector if e == "dve" else self.nc.gpsimd
        return self.op(e, lambda: eo.memset(tb.ap(), val), [], [tb])


def bc(ap, axis, shape):
    return ap.unsqueeze(axis).broadcast_to(list(shape))


def seq_layout(T, NS):
    seqs = []
    r = 3
    seqs.append(dict(start=r, len=16 + T, chunks=[16] + [64] * (T // 64), init=None))
    r += 16 + T
    for i in range(NS):
        r += 3
        seqs.append(dict(start=r, len=64, chunks=[64], init=i))
        r += 64
    NT = -(-r // 128) * 128
    return seqs, NT


def load_weight(S, Wsb, Wd, K):
    for k in range(K // 128):
        S.dma(Wsb[:, k, :], Wd[k * 128:(k + 1) * 128, :], [], [], q="pool")


def rmsnorm_tile(S, xt, wbc, h, junk, ss, rstd):
    S.af(junk.ap(), xt.ap(), AF.Square, [xt], [junk, ss], accum=ss.ap())
    S.af(ss.ap(), ss.ap(), AF.Sqrt, [ss], [ss], scale=1.0 / D, bias=EPS)
    S.recip(rstd.ap(), ss.ap(), [ss], [rstd])
    S.stt(h.ap(), xt.ap(), rstd.ap(), wbc.ap(), ALU.mult, ALU.mult, [xt, rstd, wbc], [h])


def transpose_tile(S, src, nk, ptb, dst, ident, e="dve"):
    for k0 in range(0, nk, 8):
        n = min(8, nk - k0)
        for k in range(n):
            S.tr(ptb[:, k, :], src[:, (k0 + k) * 128:(k0 + k + 1) * 128], ident.ap(), [src, ident], [ptb])
        S.cp(dst[:, k0:k0 + n, :], ptb[:, 0:n, :], [ptb], [dst], e=e)


def run_pipelined(n, A1, B, A2):
    A1(0)
    A2(0)
    for i in range(n):
        if i + 1 < n:
            A1(i + 1)
        B(i)
        if i + 1 < n:
            A2(i + 1)


def load_bcast(S, tb, dram_row, n):
    S.dma(tb, dram_row.broadcast_to([128, n]), [], [tb])


def build(T=T_FULL, NS=NS_FULL, debug=False):
    nc = bass.Bass("TRN2", target_bir_lowering=False)
    seqs, NT = seq_layout(T, NS)
    NSEQ = 1 + NS
    ntile = NT // 128

    def din(name, shape, dt=F32):
        return nc.dram_tensor(name, list(shape), dt, kind="ExternalInput").ap()

    def dout(name, shape, dt=F32):
        return nc.dram_tensor(name, list(shape), dt, kind="ExternalOutput").ap()

    def dscr(name, shape, dt=F32):
        return nc.dram_tensor(name, list(shape), dt, kind="ExternalOutput" if debug else "Internal").ap()

    xin = din("xin", [NT, D])
    st_hS = din("st_hS", [NS, 4, 128, 128])
    st_mC = din("st_mC", [NS, 4, 64, 128])
    st_mn = din("st_mn", [NS, 4, 64])
    st_mm = din("st_mm", [NS, 4])
    st_gS = din("st_gS", [NS, 16, 128, 128])
    st_gc = din("st_gc", [NS, 3, 4096])
    norm_mix = din("norm_mix", [2, D])
    norm_ffn = din("norm_ffn", [2, D])
    norm_final = din("norm_final", [1, D])
    even_w_in = din("even_w_in", [D, EVEN_COLS])
    even_w_out = din("even_w_out", [D, D])
    lb_logits = din("lb_logits", [2, 512])
    hgrn_norm = din("hgrn_norm", [1, 512])
    mlstm_bif = din("mlstm_bif", [1, 8])
    mlstm_norm = din("mlstm_norm", [1, 512])
    odd_w_in = din("odd_w_in", [D, ODD_COLS])
    odd_conv_w = din("odd_conv_w", [4, 4096])
    gdn_a_log = din("gdn_a_log", [1, 16])
    gdn_dt_bias = din("gdn_dt_bias", [1, 16])
    gdn_norm = din("gdn_norm", [1, 128])
    odd_w_out = din("odd_w_out", [2048, D])
    ffn_w_in = din("ffn_w_in", [2, D, 2 * FF])
    ffn_w_out = din("ffn_w_out", [2, FF, D])
    c_ident = din("c_ident", [128, 128])
    c_U = din("c_U", [64, 64])
    c_SU = din("c_SU", [64, 64])
    c_SL = din("c_SL", [64, 64])
    c_bm4 = din("c_bm4", [4, 4, 64])
    c_Ublk = din("c_Ublk", [128, 128])
    c_SLblk = din("c_SLblk", [128, 128])
    c_rep = din("c_rep", [128, 3, 64])

    y = dout("y", [NT, D])
    o_hS = dout("o_hS", [NSEQ, 4, 128, 128])
    o_mC = dout("o_mC", [NSEQ, 4, 64, 128])
    o_mn = dout("o_mn", [NSEQ, 4, 64])
    o_mm = dout("o_mm", [NSEQ, 4])
    o_gS = dout("o_gS", [NSEQ, 16, 128, 128])
    o_gc = dout("o_gc", [NSEQ, 3, 4096])

    HG = dscr("HG", [NT, 2560])
    ML = dscr("ML", [NT, 1544])
    MG0 = dscr("MG0", [NT, 1024], BF16)
    X1 = dscr("X1", [NT, D])
    ACTF = dscr("ACTF", [NT, FF], BF16)
    X2 = dscr("X2", [NT, D])
    PQ = dscr("PQ", [NT + 3, 4096])
    GD = dscr("GD", [NT, 6176])
    MG1 = dscr("MG1", [NT, 2048], BF16)
    X3 = dscr("X3", [NT, D])

    S = Sched(nc)

    I32 = S.sb("I32", [128, 128], F32)
    Ibf = S.sb("Ibf", [128, 128], BF16)
    U32 = S.sb("U32", [64, 64], F32)
    SU32 = S.sb("SU32", [64, 64], F32)
    SL32 = S.sb("SL32", [64, 64], F32)
    ones32 = S.sb("ones32", [128, 128], F32)
    bm4 = S.sb("bm4", [4, 4, 64], F32)
    S.dma(I32, c_ident, [], [I32])
    S.dma(Ibf, c_ident, [], [Ibf], q="pool")
    S.dma(U32, c_U, [], [U32])
    S.dma(SU32, c_SU, [], [SU32])
    S.dma(SL32, c_SL, [], [SL32])
    S.dma(bm4, c_bm4, [], [bm4])
    S.memset(ones32, 1.0)
    S.barrier()

    def dense_blocks(hT, nk, W, blocks, pms, pmi):
        for (c0, ncols, epi) in blocks:
            pm = pms[pmi % len(pms)]
            pmi += 1
            for k in range(nk):
                S.mm(pm[:, 0:ncols], hT[:, k, :], W[:, k, c0:c0 + ncols], [hT, W], [pm], start=(k == 0), stop=(k == nk - 1))
            epi(pm)
        return pmi

    def phase_d1_even():
        S.push()
        W = S.sb("W", [128, 8, EVEN_COLS], BF16)
        wn = S.sb("wn", [128, D], F32)
        lb = S.sb("lb", [128, 512], F32)
        oml = S.sb("oml", [128, 512], F32)
        hnw = S.sb("hnw", [128, 512], F32)
        mnw = S.sb("mnw", [128, 512], F32)
        bif = S.sb("bif", [128, 8], F32)
        junk = S.sb("junk", [128, D], F32)
        tmp = [S.sb("tmp%d" % i, [128, 512], F32) for i in range(2)]
        t8 = S.sb("t8", [128, 8], F32)
        t8b = S.sb("t8b", [128, 4], F32)
        xt = [S.sb("xt%d" % i, [128, D], F32) for i in range(2)]
        ss = [S.sb("ss%d" % i, [128, 1], F32) for i in range(2)]
        rstd = [S.sb("rstd%d" % i, [128, 1], F32) for i in range(2)]
        h = [S.sb("h%d" % i, [128, D], BF16) for i in range(2)]
        hT = [S.sb("hT%d" % i, [128, 8, 128], BF16) for i in range(2)]
        HGt = [S.sb("HGt%d" % i, [128, 5, 512], F32) for i in range(2)]
        MLt = [S.sb("MLt%d" % i, [128, 1544], F32) for i in range(2)]
        ptb = [S.ps("ptb%d" % i, [128, 8, 128], BF16) for i in range(2)]
        pms = [S.ps("pm%d" % i, [128, 512], F32) for i in range(6)]

        load_weight(S, W, even_w_in, D)
        load_bcast(S, wn, norm_mix[0:1, :], D)
        load_bcast(S, lb, lb_logits[0:1, :], 512)
        load_bcast(S, oml, lb_logits[1:2, :], 512)
        load_bcast(S, hnw, hgrn_norm[0:1, :], 512)
        load_bcast(S, mnw, mlstm_norm[0:1, :], 512)
        load_bcast(S, bif, mlstm_bif[0:1, :], 8)
        S.tt(lb.ap(), lb.ap(), oml.ap(), ALU.subtract, [lb, oml], [lb])
        S.af(lb.ap(), lb.ap(), AF.Sigmoid, [lb], [lb])
        S.ts(oml.ap(), lb.ap(), -1.0, ALU.mult, [lb], [oml], s2=1.0, op1=ALU.add)
        S.barrier()

        pmi_box = [0]

        def stA1(i):
            b = i % 2
            S.dma(xt[b], xin[i * 128:i * 128 + 128, :], [], [xt[b]])
            rmsnorm_tile(S, xt[b], wn, h[b], junk, ss[b], rstd[b])

        def stA2(i):
            b = i % 2
            transpose_tile(S, h[b], 8, ptb[b], hT[b], Ibf)

        def stB(i):
            pmi = pmi_box[0]
            b = i % 2
            r0 = i * 128
            hg, ml = HGt[b], MLt[b]
            t0, t1 = tmp

            def e_q(pm):
                S.af(hg[:, 0, :], pm.ap(), AF.Silu, [pm], [hg])

            def e_f(pm):
                S.af(t0.ap(), pm.ap(), AF.Sigmoid, [pm], [t0])
                S.tt(t0.ap(), t0.ap(), oml.ap(), ALU.mult, [t0, oml], [t0])
                S.tt(t0.ap(), t0.ap(), lb.ap(), ALU.add, [t0, lb], [t0])
                S.af(hg[:, 1, :], t0.ap(), AF.Ln, [t0], [hg])
                S.ts(hg[:, 2, :], t0.ap(), -1.0, ALU.mult, [t0], [hg], s2=1.0, op1=ALU.add)

            def e_v(pm):
                S.cp(hg[:, 3, :], pm.ap(), [pm], [hg])

            def e_g(pm):
                S.af(t1.ap(), pm.ap(), AF.Silu, [pm], [t1])
                S.tt(hg[:, 4, :], t1.ap(), hnw.ap(), ALU.mult, [t1, hnw], [hg])

            def e_qk(pm):
                S.cp(ml[:, 0:256], pm[:, 0:256], [pm], [ml])
                S.ts(ml[:, 256:512], pm[:, 256:512], 0.125, ALU.mult, [pm], [ml])

            def e_mv(pm):
                S.cp(ml[:, 512:1024], pm.ap(), [pm], [ml])

            def e_o(pm):
                S.af(t1.ap(), pm.ap(), AF.Sigmoid, [pm], [t1])
                S.tt(ml[:, 1024:1536], t1.ap(), mnw.ap(), ALU.mult, [t1, mnw], [ml])

            def e_if(pm):
                S.tt(t8.ap(), pm[:, 0:8], bif.ap(), ALU.add, [pm, bif], [t8])
                S.af(t8.ap(), t8.ap(), AF.Tanh, [t8], [t8], scale=1.0 / 15.0)
                S.ts(ml[:, 1536:1540], t8[:, 0:4], 15.0, ALU.mult, [t8], [ml])
                S.af(t8b.ap(), t8[:, 4:8], AF.Exp, [t8], [t8b], scale=-15.0)
                S.af(t8b.ap(), t8b.ap(), AF.Ln, [t8b], [t8b], bias=1.0)
                S.ts(ml[:, 1540:1544], t8b.ap(), -1.0, ALU.mult, [t8b], [ml])

            blocks = [(0, 512, e_q), (1536, 512, e_g), (512, 512, e_f), (1024, 512, e_v),
                      (2048, 512, e_qk), (2560, 512, e_mv), (3072, 512, e_o), (3584, 8, e_if)]
            pmi_box[0] = dense_blocks(hT[b], 8, W, blocks, pms, pmi)
            S.dma(HG[r0:r0 + 128, :], hg.ap().rearrange("p a b -> p (a b)"), [hg], [], q="pool")
            S.dma(ML[r0:r0 + 128, :], ml, [ml], [], q="pool")

        run_pipelined(ntile, stA1, stB, stA2)
        S.pop()

    def phase_r0():
        S.push()
        Sh = S.sb("Sh", [128, 4, 128], F32)
        Shb = S.sb("Shb", [128, 4, 128], BF16)
        Cn = S.sb("Cn", [64, 4, 129], F32)
        Cnb = S.sb("Cnb", [64, 4, 129], BF16)
        m_row = S.sb("m_row", [4, 1], F32)
        m_bc = S.sb("m_bc", [64, 4], F32)
        hgc = [S.sb("hgc%d" % i, [64, 5, 512], F32) for i in range(2)]
        mlc = [S.sb("mlc%d" % i, [64, 1544], F32) for i in range(2)]
        mg = [S.sb("mg%d" % i, [64, 1024], BF16) for i in range(2)]
        eb = S.sb("eb", [64, 512], F32)
        enb = S.sb("enb", [64, 512], F32)
        qt = S.sb("qt", [64, 512], BF16)
        kt = S.sb("kt", [64, 512], BF16)
        qkT = S.sb("qkT", [128, 8, 64], BF16)
        ebl = S.sb("ebl", [128, 4, 2], F32)
        scm = S.sb("scm", [64, 4, 64], BF16)
        vb = S.sb("vb", [64, 512], BF16)
        stmp = S.sb("stmp", [128, 4, 128], F32)
        osq = S.sb("osq", [64, 512], F32)
        oss = S.sb("oss", [64, 4], F32)
        orstd = S.sb("orstd", [64, 4], F32)
        on = S.sb("on", [64, 4, 128], F32)
        a_t = S.sb("a_t", [64, 4], F32)
        aT = S.sb("aT", [4, 64], F32)
        M_row = S.sb("M_row", [4, 64], F32)
        Mblk = S.sb("Mblk", [4, 4, 64], F32)
        Mlb = S.sb("Mlb", [4, 4], F32)
        M_t = S.sb("M_t", [64, 4], F32)
        F_t = S.sb("F_t", [64, 4], F32)
        DT = S.sb("DT", [64, 4, 64], F32)
        Wbc = S.sb("Wbc", [64, 4, 64], F32)
        mq = S.sb("mq", [64, 512], BF16)
        mqkT = S.sb("mqkT", [64, 8, 64], BF16)
        qTt = S.sb("qTt", [64, 4, 64], BF16)
        qkD = S.sb("qkD", [64, 4, 64], BF16)
        vext = S.sb("vext", [64, 4, 129], BF16)
        den = S.sb("den", [64, 4], F32)
        nden = S.sb("nden", [64, 4], F32)
        emt = S.sb("emt", [64, 4], F32)
        rden = S.sb("rden", [64, 4], F32)
        hn = S.sb("hn", [64, 4, 128], F32)
        w_s = S.sb("w_s", [64, 4], F32)
        khat = S.sb("khat", [64, 4, 64], BF16)
        dec = S.sb("dec", [64, 4], F32)
        Flb = S.sb("Flb", [64, 4], F32)
        Mlbc = S.sb("Mlbc", [64, 4], F32)
        fl_row = S.sb("fl_row", [4, 2], F32)
        ones_c2 = S.sb("ones_c2", [64, 2], F32)
        pA = S.ps("pA", [128, 512], F32)
        pB = S.ps("pB", [128, 8, 64], BF16)
        pC = S.ps("pC", [128, 512], F32)
        pD = S.ps("pD", [128, 512], F32)
        pE = S.ps("pE", [128, 512], F32)
        pF = S.ps("pF", [64, 4, 256], F32)
        pB2 = S.ps("pB2", [128, 8, 64], BF16)
        osq2 = S.sb("osq2", [64, 512], F32)
        oss2 = S.sb("oss2", [64, 4], F32)
        orstd2 = S.sb("orstd2", [64, 4], F32)
        S.memset(ones_c2, 1.0)

        def interleave(gens):
            gens = [g for g in gens if g is not None]
            while gens:
                for g in list(gens):
                    try:
                        next(g)
                    except StopIteration:
                        gens.remove(g)

        for si, sq in enumerate(seqs):
            if sq["init"] is None:
                S.memset(Sh, 0.0)
                S.memset(Cn, 0.0)
                S.memset(m_row, 0.0)
                S.memset(m_bc, 0.0)
            else:
                j = sq["init"]
                S.dma(Sh, st_hS[j].rearrange("h k v -> k h v"), [], [Sh])
                S.dma(Cn[:, :, 0:128], st_mC[j].rearrange("h k v -> k h v"), [], [Cn])
                S.dma(Cn[:, :, 128:129], st_mn[j].rearrange("h (k o) -> k h o", o=1), [], [Cn], allow_slow_non_contiguous=True)
                S.dma(m_row, st_mm[j].rearrange("(h o) -> h o", o=1), [], [m_row])
                S.dma(m_bc, st_mm[j:j + 1, :].broadcast_to([64, 4]), [], [m_bc])
            S.cp(Shb.ap(), Sh.ap(), [Sh], [Shb], e="act")
            S.cp(Cnb.ap(), Cn.ap(), [Cn], [Cnb], e="act")
            row = sq["start"]
            for ci, c in enumerate(sq["chunks"]):
                b = ci % 2
                hg, ml, mgo = hgc[b], mlc[b], mg[b]
                S.dma(hg[0:c], HG[row:row + c, :].rearrange("p (a b) -> p a b", a=5), [], [hg])
                S.dma(ml[0:c], ML[row:row + c, :], [], [ml])
                def hgrn_chunk(c=c, hg=hg, mgo=mgo):
                    S.mm(pA[0:c, :], U32[0:c, 0:c], hg[0:c, 1, :], [U32, hg], [pA])
                    S.af(eb[0:c], pA[0:c, :], AF.Exp, [pA], [eb])
                    S.af(enb[0:c], pA[0:c, :], AF.Exp, [pA], [enb], scale=-1.0)
                    yield
                    S.tt(qt[0:c], hg[0:c, 0, :], eb[0:c], ALU.mult, [hg, eb], [qt])
                    S.tt(kt[0:c], hg[0:c, 2, :], enb[0:c], ALU.mult, [hg, enb], [kt])
                    S.cp(vb[0:c], hg[0:c, 3, :], [hg], [vb], e="act")
                    yield
                    for hh in range(4):
                        S.tr(pB[:, hh, 0:c], qt[0:c, hh * 128:(hh + 1) * 128], Ibf[0:c, 0:c], [qt, Ibf], [pB])
                        S.tr(pB[:, 4 + hh, 0:c], kt[0:c, hh * 128:(hh + 1) * 128], Ibf[0:c, 0:c], [kt, Ibf], [pB])
                    S.cp(qkT[:, :, 0:c], pB[:, :, 0:c], [pB], [qkT])
                    yield
                    for hh in range(4):
                        S.mm(pC[:, 2 * hh:2 * hh + 2], eb[0:c, hh * 128:(hh + 1) * 128], I32[0:c, c - 2:c], [eb, I32], [pC])
                    S.cp(ebl.ap().rearrange("p h t -> p (h t)"), pC[:, 0:8], [pC], [ebl], e="act")
                    for hh in range(4):
                        S.mm(pC[0:c, 64 + hh * 64:64 + hh * 64 + c], qkT[:, 4 + hh, 0:c], qkT[:, hh, 0:c], [qkT], [pC])
                    S.tt(scm[0:c, :, 0:c], pC[0:c, 64:320].rearrange("p (h t) -> p h t", h=4)[:, :, 0:c],
                         bc(U32[0:c, 0:c], 1, [c, 4, c]), ALU.mult, [pC, U32], [scm])
                    yield
                    for hh in range(4):
                        S.mm(pA[0:c, hh * 128:(hh + 1) * 128], scm[0:c, hh, 0:c], vb[0:c, hh * 128:(hh + 1) * 128], [scm, vb], [pA], start=True, stop=False)
                        S.mm(pA[0:c, hh * 128:(hh + 1) * 128], qkT[:, hh, 0:c], Shb[:, hh, :], [qkT, Shb], [pA], start=False, stop=True)
                    for hh in range(4):
                        S.mm(pD[:, hh * 128:(hh + 1) * 128], kt[0:c, hh * 128:(hh + 1) * 128], vb[0:c, hh * 128:(hh + 1) * 128], [kt, vb], [pD])
                    yield
                    S.tt(stmp.ap().rearrange("p h v -> p (h v)"), Sh.ap().rearrange("p h v -> p (h v)"), pD.ap(), ALU.add, [Sh, pD], [stmp])
                    S.tt(Sh.ap(), stmp.ap(), bc(ebl[:, :, 1], 2, [128, 4, 128]), ALU.mult, [stmp, ebl], [Sh])
                    S.cp(Shb.ap(), Sh.ap(), [Sh], [Shb], e="act")
                    yield
                    S.af(osq[0:c], pA[0:c, :], AF.Square, [pA], [osq])
                    S.red(oss[0:c], osq[0:c].rearrange("p (h v) -> p h v", h=4), ALU.add, [osq], [oss])
                    S.af(oss[0:c], oss[0:c], AF.Sqrt, [oss], [oss], scale=1.0 / 128.0, bias=EPS)
                    S.recip(orstd[0:c], oss[0:c], [oss], [orstd])
                    S.tt(on[0:c], pA[0:c, :].rearrange("p (h v) -> p h v", h=4), bc(orstd[0:c], 2, [c, 4, 128]), ALU.mult, [pA, orstd], [on])
                    S.tt(mgo[0:c, 0:512], on[0:c].rearrange("p h v -> p (h v)"), hg[0:c, 4, :], ALU.mult, [on, hg], [mgo])

                def mlstm_chunk(c=c, ml=ml, mgo=mgo):
                    S.mm(pE[0:c, 0:4], U32[0:c, 0:c], ml[0:c, 1540:1544], [U32, ml], [pE])
                    S.mm(pE[0:64, 8:12], ones32[0:c, 0:64], ml[0:c, 1540:1544], [ones32, ml], [pE])
                    S.mm(pE[0:4, 16:18], ml[0:c, 1540:1544], ones_c2[0:c, :], [ml, ones_c2], [pE])
                    S.cp(F_t[0:c], pE[0:c, 0:4], [pE], [F_t])
                    S.cp(Flb.ap(), pE[0:64, 8:12], [pE], [Flb])
                    S.cp(fl_row.ap(), pE[0:4, 16:18], [pE], [fl_row])
                    yield
                    S.tt(a_t[0:c], ml[0:c, 1536:1540], F_t[0:c], ALU.subtract, [ml, F_t], [a_t])
                    S.tr(pE[0:4, 32:32 + c], a_t[0:c, :], I32[0:c, 0:c], [a_t, I32], [pE])
                    S.cp(aT[:, 0:c], pE[0:4, 32:32 + c], [pE], [aT])
                    S.op("dve", lambda: nc.vector.tensor_tensor_scan(out=M_row[:, 0:c], data0=aT[:, 0:c], data1=aT[:, 0:c],
                                                                     initial=m_row.ap(), op0=ALU.max, op1=ALU.max),
                         [aT, m_row], [M_row])
                    yield
                    S.tt(Mblk[:, :, 0:c], bc(M_row[:, 0:c], 1, [4, 4, c]), bm4[:, :, 0:c], ALU.mult, [M_row, bm4], [Mblk])
                    S.tt(Mlb.ap(), M_row[:, c - 1:c].broadcast_to([4, 4]), bm4[:, :, 0], ALU.mult, [M_row, bm4], [Mlb])
                    S.mm(pE[0:64, 128:128 + 4 * c], ones32[0:4, 0:64], Mblk[:, :, 0:c], [ones32, Mblk], [pE])
                    S.mm(pE[0:64, 400:404], ones32[0:4, 0:64], Mlb.ap(), [ones32, Mlb], [pE])
                    S.tr(pE[0:c, 416:420], M_row[:, 0:c], I32[0:4, 0:4], [M_row, I32], [pE])
                    S.cp(M_t[0:c], pE[0:c, 416:420], [pE], [M_t])
                    S.cp(Mlbc.ap(), pE[0:64, 400:404], [pE], [Mlbc])
                    yield
                    Mb3 = pE[0:64, 128:128 + 4 * c].rearrange("p (h t) -> p h t", h=4)
                    for hh in range(4):
                        S.af(DT[0:c, hh, 0:c], Mb3[0:c, hh, 0:c], AF.Exp, [pE, a_t], [DT], scale=-1.0, bias=a_t[0:c, hh:hh + 1])
                        S.af(Wbc[:, hh, 0:c], Mb3[:, hh, 0:c], AF.Exp, [pE, m_bc], [Wbc], scale=-1.0, bias=m_bc[:, hh:hh + 1])
                    S.tt(DT[0:c, :, 0:c], DT[0:c, :, 0:c], bc(U32[0:c, 0:c], 1, [c, 4, c]), ALU.mult, [DT, U32], [DT])
                    yield
                    S.cp(mq[0:c], ml[0:c, 0:512], [ml], [mq], e="act")
                    for hh in range(4):
                        S.tr(pB2[0:64, hh, 0:c], mq[0:c, hh * 64:(hh + 1) * 64], Ibf[0:c, 0:c], [mq, Ibf], [pB2])
                        S.tr(pB2[0:64, 4 + hh, 0:c], mq[0:c, 256 + hh * 64:256 + (hh + 1) * 64], Ibf[0:c, 0:c], [mq, Ibf], [pB2])
                    S.cp(mqkT[:, :, 0:c], pB2[0:64, :, 0:c], [pB2], [mqkT])
                    yield
                    S.tt(qTt[:, :, 0:c], mqkT[:, 0:4, 0:c], Wbc[:, :, 0:c], ALU.mult, [mqkT, Wbc], [qTt])
                    for hh in range(4):
                        S.mm(pF[0:c, hh, 192:192 + c], mqkT[:, 4 + hh, 0:c], mqkT[:, hh, 0:c], [mqkT], [pF])
                    S.tt(qkD[0:c, :, 0:c], pF[0:c, :, 192:192 + c], DT[0:c, :, 0:c], ALU.mult, [pF, DT], [qkD])
                    S.cp(vext[0:c, :, 0:128], ml[0:c, 512:1024].rearrange("p (h v) -> p h v", h=4), [ml], [vext], e="act")
                    S.cp(vext[0:c, :, 128:129], bc(ones32[0:c, 0:4], 2, [c, 4, 1]), [ones32], [vext])
                    yield
                    for hh in range(4):
                        S.mm(pF[0:c, hh, 0:129], qkD[0:c, hh, 0:c], vext[0:c, hh, :], [qkD, vext], [pF], start=True, stop=False)
                        S.mm(pF[0:c, hh, 0:129], qTt[:, hh, 0:c], Cnb[:, hh, :], [qTt, Cnb], [pF], start=False, stop=True)
                    yield
                    S.cp(den[0:c], pF[0:c, :, 128], [pF], [den])
                    S.ts(nden[0:c], den[0:c], -1.0, ALU.mult, [den], [nden])
                    S.tt(den[0:c], den[0:c], nden[0:c], ALU.max, [den, nden], [den])
                    S.tt(emt[0:c], F_t[0:c], M_t[0:c], ALU.add, [F_t, M_t], [emt])
                    S.af(emt[0:c], emt[0:c], AF.Exp, [emt], [emt], scale=-1.0)
                    S.tt(den[0:c], den[0:c], emt[0:c], ALU.max, [den, emt], [den])
                    S.recip(rden[0:c], den[0:c], [den], [rden])
                    S.tt(hn[0:c], pF[0:c, :, 0:128], bc(rden[0:c], 2, [c, 4, 128]), ALU.mult, [pF, rden], [hn])
                    yield
                    S.af(osq2[0:c], hn[0:c].rearrange("p h v -> p (h v)"), AF.Square, [hn], [osq2])
                    S.red(oss2[0:c], osq2[0:c].rearrange("p (h v) -> p h v", h=4), ALU.add, [osq2], [oss2])
                    S.af(oss2[0:c], oss2[0:c], AF.Sqrt, [oss2], [oss2], scale=1.0 / 128.0, bias=EPS)
                    S.recip(orstd2[0:c], oss2[0:c], [oss2], [orstd2])
                    S.tt(hn[0:c], hn[0:c], bc(orstd2[0:c], 2, [c, 4, 128]), ALU.mult, [hn, orstd2], [hn])
                    S.tt(mgo[0:c, 512:1024], hn[0:c].rearrange("p h v -> p (h v)"), ml[0:c, 1024:1536], ALU.mult, [hn, ml], [mgo])
                    yield
                    S.tt(w_s[0:c], a_t[0:c], Mlbc[0:c], ALU.subtract, [a_t, Mlbc], [w_s])
                    S.af(w_s[0:c], w_s[0:c], AF.Exp, [w_s], [w_s])
                    S.tt(dec.ap(), m_bc.ap(), Mlbc.ap(), ALU.subtract, [m_bc, Mlbc], [dec])
                    S.af(dec.ap(), dec.ap(), AF.Exp, [dec], [dec])
                    S.tt(khat[0:c], ml[0:c, 256:512].rearrange("p (h d) -> p h d", h=4), bc(w_s[0:c], 2, [c, 4, 64]), ALU.mult, [ml, w_s], [khat])
                    for hh in range(4):
                        S.mm(pF[:, hh, 0:129], khat[0:c, hh, :], vext[0:c, hh, :], [khat, vext], [pF])
                    S.tt(Cn.ap(), Cn.ap(), bc(dec.ap(), 2, [64, 4, 129]), ALU.mult, [Cn, dec], [Cn])
                    S.tt(Cn.ap(), Cn.ap(), pF[:, :, 0:129], ALU.add, [Cn, pF], [Cn])
                    S.cp(Cnb.ap(), Cn.ap(), [Cn], [Cnb], e="act")
                    S.tt(m_bc.ap(), Mlbc.ap(), Flb.ap(), ALU.add, [Mlbc, Flb], [m_bc])
                    S.tt(m_row.ap(), M_row[:, c - 1:c], fl_row[:, 0:1], ALU.add, [M_row, fl_row], [m_row])

                interleave([hgrn_chunk(), mlstm_chunk()])
                S.dma(MG0[row:row + c, :], mgo[0:c], [mgo], [], q="pool")
                row += c
            S.dma(o_hS[si].rearrange("h k v -> k h v"), Sh, [Sh], [], q="pool")
            S.dma(o_mC[si].rearrange("h k v -> k h v"), Cn[:, :, 0:128], [Cn], [], q="pool")
            S.dma(o_mn[si].rearrange("h (k o) -> k h o", o=1), Cn[:, :, 128:129], [Cn], [], q="pool", allow_slow_non_contiguous=True)
            S.dma(o_mm[si].rearrange("(h o) -> h o", o=1), m_row, [m_row], [], q="pool")
        S.pop()

    def phase_d3ab(layer, MG, KM, w_out_d, xsrc):
        S.push()
        nkm = KM // 128
        Wo = S.sb("Wo", [128, nkm, D], BF16)
        W1 = S.sb("W1", [128, 8, 2 * FF], BF16)
        wn = S.sb("wn", [128, D], F32)
        junk = S.sb("junk", [128, D], F32)
        mgt = [S.sb("mgt%d" % i, [128, KM], BF16) for i in range(2)]
        mT = [S.sb("mT%d" % i, [128, nkm, 128], BF16) for i in range(2)]
        xt = [S.sb("xt%d" % i, [128, D], F32) for i in range(2)]
        x1 = [S.sb("x1%d" % i, [128, D], F32) for i in range(2)]
        ss = [S.sb("ss%d" % i, [128, 1], F32) for i in range(2)]
        rstd = [S.sb("rstd%d" % i, [128, 1], F32) for i in range(2)]
        h = [S.sb("h%d" % i, [128, D], BF16) for i in range(2)]
        hT = [S.sb("hT%d" % i, [128, 8, 128], BF16) for i in range(2)]
        sg = [S.sb("sg%d" % i, [128, 512], F32) for i in range(2)]
        act = [S.sb("act%d" % i, [128, FF], BF16) for i in range(2)]
        ptb = [S.ps("ptb%d" % i, [128, 8, 128], BF16) for i in range(2)]
        pms = [S.ps("pm%d" % i, [128, 512], F32) for i in range(6)]
        load_weight(S, Wo, w_out_d, KM)
        load_weight(S, W1, ffn_w_in[layer], D)
        load_bcast(S, wn, norm_ffn[layer:layer + 1, :], D)
        S.barrier()
        box = [0, 0]

        def stA1(i):
            pmi = box[0]
            b = i % 2
            r0 = i * 128
            S.dma(mgt[b], MG[r0:r0 + 128, :], [], [mgt[b]])
            S.dma(xt[b], xsrc[r0:r0 + 128, :], [], [xt[b]])
            transpose_tile(S, mgt[b], nkm, ptb[b], mT[b], Ibf)
            x1b, xb = x1[b], xt[b]
            blocks = []
            for cb in range(2):
                def epi(pm, cb=cb):
                    S.tt(x1b[:, cb * 512:(cb + 1) * 512], xb[:, cb * 512:(cb + 1) * 512], pm.ap(), ALU.add, [xb, pm], [x1b])
                blocks.append((cb * 512, 512, epi))
            box[0] = dense_blocks(mT[b], nkm, Wo, blocks, pms, pmi)
            S.dma(X1[r0:r0 + 128, :], x1b, [x1b], [], q="pool")
            rmsnorm_tile(S, x1b, wn, h[b], junk, ss[b], rstd[b])

        def stA2(i):
            b = i % 2
            transpose_tile(S, h[b], 8, ptb[b], hT[b], Ibf)

        def stB(i):
            pmi, sgi = box
            b = i % 2
            r0 = i * 128
            ab = act[b]
            for o in range(0, FF, 512):
                n = min(512, FF - o)
                pg = pms[pmi % 6]
                pu = pms[(pmi + 1) % 6]
                pmi += 2
                for k in range(8):
                    S.mm(pg[:, 0:n], hT[b][:, k, :], W1[:, k, o:o + n], [hT[b], W1], [pg], start=(k == 0), stop=(k == 7))
                for k in range(8):
                    S.mm(pu[:, 0:n], hT[b][:, k, :], W1[:, k, FF + o:FF + o + n], [hT[b], W1], [pu], start=(k == 0), stop=(k == 7))
                s_ = sg[sgi % 2]
                sgi += 1
                S.af(s_[:, 0:n], pg[:, 0:n], AF.Silu, [pg], [s_])
                S.tt(ab[:, o:o + n], s_[:, 0:n], pu[:, 0:n], ALU.mult, [s_, pu], [ab])
            box[0], box[1] = pmi, sgi
            S.dma(ACTF[r0:r0 + 128, :], ab, [ab], [], q="pool")

        run_pipelined(ntile, stA1, stB, stA2)
        S.pop()

    def phase_d3c(layer, xdst, final):
        S.push()
        nk = FF // 128
        W2 = S.sb("W2", [128, nk, D], BF16)
        wn = S.sb("wn", [128, D], F32)
        junk = S.sb("junk", [128, D], F32)
        at = [S.sb("at%d" % i, [128, FF], BF16) for i in range(2)]
        aT = [S.sb("aT%d" % i, [128, nk, 128], BF16) for i in range(2)]
        xt = [S.sb("xt%d" % i, [128, D], F32) for i in range(2)]
        x2 = [S.sb("x2%d" % i, [128, D], F32) for i in range(2)]
        yt = [S.sb("yt%d" % i, [128, D], F32) for i in range(2)]
        ss = [S.sb("ss%d" % i, [128, 1], F32) for i in range(2)]
        rstd = [S.sb("rstd%d" % i, [128, 1], F32) for i in range(2)]
        ptb = [S.ps("ptb%d" % i, [128, 8, 128], BF16) for i in range(2)]
        pms = [S.ps("pm%d" % i, [128, 512], F32) for i in range(4)]
        load_weight(S, W2, ffn_w_out[layer], FF)
        if final:
            load_bcast(S, wn, norm_final[0:1, :], D)
        S.barrier()
        box = [0]

        def stA1(i):
            b = i % 2
            r0 = i * 128
            S.dma(at[b], ACTF[r0:r0 + 128, :], [], [at[b]])
            S.dma(xt[b], X1[r0:r0 + 128, :], [], [xt[b]])

        def stA2(i):
            b = i % 2
            transpose_tile(S, at[b], nk, ptb[b], aT[b], Ibf)

        def stB(i):
            pmi = box[0]
            b = i % 2
            r0 = i * 128
            x2b, xb = x2[b], xt[b]
            blocks = []
            for cb in range(2):
                def epi(pm, cb=cb):
                    S.tt(x2b[:, cb * 512:(cb + 1) * 512], xb[:, cb * 512:(cb + 1) * 512], pm.ap(), ALU.add, [xb, pm], [x2b])
                blocks.append((cb * 512, 512, epi))
            box[0] = dense_blocks(aT[b], nk, W2, blocks, pms, pmi)
            if not final:
                S.dma(xdst[r0:r0 + 128, :], x2b, [x2b], [], q="pool")
            else:
                S.af(junk.ap(), x2b.ap(), AF.Square, [x2b], [junk, ss[b]], accum=ss[b].ap())
                S.af(ss[b].ap(), ss[b].ap(), AF.Sqrt, [ss[b]], [ss[b]], scale=1.0 / D, bias=EPS)
                S.recip(rstd[b].ap(), ss[b].ap(), [ss[b]], [rstd[b]])
                S.stt(yt[b].ap(), x2b.ap(), rstd[b].ap(), wn.ap(), ALU.mult, ALU.mult, [x2b, rstd[b], wn], [yt[b]])
                S.dma(xdst[r0:r0 + 128, :], yt[b], [yt[b]], [], q="pool")

        run_pipelined(ntile, stA1, stB, stA2)
        S.pop()

    def phase_d1_odd():
        S.push()
        W = S.sb("W", [128, 8, ODD_COLS], BF16)
        wn = S.sb("wn", [128, D], F32)
        gnw = S.sb("gnw", [128, 128], F32)
        negA = S.sb("negA", [128, 16], F32)
        dtb = S.sb("dtb", [128, 16], F32)
        junk = S.sb("junk", [128, D], F32)
        tmp = [S.sb("tmp%d" % i, [128, 512], F32) for i in range(2)]
        t16 = S.sb("t16", [128, 16], F32)
        xt = [S.sb("xt%d" % i, [128, D], F32) for i in range(2)]
        ss = [S.sb("ss%d" % i, [128, 1], F32) for i in range(2)]
        rstd = [S.sb("rstd%d" % i, [128, 1], F32) for i in range(2)]
        h = [S.sb("h%d" % i, [128, D], BF16) for i in range(2)]
        hT = [S.sb("hT%d" % i, [128, 8, 128], BF16) for i in range(2)]
        pq = [S.sb("pq%d" % i, [128, 4096], F32) for i in range(2)]
        gzt = [S.sb("gzt%d" % i, [128, 2, 1040], F32) for i in range(2)]
        ptb = [S.ps("ptb%d" % i, [128, 8, 128], BF16) for i in range(2)]
        pms = [S.ps("pm%d" % i, [128, 512], F32) for i in range(6)]
        load_weight(S, W, odd_w_in, D)
        load_bcast(S, wn, norm_mix[1:2, :], D)
        load_bcast(S, gnw, gdn_norm[0:1, :], 128)
        load_bcast(S, negA, gdn_a_log[0:1, :], 16)
        load_bcast(S, dtb, gdn_dt_bias[0:1, :], 16)
        S.af(negA.ap(), negA.ap(), AF.Exp, [negA], [negA])
        S.ts(negA.ap(), negA.ap(), -1.0, ALU.mult, [negA], [negA])
        S.barrier()
        box = [0]

        def stA1(i):
            b = i % 2
            S.dma(xt[b], X2[i * 128:i * 128 + 128, :], [], [xt[b]])
            rmsnorm_tile(S, xt[b], wn, h[b], junk, ss[b], rstd[b])

        def stA2(i):
            b = i % 2
            transpose_tile(S, h[b], 8, ptb[b], hT[b], Ibf)

        def stB(i):
            pmi = box[0]
            b = i % 2
            r0 = i * 128
            pqb, gz = pq[b], gzt[b]
            blocks = []
            for cb in range(8):
                def epi(pm, cb=cb):
                    S.cp(pqb[:, cb * 512:(cb + 1) * 512], pm.ap(), [pm], [pqb], e=("act" if cb % 2 else "dve"))
                blocks.append((cb * 512, 512, epi))
            for cb in range(4):
                def epi(pm, cb=cb):
                    t_ = tmp[cb % 2]
                    S.af(t_.ap(), pm.ap(), AF.Silu, [pm], [t_])
                    S.tt(gz[:, cb // 2, (cb % 2) * 512:(cb % 2) * 512 + 512].rearrange("p (h v) -> p h v", h=4), t_.ap().rearrange("p (h v) -> p h v", h=4),
                         bc(gnw.ap(), 1, [128, 4, 128]), ALU.mult, [t_, gnw], [gz])
                blocks.append((4096 + cb * 512, 512, epi))

            def epi_ba(pm):
                S.af(gz[:, :, 1024:1032], pm[:, 0:16].rearrange("p (g h) -> p g h", g=2), AF.Sigmoid, [pm], [gz])
                S.tt(t16.ap(), pm[:, 16:32], dtb.ap(), ALU.add, [pm, dtb], [t16])
                S.af(t16.ap(), t16.ap(), AF.Exp, [t16], [t16])
                S.af(t16.ap(), t16.ap(), AF.Ln, [t16], [t16], bias=1.0)
                S.tt(gz[:, :, 1032:1040], t16.ap().rearrange("p (g h) -> p g h", g=2), negA.ap().rearrange("p (g h) -> p g h", g=2), ALU.mult, [t16, negA], [gz])
            blocks.append((6144, 32, epi_ba))
            box[0] = dense_blocks(hT[b], 8, W, blocks, pms, pmi)
            S.dma(PQ[3 + r0:3 + r0 + 128, :], pqb, [pqb], [], q="pool")
            for g in range(2):
                S.dma(GD[r0:r0 + 128, g * 3088 + 2048:g * 3088 + 3088], gz[:, g, :], [gz], [], q="pool")

        run_pipelined(ntile, stA1, stB, stA2)
        S.pop()
        S.push()
        zr = S.sb("zr", [3, 4096], F32)
        cs = [S.sb("cs%d" % i, [3, 4096], F32) for i in range(NSEQ)]
        S.memset(zr, 0.0)
        for si, sq in enumerate(seqs):
            st = sq["start"]
            if sq["init"] is None:
                S.dma(PQ[3 + st - 3:3 + st, :], zr, [zr], [])
            else:
                S.dma(cs[si], st_gc[sq["init"]], [], [cs[si]])
                S.dma(PQ[3 + st - 3:3 + st, :], cs[si], [cs[si]], [])
        for si, sq in enumerate(seqs):
            e = sq["start"] + sq["len"]
            t_ = S.sb("co%d" % si, [3, 4096], F32)
            S.dma(t_, PQ[3 + e - 3:3 + e, :], [], [t_])
            S.dma(o_gc[si], t_, [t_], [], q="pool")
        S.pop()

    def phase_c1():
        S.push()
        cw = [S.sb("cw%d" % j, [128, 4096], F32) for j in range(4)]
        pieces = [(0, 1024, "dve", 8, True), (1024, 1024, "dve", 8, False), (2048, 512, "dve", 0, False), (2560, 1536, "pool", 0, False)]
        bufA = [[S.sb("xa%d_%d" % (j, i), [128, 1024], F32) for j in range(4)] for i in range(3)]
        bufB = [[S.sb("xb%d_%d" % (j, i), [128, 512], F32) for j in range(4)] for i in range(2)]
        bufC = [[S.sb("xc%d_%d" % (j, i), [128, 1536], F32) for j in range(4)] for i in range(2)]
        t2 = S.sb("t2", [128, 1024], F32)
        ssq = S.sb("ssq", [128, 8], F32)
        rs = S.sb("rs", [128, 8], F32)
        for j in range(4):
            load_bcast(S, cw[j], odd_conv_w[j:j + 1, :], 4096)
        S.barrier()
        ia = 0
        for i in range(ntile):
            r0 = i * 128
            for pi, (c0, ncol, e, nl2, qs) in enumerate(pieces):
                if pi < 2:
                    xw = bufA[ia % 3]
                    ia += 1
                elif pi == 2:
                    xw = bufB[i % 2]
                else:
                    xw = bufC[i % 2]
                sl = slice(c0, c0 + ncol)
                S.dma(xw[3], PQ[r0 + 3:r0 + 131, sl], [], [xw[3]])
                for j in range(3):
                    d = 3 - j
                    S.dma(xw[j][0:d, :], PQ[r0 + j:r0 + 3, sl], [], [xw[j]])
                    S.dma(xw[j][d:128, :], xw[3][0:128 - d, :], [xw[3]], [xw[j]])
                a = xw[0]
                S.tt(a.ap(), a.ap(), cw[0][:, sl], ALU.mult, [a, cw[0]], [a], e=e)
                for j in range(1, 4):
                    S.tt(xw[j].ap(), xw[j].ap(), cw[j][:, sl], ALU.mult, [xw[j], cw[j]], [xw[j]], e=e)
                    S.tt(a.ap(), a.ap(), xw[j].ap(), ALU.add, [a, xw[j]], [a], e=e)
                S.af(a.ap(), a.ap(), AF.Silu, [a], [a])
                if nl2:
                    S.af(t2.ap(), a.ap(), AF.Square, [a], [t2])
                    S.red(ssq.ap(), t2.ap().rearrange("p (h d) -> p h d", h=8), ALU.add, [t2], [ssq])
                    S.af(ssq.ap(), ssq.ap(), AF.Sqrt, [ssq], [ssq], bias=EPS)
                    S.recip(rs.ap(), ssq.ap(), [ssq], [rs])
                    if qs:
                        S.ts(rs.ap(), rs.ap(), 128.0 ** -0.5, ALU.mult, [rs], [rs])
                    S.tt(a.ap().rearrange("p (h d) -> p h d", h=8), a.ap().rearrange("p (h d) -> p h d", h=8),
                         bc(rs.ap(), 2, [128, 8, 128]), ALU.mult, [a, rs], [a])
                if pi < 2:
                    off = 0 if pi == 0 else 512
                    for g in range(2):
                        S.dma(GD[r0:r0 + 128, g * 3088 + off:g * 3088 + off + 512], a[:, 512 * g:512 * g + 512], [a], [], q="pool")
                elif pi == 2:
                    S.dma(GD[r0:r0 + 128, 1024:1536], a, [a], [], q="pool")
                else:
                    S.dma(GD[r0:r0 + 128, 1536:2048], a[:, 0:512], [a], [], q="pool")
                    S.dma(GD[r0:r0 + 128, 3088 + 1024:3088 + 2048], a[:, 512:1536], [a], [], q="pool")
        S.pop()

    def phase_r1():
        import os as _os
        S.push()
        Ublk = S.sb("Ublk", [128, 128], F32)
        SLblk = S.sb("SLblk", [128, 128], F32)
        rep = S.sb("rep", [128, 3, 64], F32)
        Ibf2 = S.sb("Ibf2", [128, 64], BF16)
        Sg = S.sb("Sg", [128, 16, 128], F32)
        Sgb = [S.sb("Sgb%d" % i, [128, 16, 128], BF16) for i in range(2)]
        gd = [S.sb("gd%d" % i, [128, 3088], F32) for i in range(3)]
        mg = [S.sb("mg%d" % i, [128, 1024], BF16) for i in range(2)]
        eGG = S.sb("eGG", [128, 16], F32)
        gg2 = S.sb("gg2", [128, 16], F32)
        eGlast = [S.sb("eGlast%d" % i, [128, 16], F32) for i in range(2)]
        gU = S.sb("gU", [128, 8, 64], F32)
        E = S.sb("E", [128, 8, 64], F32)
        qkb = S.sb("qkb", [128, 1024], BF16)
        qkT = S.sb("qkT", [128, 8, 2, 64], BF16)
        T1 = S.sb("T1", [128, 8, 64], F32)
        X = [S.sb("X%d" % i, [128, 8, 64], BF16) for i in range(2)]
        Y = [S.sb("Y%d" % i, [128, 8, 64], BF16) for i in range(2)]
        Q = [S.sb("Q%d" % i, [128, 8, 64], BF16) for i in range(2)]
        TT = [S.sb("TT%d" % i, [128, 8, 64], BF16) for i in range(2)]
        attnT = [S.sb("attnT%d" % i, [128, 8, 64], BF16) for i in range(2)]
        nbeta = S.sb("nbeta", [128, 8], F32)
        IeG2 = S.sb("IeG2", [128, 16, 64], F32)
        kTt = [S.sb("kTt%d" % i, [128, 16, 64], BF16) for i in range(2)]
        qTt = [S.sb("qTt%d" % i, [128, 16, 64], BF16) for i in range(2)]
        khat = [S.sb("khat%d" % i, [128, 8, 128], BF16) for i in range(2)]
        yv = S.sb("yv", [128, 8, 128], BF16)
        vnew = S.sb("vnew", [128, 8, 128], BF16)
        osq = S.sb("osq", [128, 8, 128], F32)
        oss = S.sb("oss", [128, 8], F32)
        orstd = S.sb("orstd", [128, 8], F32)
        on = S.sb("on", [128, 8, 128], F32)
        pB = [S.ps("pB%d" % i, [128, 8, 128], F32) for i in range(2)]
        pA = [S.ps("pAa%d" % i, [128, 512], F32) for i in range(4)]
        S.dma(Ublk, c_Ublk, [], [Ublk])
        S.dma(SLblk, c_SLblk, [], [SLblk])
        S.dma(rep, c_rep, [], [rep])
        S.cp(Ibf2.ap(), rep[:, 2, :], [rep], [Ibf2])
        for t_ in gd + [gg2, eGG, gU, E, T1, nbeta, qkb, yv, vnew, osq, oss, orstd, on] + X + Y + Q + TT + attnT + khat + mg + eGlast:
            S.memset(t_, 0.0)
        for t_ in [IeG2] + kTt + qTt + [qkT]:
            S.memset(t_, 0.0, e="pool")
        for t_ in pB + pA:
            S.memset(t_, 0.0)
        U2, SU2, I2 = rep[:, 0, :], rep[:, 1, :], rep[:, 2, :]

        chunks = []
        for si, sq in enumerate(seqs):
            row = sq["start"]
            for ci, c in enumerate(sq["chunks"]):
                chunks.append((si, ci, c, row, ci == len(sq["chunks"]) - 1))
                row += c

        def part_a(n):
            si, ci, c, row, last = chunks[n]
            b = n % 2
            g_ = gd[n % 3]
            for g in range(2):
                S.dma(g_[64 * g:64 * g + c, :], GD[row:row + c, 3088 * g:3088 * g + 3088], [], [g_])
            yield
            beta = g_[:, 3072:3080]
            gg = g_[:, 3080:3088]
            kn4 = g_[:, 512:1024].rearrange("p (j d) -> p j d", j=4)
            a0, a1, a2, a3 = pA
            S.mm(a3[:, 0:8], Ublk.ap(), gg, [Ublk, g_], [a3])
            S.mm(a3[:, 8:16], SLblk.ap(), gg, [SLblk, g_], [a3])
            for g in range(2):
                S.cp(gg2[64 * g:64 * g + 64, 8 * g:8 * g + 8], gg[64 * g:64 * g + 64, :], [g_], [gg2], e="pool")
            S.mm(a3[:, 16:32], ones32.ap(), gg2.ap(), [ones32, gg2], [a3])
            S.tt(gU.ap(), bc(U2, 1, [128, 8, 64]), bc(gg, 2, [128, 8, 64]), ALU.mult, [rep, g_], [gU], e="pool")
            S.ts(nbeta.ap(), beta, -1.0, ALU.mult, [g_], [nbeta], e="pool")
            S.af(eGG.ap(), a3[:, 0:16], AF.Exp, [a3], [eGG])
            S.af(eGlast[b].ap(), a3[:, 16:32], AF.Exp, [a3], [eGlast[b]])
            S.cp(qkb.ap(), g_[:, 0:1024], [g_], [qkb], e="act")
            yield
            if _os.environ.get("R1_NODMT", "") != "1":
                S.mm(a0[:, 0:8 * c], SLblk.ap(), gU[:, :, 0:c], [SLblk, gU], [a0])
            pTv = a1.ap().bitcast(BF16).rearrange("p (j a t) -> p j a t", j=8, a=2)
            for g in range(2):
                ps_ = slice(64 * g, 64 * g + c)
                for jl in range(4):
                    j = 4 * g + jl
                    if g == 0:
                        S.tr(pTv[:, j, 0, 0:c], qkb[ps_, 512 + jl * 128:512 + (jl + 1) * 128], Ibf2[ps_, 0:c], [qkb, Ibf2], [a1])
                        S.tr(pTv[:, j, 1, 0:c], qkb[ps_, jl * 128:(jl + 1) * 128], Ibf2[ps_, 0:c], [qkb, Ibf2], [a1])
                    else:
                        for dh in range(2):
                            do = slice(64 * dh, 64 * dh + 64)
                            S.tr(pTv[do, j, 0, 0:c], qkb[ps_, 512 + jl * 128 + 64 * dh:512 + jl * 128 + 64 * dh + 64], Ibf2[ps_, 0:c], [qkb, Ibf2], [a1])
                            S.tr(pTv[do, j, 1, 0:c], qkb[ps_, jl * 128 + 64 * dh:jl * 128 + 64 * dh + 64], Ibf2[ps_, 0:c], [qkb, Ibf2], [a1])
            S.af(E[:, :, 0:c], a0[:, 0:8 * c].rearrange("p (h t) -> p h t", h=8), AF.Exp, [a0], [E])
            S.cp(qkT[:, :, :, 0:c], pTv[:, :, :, 0:c], [a1], [qkT])
            yield
            for g in range(2):
                ps_ = slice(64 * g, 64 * g + c)
                for jl in range(4):
                    j = 4 * g + jl
                    S.mm(a2[ps_, jl * 128:jl * 128 + 2 * c], qkT[:, j, 0, 0:c], qkT[:, j, :, 0:c], [qkT], [a2])
            S.tt(T1[:, :, 0:c], E[:, :, 0:c], bc(SU2[:, 0:c], 1, [128, 8, c]), ALU.mult, [E, rep], [T1], e="pool")
            S.tt(E[:, :, 0:c], E[:, :, 0:c], bc(U2[:, 0:c], 1, [128, 8, c]), ALU.mult, [E, rep], [E], e="pool")
            for g in range(2):
                S.tt(IeG2[64 * g:64 * g + 64, 8 * g:8 * g + 8, 0:c], bc(I2[64 * g:64 * g + 64, 0:c], 1, [64, 8, c]),
                     bc(eGG[64 * g:64 * g + 64, 0:8], 2, [64, 8, c]), ALU.mult, [rep, eGG], [IeG2], e="pool")
            yield
            pv = a2[:, :].rearrange("p (j x) -> p j x", j=4)[:, :, 0:2 * c].rearrange("p j (a t) -> p j a t", a=2)
            for r_ in range(2):
                hs = slice(r_, 8, 2)
                S.tt(T1[:, hs, 0:c], T1[:, hs, 0:c], pv[:, :, 0, :], ALU.mult, [T1, a2], [T1])
                S.tt(attnT[b][:, hs, 0:c], E[:, hs, 0:c], pv[:, :, 1, :], ALU.mult, [E, a2], [attnT[b]])
            X0, Y0, Q0 = X[0], Y[0], Q[0]
            S.tt(X0[:, :, 0:c], T1[:, :, 0:c], bc(nbeta.ap(), 2, [128, 8, c]), ALU.mult, [T1, nbeta], [X0])
            for hf in range(2):
                pp = pA[hf]
                S.mm(pp[:, 0:8 * c], ones32.ap(), IeG2[:, 8 * hf:8 * hf + 8, 0:c], [ones32, IeG2], [pp])
            yield
            pYv = a3.ap().bitcast(BF16)
            for g in range(2):
                ps_ = slice(64 * g, 64 * g + c)
                for hl in range(8):
                    S.tr(pYv[ps_, hl * c:(hl + 1) * c], X0[ps_, hl, 0:c], Ibf2[ps_, 0:c], [X0, Ibf2], [a3])
            S.cp(Y0[:, :, 0:c], pYv[:, 0:8 * c].rearrange("p (h t) -> p h t", h=8), [a3], [Y0], e="act")
            S.tt(Q0[:, :, 0:c], X0[:, :, 0:c], bc(I2[:, 0:c], 1, [128, 8, c]), ALU.add, [X0, rep], [Q0])
            for hf in range(2):
                pp = pA[hf]
                ev = pp[:, 0:8 * c].rearrange("p (h t) -> p h t", h=8)
                for r_ in range(2):
                    hsl = slice(8 * hf + r_, 8 * hf + 8, 2)
                    jsl = slice(4 * hf, 4 * hf + 4)
                    S.tt(kTt[b][:, hsl, 0:c], ev[:, r_:8:2, :], qkT[:, jsl, 0, 0:c], ALU.mult, [pp, qkT], [kTt[b]])
                    S.tt(qTt[b][:, hsl, 0:c], ev[:, r_:8:2, :], qkT[:, jsl, 1, 0:c], ALU.mult, [pp, qkT], [qTt[b]])
            for r_ in range(2):
                S.tt(khat[b][:, r_:8:2, :], kn4, bc(eGG[:, 8 + r_:16:2], 2, [128, 4, 128]), ALU.mult, [g_, eGG], [khat[b]], e="pool")
            nq = 5 if c == 64 else 3
            for lv in range(nq):
                yield
                Xc, Yc, Qc = X[lv % 2], Y[lv % 2], Q[lv % 2]
                Xn, Yn, Qn = X[(lv + 1) % 2], Y[(lv + 1) % 2], Q[(lv + 1) % 2]
                py, px, pq = a2, a3, (a0 if lv % 2 == 0 else a1)
                for hl in range(8):
                    for g in range(2):
                        ps_ = slice(64 * g, 64 * g + c)
                        S.mm(py[ps_, hl * c:(hl + 1) * c], Xc[ps_, hl, 0:c], Yc[ps_, hl, 0:c], [Xc, Yc], [py])
                S.cp(Yn[:, :, 0:c], py[:, 0:8 * c].rearrange("p (h t) -> p h t", h=8), [py], [Yn], e="act")
                if lv < nq - 1:
                    for hl in range(8):
                        for g in range(2):
                            ps_ = slice(64 * g, 64 * g + c)
                            S.mm(px[ps_, hl * c:(hl + 1) * c], Yc[ps_, hl, 0:c], Xc[ps_, hl, 0:c], [Xc, Yc], [px])
                    S.cp(Xn[:, :, 0:c], px[:, 0:8 * c].rearrange("p (h t) -> p h t", h=8), [px], [Xn], e="dve")
                for hl in range(8):
                    for g in range(2):
                        ps_ = slice(64 * g, 64 * g + c)
                        S.mm(pq[ps_, hl * c:(hl + 1) * c], Yn[ps_, hl, 0:c], Qc[ps_, hl, 0:c], [Yn, Qc], [pq])
                dst = TT[b] if lv == nq - 1 else Qn
                S.tt(dst[:, :, 0:c], Qc[:, :, 0:c], pq[:, 0:8 * c].rearrange("p (h t) -> p h t", h=8), ALU.add, [Qc, pq], [dst])

        sgi = [0]

        def part_b(n):
            si, ci, c, row, last = chunks[n]
            sq = seqs[si]
            b = n % 2
            g_, mgo = gd[n % 3], mg[b]
            if ci == 0:
                if sq["init"] is None:
                    S.memset(Sg, 0.0)
                else:
                    S.dma(Sg, st_gS[sq["init"]].rearrange("h k v -> k h v"), [], [Sg])
                S.cp(Sgb[sgi[0] % 2].ap(), Sg.ap(), [Sg], [Sgb[sgi[0] % 2]], e="act")
            Sb = Sgb[sgi[0] % 2]
            Sbn = Sgb[(sgi[0] + 1) % 2]
            sgi[0] += 1
            v3 = g_[:, 1024:2048].rearrange("p (h v) -> p h v", h=8)
            gz = g_[:, 2048:3072]
            beta = g_[:, 3072:3080]
            for hl in range(8):
                for g in range(2):
                    ps_ = slice(64 * g, 64 * g + c)
                    S.mm(pB[0][ps_, hl, :], kTt[b][:, 8 * g + hl, 0:c], Sb[:, 8 * g + hl, :], [kTt[b], Sb], [pB[0]])
            S.tt(yv.ap(), v3, pB[0].ap(), ALU.subtract, [g_, pB[0]], [yv])
            yield
            for hl in range(8):
                for g in range(2):
                    ps_ = slice(64 * g, 64 * g + c)
                    S.mm(pB[1][ps_, hl, :], TT[b][ps_, hl, 0:c], yv[ps_, hl, :], [TT[b], yv], [pB[1]])
            S.tt(vnew.ap(), pB[1].ap(), bc(beta, 2, [128, 8, 128]), ALU.mult, [pB[1], g_], [vnew])
            yield
            for g in range(2):
                ps_ = slice(64 * g, 64 * g + c)
                for hl in range(8):
                    S.mm(pB[g][:, hl, :], khat[b][ps_, hl, :], vnew[ps_, hl, :], [khat[b], vnew], [pB[g]])
                for hl in range(8):
                    hh = 8 * g + hl
                    S.stt(Sg[:, hh, :], Sg[:, hh, :], eGlast[b][:, hh:hh + 1], pB[g][:, hl, :], ALU.mult, ALU.add, [Sg, eGlast[b], pB[g]], [Sg])
            if not last:
                S.cp(Sbn.ap(), Sg.ap(), [Sg], [Sbn], e="act")
            yield
            for hl in range(8):
                for g in range(2):
                    ps_ = slice(64 * g, 64 * g + c)
                    S.mm(pB[0][ps_, hl, :], qTt[b][:, 8 * g + hl, 0:c], Sb[:, 8 * g + hl, :], [qTt[b], Sb], [pB[0]], start=True, stop=False)
                    S.mm(pB[0][ps_, hl, :], attnT[b][ps_, hl, 0:c], vnew[ps_, hl, :], [attnT[b], vnew], [pB[0]], start=False, stop=True)
            S.af(osq.ap(), pB[0].ap(), AF.Square, [pB[0]], [osq])
            yield
            S.red(oss.ap(), osq.ap(), ALU.add, [osq], [oss])
            S.af(oss.ap(), oss.ap(), AF.Sqrt, [oss], [oss], scale=1.0 / 128.0, bias=EPS)
            S.recip(orstd.ap(), oss.ap(), [oss], [orstd])
            for hl in range(8):
                S.af(on[:, hl, :], pB[0][:, hl, :], AF.Copy, [pB[0], orstd], [on], scale=orstd[:, hl:hl + 1])
            yield
            S.tt(mgo.ap(), on.ap().rearrange("p h v -> p (h v)"), gz, ALU.mult, [on, g_], [mgo])
            for g in range(2):
                S.dma(MG1[row:row + c, 1024 * g:1024 * g + 1024], mgo[64 * g:64 * g + c, :], [mgo], [], q="pool")
            if last:
                S.dma(o_gS[si].rearrange("h k v -> k h v"), Sg, [Sg], [], q="pool")

        def interleave(gens):
            gens = [g for g in gens if g is not None]
            while gens:
                for g in list(gens):
                    try:
                        next(g)
                    except StopIteration:
                        gens.remove(g)

        _mode = _os.environ.get("R1_MODE", "")
        _lim = int(_os.environ.get("R1_LIM", "99"))

        def lim(gen):
            for i, _ in enumerate(gen):
                if i + 1 >= _lim:
                    break
                yield

        if _mode == "":
            interleave([part_a(0)])
        for n in range(len(chunks)):
            if _mode == "":
                interleave([part_b(n), part_a(n + 1) if n + 1 < len(chunks) else None])
            elif _mode == "a_only":
                interleave([lim(part_a(n))])
        S.pop()

    phase_d1_even()
    phase_r0()
    phase_d3ab(0, MG0, 1024, even_w_out, xin)
    phase_d3c(0, X2, False)
    phase_d1_odd()
    phase_c1()
    phase_r1()
    phase_d3ab(1, MG1, 2048, odd_w_out, X2)
    phase_d3c(1, y, True)
    S.finish()
    return nc, seqs, NT, S


def make_consts():
    idx = np.arange(64)
    U = (idx[:, None] <= idx[None, :]).astype(np.float32)
    SU = (idx[:, None] < idx[None, :]).astype(np.float32)
    SL = (idx[:, None] > idx[None, :]).astype(np.float32)
    bm4 = np.zeros((4, 4, 64), np.float32)
    for r in range(4):
        bm4[r, r, :] = 1.0
    Z = np.zeros((64, 64), np.float32)
    Ublk = np.block([[U, Z], [Z, U]])
    SLblk = np.block([[SL, Z], [Z, SL]])
    I64 = np.eye(64, dtype=np.float32)
    rep = np.stack([np.concatenate([U, U], 0), np.concatenate([SU, SU], 0), np.concatenate([I64, I64], 0)], 1)
    return dict(c_ident=np.eye(128, dtype=np.float32), c_U=U, c_SU=SU, c_SL=SL, c_bm4=bm4,
                c_Ublk=np.ascontiguousarray(Ublk), c_SLblk=np.ascontiguousarray(SLblk), c_rep=np.ascontiguousarray(rep))


def make_in_maps(inp, T, NS, n_cores, seqs, NT):
    f = lambda a: np.ascontiguousarray(np.asarray(a, dtype=np.float32))
    xp = f(inp["x_prompt"])
    xs = f(inp["x_sample"])
    meta = f(inp["meta_tokens"])
    consts = make_consts()
    shared = dict(
        norm_mix=f(inp["norm_mix"]), norm_ffn=f(inp["norm_ffn"]), norm_final=f(inp["norm_final"]).reshape(1, D),
        even_w_in=f(inp["even_w_in"])[0], even_w_out=f(inp["even_w_out"])[0], lb_logits=f(inp["hgrn_lb_logits"]),
        hgrn_norm=f(inp["hgrn_norm"])[0:1], mlstm_bif=np.concatenate([f(inp["mlstm_b_i"])[0], f(inp["mlstm_b_f"])[0]]).reshape(1, 8),
        mlstm_norm=f(inp["mlstm_norm"])[0:1], odd_w_in=f(inp["odd_w_in"])[0], odd_conv_w=f(inp["odd_conv_w"])[0],
        gdn_a_log=f(inp["gdn_a_log"])[0:1], gdn_dt_bias=f(inp["gdn_dt_bias"])[0:1], gdn_norm=f(inp["gdn_norm"])[0:1],
        odd_w_out=f(inp["odd_w_out"])[0], ffn_w_in=f(inp["ffn_w_in"]), ffn_w_out=f(inp["ffn_w_out"]), **consts)
    B = xp.shape[0]
    maps = []
    for c in range(n_cores):
        xin = np.zeros((NT, D), np.float32)
        s0 = seqs[0]["start"]
        if c < B:
            xin[s0:s0 + 16] = meta
            xin[s0 + 16:s0 + 16 + T] = xp[c]
        sidx = [c * NS + i for i in range(NS)]
        for i, sj in enumerate(sidx):
            st = seqs[1 + i]["start"]
            if sj < xs.shape[0]:
                xin[st:st + 64] = xs[sj]

        def gather(a):
            a = f(a)[0]
            out = np.zeros((NS,) + a.shape[1:], np.float32)
            for i, sj in enumerate(sidx):
                if sj < a.shape[0]:
                    out[i] = a[sj]
            return out
        m = dict(shared)
        m.update(xin=xin, st_hS=gather(inp["state_hgrn_S"]), st_mC=gather(inp["state_mlstm_C"]), st_mn=gather(inp["state_mlstm_n"]),
                 st_mm=gather(inp["state_mlstm_m"]), st_gS=gather(inp["state_gdn_S"]), st_gc=gather(inp["state_gdn_conv"]))
        maps.append(m)
    return maps


_CACHE = {}


def run(inp, T, NS, n_cores, debug=False, trace=False):
    key = (T, NS, debug)
    if key not in _CACHE:
        _CACHE[key] = build(T, NS, debug)
    nc, seqs, NT, S = _CACHE[key]
    maps = make_in_maps(inp, T, NS, n_cores, seqs, NT)
    res = run_bass_kernel_spmd(nc, maps, core_ids=list(range(n_cores)), **({"trace": True} if trace else {}))
    return res, seqs, NT


def assemble(res, seqs, NT, B, T, NSAMP, NS, n_cores):
    R = res.results
    s0 = seqs[0]["start"]
    y_prompt = np.stack([R[c]["y"][s0 + 16:s0 + 16 + T] for c in range(B)], 0)
    y_sample = np.zeros((NSAMP, 64, D), np.float32)
    names = ["o_hS", "o_mC", "o_mn", "o_mm", "o_gS", "o_gc"]
    pst = {n: np.stack([R[c][n][0] for c in range(B)], 0)[None] for n in names}
    sst = {n: np.zeros((1, NSAMP) + R[0][n].shape[1:], np.float32) for n in names}
    for c in range(n_cores):
        for i in range(NS):
            sj = c * NS + i
            if sj >= NSAMP:
                continue
            st = seqs[1 + i]["start"]
            y_sample[sj] = R[c]["y"][st:st + 64]
            for n in names:
                sst[n][0, sj] = R[c][n][1 + i]
    outs = [y_prompt, y_sample] + [pst[n] for n in names] + [sst[n] for n in names]
    return tuple(np.ascontiguousarray(o.astype(np.float32)) for o in outs)


def kernel(**inputs):
    B = inputs["x_prompt"].shape[0]
    T = inputs["x_prompt"].shape[1]
    NSAMP = inputs["x_sample"].shape[0]
    NS = -(-NSAMP // N_CORES)
    res, seqs, NT = run(inputs, T, NS, N_CORES)
    return assemble(res, seqs, NT, B, T, NSAMP, NS, N_CORES)
```

```python
import numpy as np
from contextlib import ExitStack
import concourse.bass as bass
import concourse.mybir as mybir
from concourse.bass_utils import run_bass_kernel_spmd

F32 = mybir.dt.float32
BF16 = mybir.dt.bfloat16
AF = mybir.ActivationFunctionType
ALU = mybir.AluOpType
AX = mybir.AxisListType

D = 1024
FF = 2816
EVEN_COLS = 3592
ODD_COLS = 6176
EPS = 1e-6
N_CORES = 8
T_FULL = 8192
NS_FULL = 2
DBL_BF16 = True


class TB:
    def __init__(self, t, name):
        self.t = t
        self.name = name
        self.lw = None
        self.rd = {}

    def ap(self):
        return self.t[:]

    def __getitem__(self, idx):
        return self.t[idx]


class Sched:
    def __init__(self, nc, n_dma_slots=8, same_engine_sync=True):
        self.nc = nc
        self.es0 = ExitStack()
        self.stack = [self.es0]
        self.eng = {"pe": nc.tensor, "act": nc.scalar, "dve": nc.vector, "pool": nc.gpsimd, "sp": nc.sync}
        self.sems = {}
        self.cnt = {}
        for e in ("pe", "act", "dve", "pool"):
            self.sems[e] = self.es0.enter_context(nc.semaphore("s_" + e))
            self.cnt[e] = 0
        self.nslots = n_dma_slots
        self.dma_q = {}
        for q in ("sp", "pool"):
            slots = []
            for i in range(n_dma_slots):
                key = ("dma", q, i)
                self.sems[key] = self.es0.enter_context(nc.semaphore("d_%s%d" % (q, i)))
                self.cnt[key] = 0
                slots.append(key)
            self.dma_q[q] = [slots, 0]
        self.waited = {e: {} for e in self.eng}
        self.same = same_engine_sync
        self.noself = {"pe"}
        self.ninst = 0

    def push(self):
        self.phase = getattr(self, "phase", 0) + 1
        self.sb_bytes = getattr(self, "sb_base", 0)
        self.stack.append(ExitStack())

    def pop(self):
        self.barrier()
        self.stack.pop().close()

    def sb(self, name, shape, dtype):
        nb = int(np.prod(shape[1:])) * (2 if dtype == BF16 else 4)
        nb = -(-nb // 32) * 32
        if len(self.stack) == 1:
            self.sb_base = getattr(self, "sb_base", 0) + nb
        self.sb_bytes = getattr(self, "sb_bytes", 0) + nb
        assert self.sb_bytes <= 200 * 1024, ("SBUF budget exceeded", name, self.sb_bytes)
        t = self.stack[-1].enter_context(self.nc.sbuf_tensor("p%d_%s" % (getattr(self, "phase", 0), name), list(shape), dtype))
        return TB(t, name)

    def ps(self, name, shape, dtype):
        t = self.stack[-1].enter_context(self.nc.psum_tensor("p%d_%s" % (getattr(self, "phase", 0), name), list(shape), dtype))
        return TB(t, name)

    def _wait(self, e, k, v):
        if v <= 0 or self.waited[e].get(k, 0) >= v:
            return
        self.eng[e].wait_ge(self.sems[k], v)
        self.waited[e][k] = v
        self.ninst += 1

    def barrier(self):
        for e in self.eng:
            for k, v in self.cnt.items():
                if k == e:
                    continue
                self._wait(e, k, v)

    def _deps(self, e, reads, writes):
        need = {}

        def add(ev):
            if ev is None:
                return
            k, v = ev
            if need.get(k, 0) < v:
                need[k] = v

        for b in reads:
            add(b.lw)
        for b in writes:
            add(b.lw)
            for k, v in b.rd.items():
                add((k, v))
        for k, v in need.items():
            if k == e and (e in self.noself or not self.same):
                continue
            self._wait(e, k, v)

    def _commit(self, ev, reads, writes):
        k, v = ev
        for b in reads:
            if b.rd.get(k, 0) < v:
                b.rd[k] = v
        for b in writes:
            b.lw = ev
            b.rd = {}

    def op(self, e, fn, reads, writes):
        self._deps(e, reads, writes)
        inst = fn()
        self.cnt[e] += 1
        inst.then_inc(self.sems[e], 1)
        self._commit((e, self.cnt[e]), reads, writes)
        self.ninst += 1
        return inst

    def dma(self, out, in_, reads, writes, q="sp", **kw):
        slots, n = self.dma_q[q]
        key = slots[n % self.nslots]
        self.dma_q[q][1] = n + 1
        self._wait(q, key, self.cnt[key])
        self._deps(q, reads, writes)
        o = out.ap() if isinstance(out, TB) else out
        i = in_.ap() if isinstance(in_, TB) else in_
        inst = self.eng[q].dma_start(out=o, in_=i, **kw)
        self.cnt[key] += 16
        inst.then_inc(self.sems[key], 16)
        self._commit((key, self.cnt[key]), reads, writes)
        self.ninst += 1
        return inst

    def finish(self):
        self.barrier()
        while self.stack:
            self.stack.pop().close()

    def mm(self, out, lhsT, rhs, R, W, start=True, stop=True):
        nc = self.nc
        return self.op("pe", lambda: nc.tensor.matmul(out, lhsT=lhsT, rhs=rhs, start=start, stop=stop), R, W)

    def tr(self, out, in_, ident, R, W):
        nc = self.nc
        return self.op("pe", lambda: nc.tensor.transpose(out, in_, ident), R, W)

    def tt(self, out, a, b, op, R, W, e="dve"):
        eo = self.nc.vector if e == "dve" else self.nc.gpsimd
        return self.op(e, lambda: eo.tensor_tensor(out=out, in0=a, in1=b, op=op), R, W)

    def ts(self, out, a, s1, op0, R, W, s2=None, op1=None, e="dve"):
        eo = self.nc.vector if e == "dve" else self.nc.gpsimd
        if op1 is None:
            return self.op(e, lambda: eo.tensor_scalar(out=out, in0=a, scalar1=s1, scalar2=None, op0=op0), R, W)
        return self.op(e, lambda: eo.tensor_scalar(out=out, in0=a, scalar1=s1, scalar2=s2, op0=op0, op1=op1), R, W)

    def stt(self, out, a, s, b, op0, op1, R, W):
        nc = self.nc
        return self.op("dve", lambda: nc.vector.scalar_tensor_tensor(out=out, in0=a, scalar=s, in1=b, op0=op0, op1=op1), R, W)

    def af(self, out, in_, func, R, W, scale=None, bias=None, accum=None):
        nc = self.nc
        kw = {}
        if scale is not None:
            kw["scale"] = scale
        if bias is not None:
            kw["bias"] = bias
        if accum is not None:
            kw["accum_out"] = accum
        return self.op("act", lambda: nc.scalar.activation(out=out, in_=in_, func=func, **kw), R, W)

    def cp(self, out, in_, R, W, e="dve"):
        nc = self.nc
        if e == "act":
            return self.op("act", lambda: nc.scalar.copy(out=out, in_=in_), R, W)
        eo = nc.vector if e == "dve" else nc.gpsimd
        return self.op(e, lambda: eo.tensor_copy(out=out, in_=in_), R, W)

    def red(self, out, in_, op, R, W):
        nc = self.nc
        return self.op("dve", lambda: nc.vector.tensor_reduce(out=out, in_=in_, axis=AX.X, op=op), R, W)

    def recip(self, out, in_, R, W):
        nc = self.nc
        return self.op("dve", lambda: nc.vector.reciprocal(out=out, in_=in_), R, W)

    def memset(self, tb, val, e="dve"):
        eo = self.nc.vector if e == "dve" else self.nc.gpsimd
        return self.op(e, lambda: eo.memset(tb.ap(), val), [], [tb])


def bc(ap, axis, shape):
    return ap.unsqueeze(axis).broadcast_to(list(shape))


def seq_layout(T, NS):
    seqs = []
    r = 3
    seqs.append(dict(start=r, len=16 + T, chunks=[16] + [64] * (T // 64), init=None))
    r += 16 + T
    for i in range(NS):
        r += 3
        seqs.append(dict(start=r, len=64, chunks=[64], init=i))
        r += 64
    NT = -(-r // 128) * 128
    return seqs, NT


def load_weight(S, Wsb, Wd, K):
    for k in range(K // 128):
        S.dma(Wsb[:, k, :], Wd[k * 128:(k + 1) * 128, :], [], [], q="pool")


def rmsnorm_tile(S, xt, wbc, h, junk, ss, rstd):
    S.af(junk.ap(), xt.ap(), AF.Square, [xt], [junk, ss], accum=ss.ap())
    S.af(ss.ap(), ss.ap(), AF.Sqrt, [ss], [ss], scale=1.0 / D, bias=EPS)
    S.recip(rstd.ap(), ss.ap(), [ss], [rstd])
    S.stt(h.ap(), xt.ap(), rstd.ap(), wbc.ap(), ALU.mult, ALU.mult, [xt, rstd, wbc], [h])


def transpose_tile(S, src, nk, ptb, dst, ident, e="dve"):
    for k0 in range(0, nk, 8):
        n = min(8, nk - k0)
        for k in range(n):
            S.tr(ptb[:, k, :], src[:, (k0 + k) * 128:(k0 + k + 1) * 128], ident.ap(), [src, ident], [ptb])
        S.cp(dst[:, k0:k0 + n, :], ptb[:, 0:n, :], [ptb], [dst], e=e)


def run_pipelined(n, A1, B, A2):
    A1(0)
    A2(0)
    for i in range(n):
        if i + 1 < n:
            A1(i + 1)
        B(i)
        if i + 1 < n:
            A2(i + 1)


def load_bcast(S, tb, dram_row, n):
    S.dma(tb, dram_row.broadcast_to([128, n]), [], [tb])


def build(T=T_FULL, NS=NS_FULL, debug=False):
    nc = bass.Bass("TRN2", target_bir_lowering=False)
    seqs, NT = seq_layout(T, NS)
    NSEQ = 1 + NS
    ntile = NT // 128

    def din(name, shape, dt=F32):
        return nc.dram_tensor(name, list(shape), dt, kind="ExternalInput").ap()

    def dout(name, shape, dt=F32):
        return nc.dram_tensor(name, list(shape), dt, kind="ExternalOutput").ap()

    def dscr(name, shape, dt=F32):
        return nc.dram_tensor(name, list(shape), dt, kind="ExternalOutput" if debug else "Internal").ap()

    xin = din("xin", [NT, D])
    st_hS = din("st_hS", [NS, 4, 128, 128])
    st_mC = din("st_mC", [NS, 4, 64, 128])
    st_mn = din("st_mn", [NS, 4, 64])
    st_mm = din("st_mm", [NS, 4])
    st_gS = din("st_gS", [NS, 16, 128, 128])
    st_gc = din("st_gc", [NS, 3, 4096])
    norm_mix = din("norm_mix", [2, D])
    norm_ffn = din("norm_ffn", [2, D])
    norm_final = din("norm_final", [1, D])
    even_w_in = din("even_w_in", [D, EVEN_COLS])
    even_w_out = din("even_w_out", [D, D])
    lb_logits = din("lb_logits", [2, 512])
    hgrn_norm = din("hgrn_norm", [1, 512])
    mlstm_bif = din("mlstm_bif", [1, 8])
    mlstm_norm = din("mlstm_norm", [1, 512])
    odd_w_in = din("odd_w_in", [D, ODD_COLS])
    odd_conv_w = din("odd_conv_w", [4, 4096])
    gdn_a_log = din("gdn_a_log", [1, 16])
    gdn_dt_bias = din("gdn_dt_bias", [1, 16])
    gdn_norm = din("gdn_norm", [1, 128])
    odd_w_out = din("odd_w_out", [2048, D])
    ffn_w_in = din("ffn_w_in", [2, D, 2 * FF])
    ffn_w_out = din("ffn_w_out", [2, FF, D])
    c_ident = din("c_ident", [128, 128])
    c_U = din("c_U", [64, 64])
    c_SU = din("c_SU", [64, 64])
    c_SL = din("c_SL", [64, 64])
    c_bm4 = din("c_bm4", [4, 4, 64])
    c_Ublk = din("c_Ublk", [128, 128])
    c_SLblk = din("c_SLblk", [128, 128])
    c_rep = din("c_rep", [128, 3, 64])

    y = dout("y", [NT, D])
    o_hS = dout("o_hS", [NSEQ, 4, 128, 128])
    o_mC = dout("o_mC", [NSEQ, 4, 64, 128])
    o_mn = dout("o_mn", [NSEQ, 4, 64])
    o_mm = dout("o_mm", [NSEQ, 4])
    o_gS = dout("o_gS", [NSEQ, 16, 128, 128])
    o_gc = dout("o_gc", [NSEQ, 3, 4096])

    HG = dscr("HG", [NT, 2560])
    ML = dscr("ML", [NT, 1544])
    MG0 = dscr("MG0", [NT, 1024], BF16)
    X1 = dscr("X1", [NT, D])
    ACTF = dscr("ACTF", [NT, FF], BF16)
    X2 = dscr("X2", [NT, D])
    PQ = dscr("PQ", [NT + 3, 4096])
    GD = dscr("GD", [NT, 6176])
    MG1 = dscr("MG1", [NT, 2048], BF16)
    X3 = dscr("X3", [NT, D])

    S = Sched(nc)

    I32 = S.sb("I32", [128, 128], F32)
    Ibf = S.sb("Ibf", [128, 128], BF16)
    U32 = S.sb("U32", [64, 64], F32)
    SU32 = S.sb("SU32", [64, 64], F32)
    SL32 = S.sb("SL32", [64, 64], F32)
    ones32 = S.sb("ones32", [128, 128], F32)
    bm4 = S.sb("bm4", [4, 4, 64], F32)
    S.dma(I32, c_ident, [], [I32])
    S.dma(Ibf, c_ident, [], [Ibf], q="pool")
    S.dma(U32, c_U, [], [U32])
    S.dma(SU32, c_SU, [], [SU32])
    S.dma(SL32, c_SL, [], [SL32])
    S.dma(bm4, c_bm4, [], [bm4])
    S.memset(ones32, 1.0)
    S.barrier()

    def dense_blocks(hT, nk, W, blocks, pms, pmi):
        for (c0, ncols, epi) in blocks:
            pm = pms[pmi % len(pms)]
            pmi += 1
            for k in range(nk):
                S.mm(pm[:, 0:ncols], hT[:, k, :], W[:, k, c0:c0 + ncols], [hT, W], [pm], start=(k == 0), stop=(k == nk - 1))
            epi(pm)
        return pmi

    def phase_d1_even():
        S.push()
        W = S.sb("W", [128, 8, EVEN_COLS], BF16)
        wn = S.sb("wn", [128, D], F32)
        lb = S.sb("lb", [128, 512], F32)
        oml = S.sb("oml", [128, 512], F32)
        hnw = S.sb("hnw", [128, 512], F32)
        mnw = S.sb("mnw", [128, 512], F32)
        bif = S.sb("bif", [128, 8], F32)
        junk = S.sb("junk", [128, D], F32)
        tmp = [S.sb("tmp%d" % i, [128, 512], F32) for i in range(2)]
        t8 = S.sb("t8", [128, 8], F32)
        t8b = S.sb("t8b", [128, 4], F32)
        xt = [S.sb("xt%d" % i, [128, D], F32) for i in range(2)]
        ss = [S.sb("ss%d" % i, [128, 1], F32) for i in range(2)]
        rstd = [S.sb("rstd%d" % i, [128, 1], F32) for i in range(2)]
        h = [S.sb("h%d" % i, [128, D], BF16) for i in range(2)]
        hT = [S.sb("hT%d" % i, [128, 8, 128], BF16) for i in range(2)]
        HGt = [S.sb("HGt%d" % i, [128, 5, 512], F32) for i in range(2)]
        MLt = [S.sb("MLt%d" % i, [128, 1544], F32) for i in range(2)]
        ptb = [S.ps("ptb%d" % i, [128, 8, 128], BF16) for i in range(2)]
        pms = [S.ps("pm%d" % i, [128, 512], F32) for i in range(6)]

        load_weight(S, W, even_w_in, D)
        load_bcast(S, wn, norm_mix[0:1, :], D)
        load_bcast(S, lb, lb_logits[0:1, :], 512)
        load_bcast(S, oml, lb_logits[1:2, :], 512)
        load_bcast(S, hnw, hgrn_norm[0:1, :], 512)
        load_bcast(S, mnw, mlstm_norm[0:1, :], 512)
        load_bcast(S, bif, mlstm_bif[0:1, :], 8)
        S.tt(lb.ap(), lb.ap(), oml.ap(), ALU.subtract, [lb, oml], [lb])
        S.af(lb.ap(), lb.ap(), AF.Sigmoid, [lb], [lb])
        S.ts(oml.ap(), lb.ap(), -1.0, ALU.mult, [lb], [oml], s2=1.0, op1=ALU.add)
        S.barrier()

        pmi_box = [0]

        def stA1(i):
            b = i % 2
            S.dma(xt[b], xin[i * 128:i * 128 + 128, :], [], [xt[b]])
            rmsnorm_tile(S, xt[b], wn, h[b], junk, ss[b], rstd[b])

        def stA2(i):
            b = i % 2
            transpose_tile(S, h[b], 8, ptb[b], hT[b], Ibf)

        def stB(i):
            pmi = pmi_box[0]
            b = i % 2
            r0 = i * 128
            hg, ml = HGt[b], MLt[b]
            t0, t1 = tmp

            def e_q(pm):
                S.af(hg[:, 0, :], pm.ap(), AF.Silu, [pm], [hg])

            def e_f(pm):
                S.af(t0.ap(), pm.ap(), AF.Sigmoid, [pm], [t0])
                S.tt(t0.ap(), t0.ap(), oml.ap(), ALU.mult, [t0, oml], [t0])
                S.tt(t0.ap(), t0.ap(), lb.ap(), ALU.add, [t0, lb], [t0])
                S.af(hg[:, 1, :], t0.ap(), AF.Ln, [t0], [hg])
                S.ts(hg[:, 2, :], t0.ap(), -1.0, ALU.mult, [t0], [hg], s2=1.0, op1=ALU.add)

            def e_v(pm):
                S.cp(hg[:, 3, :], pm.ap(), [pm], [hg])

            def e_g(pm):
                S.af(t1.ap(), pm.ap(), AF.Silu, [pm], [t1])
                S.tt(hg[:, 4, :], t1.ap(), hnw.ap(), ALU.mult, [t1, hnw], [hg])

            def e_qk(pm):
                S.cp(ml[:, 0:256], pm[:, 0:256], [pm], [ml])
                S.ts(ml[:, 256:512], pm[:, 256:512], 0.125, ALU.mult, [pm], [ml])

            def e_mv(pm):
                S.cp(ml[:, 512:1024], pm.ap(), [pm], [ml])

            def e_o(pm):
                S.af(t1.ap(), pm.ap(), AF.Sigmoid, [pm], [t1])
                S.tt(ml[:, 1024:1536], t1.ap(), mnw.ap(), ALU.mult, [t1, mnw], [ml])

            def e_if(pm):
                S.tt(t8.ap(), pm[:, 0:8], bif.ap(), ALU.add, [pm, bif], [t8])
                S.af(t8.ap(), t8.ap(), AF.Tanh, [t8], [t8], scale=1.0 / 15.0)
                S.ts(ml[:, 1536:1540], t8[:, 0:4], 15.0, ALU.mult, [t8], [ml])
                S.af(t8b.ap(), t8[:, 4:8], AF.Exp, [t8], [t8b], scale=-15.0)
                S.af(t8b.ap(), t8b.ap(), AF.Ln, [t8b], [t8b], bias=1.0)
                S.ts(ml[:, 1540:1544], t8b.ap(), -1.0, ALU.mult, [t8b], [ml])

            blocks = [(0, 512, e_q), (1536, 512, e_g), (512, 512, e_f), (1024, 512, e_v),
                      (2048, 512, e_qk), (2560, 512, e_mv), (3072, 512, e_o), (3584, 8, e_if)]
            pmi_box[0] = dense_blocks(hT[b], 8, W, blocks, pms, pmi)
            S.dma(HG[r0:r0 + 128, :], hg.ap().rearrange("p a b -> p (a b)"), [hg], [], q="pool")
            S.dma(ML[r0:r0 + 128, :], ml, [ml], [], q="pool")

        run_pipelined(ntile, stA1, stB, stA2)
        S.pop()

    def phase_r0():
        S.push()
        Sh = S.sb("Sh", [128, 4, 128], F32)
        Shb = S.sb("Shb", [128, 4, 128], BF16)
        Cn = S.sb("Cn", [64, 4, 129], F32)
        Cnb = S.sb("Cnb", [64, 4, 129], BF16)
        m_row = S.sb("m_row", [4, 1], F32)
        m_bc = S.sb("m_bc", [64, 4], F32)
        hgc = [S.sb("hgc%d" % i, [64, 5, 512], F32) for i in range(2)]
        mlc = [S.sb("mlc%d" % i, [64, 1544], F32) for i in range(2)]
        mg = [S.sb("mg%d" % i, [64, 1024], BF16) for i in range(2)]
        eb = S.sb("eb", [64, 512], F32)
        enb = S.sb("enb", [64, 512], F32)
        qt = S.sb("qt", [64, 512], BF16)
        kt = S.sb("kt", [64, 512], BF16)
        qkT = S.sb("qkT", [128, 8, 64], BF16)
        ebl = S.sb("ebl", [128, 4, 2], F32)
        scm = S.sb("scm", [64, 4, 64], BF16)
        vb = S.sb("vb", [64, 512], BF16)
        stmp = S.sb("stmp", [128, 4, 128], F32)
        osq = S.sb("osq", [64, 512], F32)
        oss = S.sb("oss", [64, 4], F32)
        orstd = S.sb("orstd", [64, 4], F32)
        on = S.sb("on", [64, 4, 128], F32)
        a_t = S.sb("a_t", [64, 4], F32)
        aT = S.sb("aT", [4, 64], F32)
        M_row = S.sb("M_row", [4, 64], F32)
        Mblk = S.sb("Mblk", [4, 4, 64], F32)
        Mlb = S.sb("Mlb", [4, 4], F32)
        M_t = S.sb("M_t", [64, 4], F32)
        F_t = S.sb("F_t", [64, 4], F32)
        DT = S.sb("DT", [64, 4, 64], F32)
        Wbc = S.sb("Wbc", [64, 4, 64], F32)
        mq = S.sb("mq", [64, 512], BF16)
        mqkT = S.sb("mqkT", [64, 8, 64], BF16)
        qTt = S.sb("qTt", [64, 4, 64], BF16)
        qkD = S.sb("qkD", [64, 4, 64], BF16)
        vext = S.sb("vext", [64, 4, 129], BF16)
        den = S.sb("den", [64, 4], F32)
        nden = S.sb("nden", [64, 4], F32)
        emt = S.sb("emt", [64, 4], F32)
        rden = S.sb("rden", [64, 4], F32)
        hn = S.sb("hn", [64, 4, 128], F32)
        w_s = S.sb("w_s", [64, 4], F32)
        khat = S.sb("khat", [64, 4, 64], BF16)
        dec = S.sb("dec", [64, 4], F32)
        Flb = S.sb("Flb", [64, 4], F32)
        Mlbc = S.sb("Mlbc", [64, 4], F32)
        fl_row = S.sb("fl_row", [4, 2], F32)
        ones_c2 = S.sb("ones_c2", [64, 2], F32)
        pA = S.ps("pA", [128, 512], F32)
        pB = S.ps("pB", [128, 8, 64], BF16)
        pC = S.ps("pC", [128, 512], F32)
        pD = S.ps("pD", [128, 512], F32)
        pE = S.ps("pE", [128, 512], F32)
        pF = S.ps("pF", [64, 4, 256], F32)
        pB2 = S.ps("pB2", [128, 8, 64], BF16)
        osq2 = S.sb("osq2", [64, 512], F32)
        oss2 = S.sb("oss2", [64, 4], F32)
        orstd2 = S.sb("orstd2", [64, 4], F32)
        S.memset(ones_c2, 1.0)

        def interleave(gens):
            gens = [g for g in gens if g is not None]
            while gens:
                for g in list(gens):
                    try:
                        next(g)
                    except StopIteration:
                        gens.remove(g)

        for si, sq in enumerate(seqs):
            if sq["init"] is None:
                S.memset(Sh, 0.0)
                S.memset(Cn, 0.0)
                S.memset(m_row, 0.0)
                S.memset(m_bc, 0.0)
            else:
                j = sq["init"]
                S.dma(Sh, st_hS[j].rearrange("h k v -> k h v"), [], [Sh])
                S.dma(Cn[:, :, 0:128], st_mC[j].rearrange("h k v -> k h v"), [], [Cn])
                S.dma(Cn[:, :, 128:129], st_mn[j].rearrange("h (k o) -> k h o", o=1), [], [Cn], allow_slow_non_contiguous=True)
                S.dma(m_row, st_mm[j].rearrange("(h o) -> h o", o=1), [], [m_row])
                S.dma(m_bc, st_mm[j:j + 1, :].broadcast_to([64, 4]), [], [m_bc])
            S.cp(Shb.ap(), Sh.ap(), [Sh], [Shb], e="act")
            S.cp(Cnb.ap(), Cn.ap(), [Cn], [Cnb], e="act")
            row = sq["start"]
            for ci, c in enumerate(sq["chunks"]):
                b = ci % 2
                hg, ml, mgo = hgc[b], mlc[b], mg[b]
                S.dma(hg[0:c], HG[row:row + c, :].rearrange("p (a b) -> p a b", a=5), [], [hg])
                S.dma(ml[0:c], ML[row:row + c, :], [], [ml])
                def hgrn_chunk(c=c, hg=hg, mgo=mgo):
                    S.mm(pA[0:c, :], U32[0:c, 0:c], hg[0:c, 1, :], [U32, hg], [pA])
                    S.af(eb[0:c], pA[0:c, :], AF.Exp, [pA], [eb])
                    S.af(enb[0:c], pA[0:c, :], AF.Exp, [pA], [enb], scale=-1.0)
                    yield
                    S.tt(qt[0:c], hg[0:c, 0, :], eb[0:c], ALU.mult, [hg, eb], [qt])
                    S.tt(kt[0:c], hg[0:c, 2, :], enb[0:c], ALU.mult, [hg, enb], [kt])
                    S.cp(vb[0:c], hg[0:c, 3, :], [hg], [vb], e="act")
                    yield
                    for hh in range(4):
                        S.tr(pB[:, hh, 0:c], qt[0:c, hh * 128:(hh + 1) * 128], Ibf[0:c, 0:c], [qt, Ibf], [pB])
                        S.tr(pB[:, 4 + hh, 0:c], kt[0:c, hh * 128:(hh + 1) * 128], Ibf[0:c, 0:c], [kt, Ibf], [pB])
                    S.cp(qkT[:, :, 0:c], pB[:, :, 0:c], [pB], [qkT])
                    yield
                    for hh in range(4):
                        S.mm(pC[:, 2 * hh:2 * hh + 2], eb[0:c, hh * 128:(hh + 1) * 128], I32[0:c, c - 2:c], [eb, I32], [pC])
                    S.cp(ebl.ap().rearrange("p h t -> p (h t)"), pC[:, 0:8], [pC], [ebl], e="act")
                    for hh in range(4):
                        S.mm(pC[0:c, 64 + hh * 64:64 + hh * 64 + c], qkT[:, 4 + hh, 0:c], qkT[:, hh, 0:c], [qkT], [pC])
                    S.tt(scm[0:c, :, 0:c], pC[0:c, 64:320].rearrange("p (h t) -> p h t", h=4)[:, :, 0:c],
                         bc(U32[0:c, 0:c], 1, [c, 4, c]), ALU.mult, [pC, U32], [scm])
                    yield
                    for hh in range(4):
                        S.mm(pA[0:c, hh * 128:(hh + 1) * 128], scm[0:c, hh, 0:c], vb[0:c, hh * 128:(hh + 1) * 128], [scm, vb], [pA], start=True, stop=False)
                        S.mm(pA[0:c, hh * 128:(hh + 1) * 128], qkT[:, hh, 0:c], Shb[:, hh, :], [qkT, Shb], [pA], start=False, stop=True)
                    for hh in range(4):
                        S.mm(pD[:, hh * 128:(hh + 1) * 128], kt[0:c, hh * 128:(hh + 1) * 128], vb[0:c, hh * 128:(hh + 1) * 128], [kt, vb], [pD])
                    yield
                    S.tt(stmp.ap().rearrange("p h v -> p (h v)"), Sh.ap().rearrange("p h v -> p (h v)"), pD.ap(), ALU.add, [Sh, pD], [stmp])
                    S.tt(Sh.ap(), stmp.ap(), bc(ebl[:, :, 1], 2, [128, 4, 128]), ALU.mult, [stmp, ebl], [Sh])
                    S.cp(Shb.ap(), Sh.ap(), [Sh], [Shb], e="act")
                    yield
                    S.af(osq[0:c], pA[0:c, :], AF.Square, [pA], [osq])
                    S.red(oss[0:c], osq[0:c].rearrange("p (h v) -> p h v", h=4), ALU.add, [osq], [oss])
                    S.af(oss[0:c], oss[0:c], AF.Sqrt, [oss], [oss], scale=1.0 / 128.0, bias=EPS)
                    S.recip(orstd[0:c], oss[0:c], [oss], [orstd])
                    S.tt(on[0:c], pA[0:c, :].rearrange("p (h v) -> p h v", h=4), bc(orstd[0:c], 2, [c, 4, 128]), ALU.mult, [pA, orstd], [on])
                    S.tt(mgo[0:c, 0:512], on[0:c].rearrange("p h v -> p (h v)"), hg[0:c, 4, :], ALU.mult, [on, hg], [mgo])

                def mlstm_chunk(c=c, ml=ml, mgo=mgo):
                    S.mm(pE[0:c, 0:4], U32[0:c, 0:c], ml[0:c, 1540:1544], [U32, ml], [pE])
                    S.mm(pE[0:64, 8:12], ones32[0:c, 0:64], ml[0:c, 1540:1544], [ones32, ml], [pE])
                    S.mm(pE[0:4, 16:18], ml[0:c, 1540:1544], ones_c2[0:c, :], [ml, ones_c2], [pE])
                    S.cp(F_t[0:c], pE[0:c, 0:4], [pE], [F_t])
                    S.cp(Flb.ap(), pE[0:64, 8:12], [pE], [Flb])
                    S.cp(fl_row.ap(), pE[0:4, 16:18], [pE], [fl_row])
                    yield
                    S.tt(a_t[0:c], ml[0:c, 1536:1540], F_t[0:c], ALU.subtract, [ml, F_t], [a_t])
                    S.tr(pE[0:4, 32:32 + c], a_t[0:c, :], I32[0:c, 0:c], [a_t, I32], [pE])
                    S.cp(aT[:, 0:c], pE[0:4, 32:32 + c], [pE], [aT])
                    S.op("dve", lambda: nc.vector.tensor_tensor_scan(out=M_row[:, 0:c], data0=aT[:, 0:c], data1=aT[:, 0:c],
                                                                     initial=m_row.ap(), op0=ALU.max, op1=ALU.max),
                         [aT, m_row], [M_row])
                    yield
                    S.tt(Mblk[:, :, 0:c], bc(M_row[:, 0:c], 1, [4, 4, c]), bm4[:, :, 0:c], ALU.mult, [M_row, bm4], [Mblk])
                    S.tt(Mlb.ap(), M_row[:, c - 1:c].broadcast_to([4, 4]), bm4[:, :, 0], ALU.mult, [M_row, bm4], [Mlb])
                    S.mm(pE[0:64, 128:128 + 4 * c], ones32[0:4, 0:64], Mblk[:, :, 0:c], [ones32, Mblk], [pE])
                    S.mm(pE[0:64, 400:404], ones32[0:4, 0:64], Mlb.ap(), [ones32, Mlb], [pE])
                    S.tr(pE[0:c, 416:420], M_row[:, 0:c], I32[0:4, 0:4], [M_row, I32], [pE])
                    S.cp(M_t[0:c], pE[0:c, 416:420], [pE], [M_t])
                    S.cp(Mlbc.ap(), pE[0:64, 400:404], [pE], [Mlbc])
                    yield
                    Mb3 = pE[0:64, 128:128 + 4 * c].rearrange("p (h t) -> p h t", h=4)
                    for hh in range(4):
                        S.af(DT[0:c, hh, 0:c], Mb3[0:c, hh, 0:c], AF.Exp, [pE, a_t], [DT], scale=-1.0, bias=a_t[0:c, hh:hh + 1])
                        S.af(Wbc[:, hh, 0:c], Mb3[:, hh, 0:c], AF.Exp, [pE, m_bc], [Wbc], scale=-1.0, bias=m_bc[:, hh:hh + 1])
                    S.tt(DT[0:c, :, 0:c], DT[0:c, :, 0:c], bc(U32[0:c, 0:c], 1, [c, 4, c]), ALU.mult, [DT, U32], [DT])
                    yield
                    S.cp(mq[0:c], ml[0:c, 0:512], [ml], [mq], e="act")
                    for hh in range(4):
                        S.tr(pB2[0:64, hh, 0:c], mq[0:c, hh * 64:(hh + 1) * 64], Ibf[0:c, 0:c], [mq, Ibf], [pB2])
                        S.tr(pB2[0:64, 4 + hh, 0:c], mq[0:c, 256 + hh * 64:256 + (hh + 1) * 64], Ibf[0:c, 0:c], [mq, Ibf], [pB2])
                    S.cp(mqkT[:, :, 0:c], pB2[0:64, :, 0:c], [pB2], [mqkT])
                    yield
                    S.tt(qTt[:, :, 0:c], mqkT[:, 0:4, 0:c], Wbc[:, :, 0:c], ALU.mult, [mqkT, Wbc], [qTt])
                    for hh in range(4):
                        S.mm(pF[0:c, hh, 192:192 + c], mqkT[:, 4 + hh, 0:c], mqkT[:, hh, 0:c], [mqkT], [pF])
                    S.tt(qkD[0:c, :, 0:c], pF[0:c, :, 192:192 + c], DT[0:c, :, 0:c], ALU.mult, [pF, DT], [qkD])
                    S.cp(vext[0:c, :, 0:128], ml[0:c, 512:1024].rearrange("p (h v) -> p h v", h=4), [ml], [vext], e="act")
                    S.cp(vext[0:c, :, 128:129], bc(ones32[0:c, 0:4], 2, [c, 4, 1]), [ones32], [vext])
                    yield
                    for hh in range(4):
                        S.mm(pF[0:c, hh, 0:129], qkD[0:c, hh, 0:c], vext[0:c, hh, :], [qkD, vext], [pF], start=True, stop=False)
                        S.mm(pF[0:c, hh, 0:129], qTt[:, hh, 0:c], Cnb[:, hh, :], [qTt, Cnb], [pF], start=False, stop=True)
                    yield
                    S.cp(den[0:c], pF[0:c, :, 128], [pF], [den])
                    S.ts(nden[0:c], den[0:c], -1.0, ALU.mult, [den], [nden])
                    S.tt(den[0:c], den[0:c], nden[0:c], ALU.max, [den, nden], [den])
                    S.tt(emt[0:c], F_t[0:c], M_t[0:c], ALU.add, [F_t, M_t], [emt])
                    S.af(emt[0:c], emt[0:c], AF.Exp, [emt], [emt], scale=-1.0)
                    S.tt(den[0:c], den[0:c], emt[0:c], ALU.max, [den, emt], [den])
                    S.recip(rden[0:c], den[0:c], [den], [rden])
                    S.tt(hn[0:c], pF[0:c, :, 0:128], bc(rden[0:c], 2, [c, 4, 128]), ALU.mult, [pF, rden], [hn])
                    yield
                    S.af(osq2[0:c], hn[0:c].rearrange("p h v -> p (h v)"), AF.Square, [hn], [osq2])
                    S.red(oss2[0:c], osq2[0:c].rearrange("p (h v) -> p h v", h=4), ALU.add, [osq2], [oss2])
                    S.af(oss2[0:c], oss2[0:c], AF.Sqrt, [oss2], [oss2], scale=1.0 / 128.0, bias=EPS)
                    S.recip(orstd2[0:c], oss2[0:c], [oss2], [orstd2])
                    S.tt(hn[0:c], hn[0:c], bc(orstd2[0:c], 2, [c, 4, 128]), ALU.mult, [hn, orstd2], [hn])
                    S.tt(mgo[0:c, 512:1024], hn[0:c].rearrange("p h v -> p (h v)"), ml[0:c, 1024:1536], ALU.mult, [hn, ml], [mgo])
                    yield
                    S.tt(w_s[0:c], a_t[0:c], Mlbc[0:c], ALU.subtract, [a_t, Mlbc], [w_s])
                    S.af(w_s[0:c], w_s[0:c], AF.Exp, [w_s], [w_s])
                    S.tt(dec.ap(), m_bc.ap(), Mlbc.ap(), ALU.subtract, [m_bc, Mlbc], [dec])
                    S.af(dec.ap(), dec.ap(), AF.Exp, [dec], [dec])
                    S.tt(khat[0:c], ml[0:c, 256:512].rearrange("p (h d) -> p h d", h=4), bc(w_s[0:c], 2, [c, 4, 64]), ALU.mult, [ml, w_s], [khat])
                    for hh in range(4):
                        S.mm(pF[:, hh, 0:129], khat[0:c, hh, :], vext[0:c, hh, :], [khat, vext], [pF])
                    S.tt(Cn.ap(), Cn.ap(), bc(dec.ap(), 2, [64, 4, 129]), ALU.mult, [Cn, dec], [Cn])
                    S.tt(Cn.ap(), Cn.ap(), pF[:, :, 0:129], ALU.add, [Cn, pF], [Cn])
                    S.cp(Cnb.ap(), Cn.ap(), [Cn], [Cnb], e="act")
                    S.tt(m_bc.ap(), Mlbc.ap(), Flb.ap(), ALU.add, [Mlbc, Flb], [m_bc])
                    S.tt(m_row.ap(), M_row[:, c - 1:c], fl_row[:, 0:1], ALU.add, [M_row, fl_row], [m_row])

                interleave([hgrn_chunk(), mlstm_chunk()])
                S.dma(MG0[row:row + c, :], mgo[0:c], [mgo], [], q="pool")
                row += c
            S.dma(o_hS[si].rearrange("h k v -> k h v"), Sh, [Sh], [], q="pool")
            S.dma(o_mC[si].rearrange("h k v -> k h v"), Cn[:, :, 0:128], [Cn], [], q="pool")
            S.dma(o_mn[si].rearrange("h (k o) -> k h o", o=1), Cn[:, :, 128:129], [Cn], [], q="pool", allow_slow_non_contiguous=True)
            S.dma(o_mm[si].rearrange("(h o) -> h o", o=1), m_row, [m_row], [], q="pool")
        S.pop()

    def phase_d3ab(layer, MG, KM, w_out_d, xsrc):
        S.push()
        nkm = KM // 128
        Wo = S.sb("Wo", [128, nkm, D], BF16)
        W1 = S.sb("W1", [128, 8, 2 * FF], BF16)
        wn = S.sb("wn", [128, D], F32)
        junk = S.sb("junk", [128, D], F32)
        mgt = [S.sb("mgt%d" % i, [128, KM], BF16) for i in range(2)]
        mT = [S.sb("mT%d" % i, [128, nkm, 128], BF16) for i in range(2)]
        xt = [S.sb("xt%d" % i, [128, D], F32) for i in range(2)]
        x1 = [S.sb("x1%d" % i, [128, D], F32) for i in range(2)]
        ss = [S.sb("ss%d" % i, [128, 1], F32) for i in range(2)]
        rstd = [S.sb("rstd%d" % i, [128, 1], F32) for i in range(2)]
        h = [S.sb("h%d" % i, [128, D], BF16) for i in range(2)]
        hT = [S.sb("hT%d" % i, [128, 8, 128], BF16) for i in range(2)]
        sg = [S.sb("sg%d" % i, [128, 512], F32) for i in range(2)]
        act = [S.sb("act%d" % i, [128, FF], BF16) for i in range(2)]
        ptb = [S.ps("ptb%d" % i, [128, 8, 128], BF16) for i in range(2)]
        pms = [S.ps("pm%d" % i, [128, 512], F32) for i in range(6)]
        load_weight(S, Wo, w_out_d, KM)
        load_weight(S, W1, ffn_w_in[layer], D)
        load_bcast(S, wn, norm_ffn[layer:layer + 1, :], D)
        S.barrier()
        box = [0, 0]

        def stA1(i):
            pmi = box[0]
            b = i % 2
            r0 = i * 128
            S.dma(mgt[b], MG[r0:r0 + 128, :], [], [mgt[b]])
            S.dma(xt[b], xsrc[r0:r0 + 128, :], [], [xt[b]])
            transpose_tile(S, mgt[b], nkm, ptb[b], mT[b], Ibf)
            x1b, xb = x1[b], xt[b]
            blocks = []
            for cb in range(2):
                def epi(pm, cb=cb):
                    S.tt(x1b[:, cb * 512:(cb + 1) * 512], xb[:, cb * 512:(cb + 1) * 512], pm.ap(), ALU.add, [xb, pm], [x1b])
                blocks.append((cb * 512, 512, epi))
            box[0] = dense_blocks(mT[b], nkm, Wo, blocks, pms, pmi)
            S.dma(X1[r0:r0 + 128, :], x1b, [x1b], [], q="pool")
            rmsnorm_tile(S, x1b, wn, h[b], junk, ss[b], rstd[b])

        def stA2(i):
            b = i % 2
            transpose_tile(S, h[b], 8, ptb[b], hT[b], Ibf)

        def stB(i):
            pmi, sgi = box
            b = i % 2
            r0 = i * 128
            ab = act[b]
            for o in range(0, FF, 512):
                n = min(512, FF - o)
                pg = pms[pmi % 6]
                pu = pms[(pmi + 1) % 6]
                pmi += 2
                for k in range(8):
                    S.mm(pg[:, 0:n], hT[b][:, k, :], W1[:, k, o:o + n], [hT[b], W1], [pg], start=(k == 0), stop=(k == 7))
                for k in range(8):
                    S.mm(pu[:, 0:n], hT[b][:, k, :], W1[:, k, FF + o:FF + o + n], [hT[b], W1], [pu], start=(k == 0), stop=(k == 7))
                s_ = sg[sgi % 2]
                sgi += 1
                S.af(s_[:, 0:n], pg[:, 0:n], AF.Silu, [pg], [s_])
                S.tt(ab[:, o:o + n], s_[:, 0:n], pu[:, 0:n], ALU.mult, [s_, pu], [ab])
            box[0], box[1] = pmi, sgi
            S.dma(ACTF[r0:r0 + 128, :], ab, [ab], [], q="pool")

        run_pipelined(ntile, stA1, stB, stA2)
        S.pop()

    def phase_d3c(layer, xdst, final):
        S.push()
        nk = FF // 128
        W2 = S.sb("W2", [128, nk, D], BF16)
        wn = S.sb("wn", [128, D], F32)
        junk = S.sb("junk", [128, D], F32)
        at = [S.sb("at%d" % i, [128, FF], BF16) for i in range(2)]
        aT = [S.sb("aT%d" % i, [128, nk, 128], BF16) for i in range(2)]
        xt = [S.sb("xt%d" % i, [128, D], F32) for i in range(2)]
        x2 = [S.sb("x2%d" % i, [128, D], F32) for i in range(2)]
        yt = [S.sb("yt%d" % i, [128, D], F32) for i in range(2)]
        ss = [S.sb("ss%d" % i, [128, 1], F32) for i in range(2)]
        rstd = [S.sb("rstd%d" % i, [128, 1], F32) for i in range(2)]
        ptb = [S.ps("ptb%d" % i, [128, 8, 128], BF16) for i in range(2)]
        pms = [S.ps("pm%d" % i, [128, 512], F32) for i in range(4)]
        load_weight(S, W2, ffn_w_out[layer], FF)
        if final:
            load_bcast(S, wn, norm_final[0:1, :], D)
        S.barrier()
        box = [0]

        def stA1(i):
            b = i % 2
            r0 = i * 128
            S.dma(at[b], ACTF[r0:r0 + 128, :], [], [at[b]])
            S.dma(xt[b], X1[r0:r0 + 128, :], [], [xt[b]])

        def stA2(i):
            b = i % 2
            transpose_tile(S, at[b], nk, ptb[b], aT[b], Ibf)

        def stB(i):
            pmi = box[0]
            b = i % 2
            r0 = i * 128
            x2b, xb = x2[b], xt[b]
            blocks = []
            for cb in range(2):
                def epi(pm, cb=cb):
                    S.tt(x2b[:, cb * 512:(cb + 1) * 512], xb[:, cb * 512:(cb + 1) * 512], pm.ap(), ALU.add, [xb, pm], [x2b])
                blocks.append((cb * 512, 512, epi))
            box[0] = dense_blocks(aT[b], nk, W2, blocks, pms, pmi)
            if not final:
                S.dma(xdst[r0:r0 + 128, :], x2b, [x2b], [], q="pool")
            else:
                S.af(junk.ap(), x2b.ap(), AF.Square, [x2b], [junk, ss[b]], accum=ss[b].ap())
                S.af(ss[b].ap(), ss[b].ap(), AF.Sqrt, [ss[b]], [ss[b]], scale=1.0 / D, bias=EPS)
                S.recip(rstd[b].ap(), ss[b].ap(), [ss[b]], [rstd[b]])
                S.stt(yt[b].ap(), x2b.ap(), rstd[b].ap(), wn.ap(), ALU.mult, ALU.mult, [x2b, rstd[b], wn], [yt[b]])
                S.dma(xdst[r0:r0 + 128, :], yt[b], [yt[b]], [], q="pool")

        run_pipelined(ntile, stA1, stB, stA2)
        S.pop()

    def phase_d1_odd():
        S.push()
        W = S.sb("W", [128, 8, ODD_COLS], BF16)
        wn = S.sb("wn", [128, D], F32)
        gnw = S.sb("gnw", [128, 128], F32)
        negA = S.sb("negA", [128, 16], F32)
        dtb = S.sb("dtb", [128, 16], F32)
        junk = S.sb("junk", [128, D], F32)
        tmp = [S.sb("tmp%d" % i, [128, 512], F32) for i in range(2)]
        t16 = S.sb("t16", [128, 16], F32)
        xt = [S.sb("xt%d" % i, [128, D], F32) for i in range(2)]
        ss = [S.sb("ss%d" % i, [128, 1], F32) for i in range(2)]
        rstd = [S.sb("rstd%d" % i, [128, 1], F32) for i in range(2)]
        h = [S.sb("h%d" % i, [128, D], BF16) for i in range(2)]
        hT = [S.sb("hT%d" % i, [128, 8, 128], BF16) for i in range(2)]
        pq = [S.sb("pq%d" % i, [128, 4096], F32) for i in range(2)]
        gzt = [S.sb("gzt%d" % i, [128, 2, 1040], F32) for i in range(2)]
        ptb = [S.ps("ptb%d" % i, [128, 8, 128], BF16) for i in range(2)]
        pms = [S.ps("pm%d" % i, [128, 512], F32) for i in range(6)]
        load_weight(S, W, odd_w_in, D)
        load_bcast(S, wn, norm_mix[1:2, :], D)
        load_bcast(S, gnw, gdn_norm[0:1, :], 128)
        load_bcast(S, negA, gdn_a_log[0:1, :], 16)
        load_bcast(S, dtb, gdn_dt_bias[0:1, :], 16)
        S.af(negA.ap(), negA.ap(), AF.Exp, [negA], [negA])
        S.ts(negA.ap(), negA.ap(), -1.0, ALU.mult, [negA], [negA])
        S.barrier()
        box = [0]

        def stA1(i):
            b = i % 2
            S.dma(xt[b], X2[i * 128:i * 128 + 128, :], [], [xt[b]])
            rmsnorm_tile(S, xt[b], wn, h[b], junk, ss[b], rstd[b])

        def stA2(i):
            b = i % 2
            transpose_tile(S, h[b], 8, ptb[b], hT[b], Ibf)

        def stB(i):
            pmi = box[0]
            b = i % 2
            r0 = i * 128
            pqb, gz = pq[b], gzt[b]
            blocks = []
            for cb in range(8):
                def epi(pm, cb=cb):
                    S.cp(pqb[:, cb * 512:(cb + 1) * 512], pm.ap(), [pm], [pqb], e=("act" if cb % 2 else "dve"))
                blocks.append((cb * 512, 512, epi))
            for cb in range(4):
                def epi(pm, cb=cb):
                    t_ = tmp[cb % 2]
                    S.af(t_.ap(), pm.ap(), AF.Silu, [pm], [t_])
                    S.tt(gz[:, cb // 2, (cb % 2) * 512:(cb % 2) * 512 + 512].rearrange("p (h v) -> p h v", h=4), t_.ap().rearrange("p (h v) -> p h v", h=4),
                         bc(gnw.ap(), 1, [128, 4, 128]), ALU.mult, [t_, gnw], [gz])
                blocks.append((4096 + cb * 512, 512, epi))

            def epi_ba(pm):
                S.af(gz[:, :, 1024:1032], pm[:, 0:16].rearrange("p (g h) -> p g h", g=2), AF.Sigmoid, [pm], [gz])
                S.tt(t16.ap(), pm[:, 16:32], dtb.ap(), ALU.add, [pm, dtb], [t16])
                S.af(t16.ap(), t16.ap(), AF.Exp, [t16], [t16])
                S.af(t16.ap(), t16.ap(), AF.Ln, [t16], [t16], bias=1.0)
                S.tt(gz[:, :, 1032:1040], t16.ap().rearrange("p (g h) -> p g h", g=2), negA.ap().rearrange("p (g h) -> p g h", g=2), ALU.mult, [t16, negA], [gz])
            blocks.append((6144, 32, epi_ba))
            box[0] = dense_blocks(hT[b], 8, W, blocks, pms, pmi)
            S.dma(PQ[3 + r0:3 + r0 + 128, :], pqb, [pqb], [], q="pool")
            for g in range(2):
                S.dma(GD[r0:r0 + 128, g * 3088 + 2048:g * 3088 + 3088], gz[:, g, :], [gz], [], q="pool")

        run_pipelined(ntile, stA1, stB, stA2)
        S.pop()
        S.push()
        zr = S.sb("zr", [3, 4096], F32)
        cs = [S.sb("cs%d" % i, [3, 4096], F32) for i in range(NSEQ)]
        S.memset(zr, 0.0)
        for si, sq in enumerate(seqs):
            st = sq["start"]
            if sq["init"] is None:
                S.dma(PQ[3 + st - 3:3 + st, :], zr, [zr], [])
            else:
                S.dma(cs[si], st_gc[sq["init"]], [], [cs[si]])
                S.dma(PQ[3 + st - 3:3 + st, :], cs[si], [cs[si]], [])
        for si, sq in enumerate(seqs):
            e = sq["start"] + sq["len"]
            t_ = S.sb("co%d" % si, [3, 4096], F32)
            S.dma(t_, PQ[3 + e - 3:3 + e, :], [], [t_])
            S.dma(o_gc[si], t_, [t_], [], q="pool")
        S.pop()

    def phase_c1():
        S.push()
        cw = [S.sb("cw%d" % j, [128, 4096], F32) for j in range(4)]
        pieces = [(0, 1024, "dve", 8, True), (1024, 1024, "dve", 8, False), (2048, 512, "dve", 0, False), (2560, 1536, "pool", 0, False)]
        bufA = [[S.sb("xa%d_%d" % (j, i), [128, 1024], F32) for j in range(4)] for i in range(3)]
        bufB = [[S.sb("xb%d_%d" % (j, i), [128, 512], F32) for j in range(4)] for i in range(2)]
        bufC = [[S.sb("xc%d_%d" % (j, i), [128, 1536], F32) for j in range(4)] for i in range(2)]
        t2 = S.sb("t2", [128, 1024], F32)
        ssq = S.sb("ssq", [128, 8], F32)
        rs = S.sb("rs", [128, 8], F32)
        for j in range(4):
            load_bcast(S, cw[j], odd_conv_w[j:j + 1, :], 4096)
        S.barrier()
        ia = 0
        for i in range(ntile):
            r0 = i * 128
            for pi, (c0, ncol, e, nl2, qs) in enumerate(pieces):
                if pi < 2:
                    xw = bufA[ia % 3]
                    ia += 1
                elif pi == 2:
                    xw = bufB[i % 2]
                else:
                    xw = bufC[i % 2]
                sl = slice(c0, c0 + ncol)
                S.dma(xw[3], PQ[r0 + 3:r0 + 131, sl], [], [xw[3]])
                for j in range(3):
                    d = 3 - j
                    S.dma(xw[j][0:d, :], PQ[r0 + j:r0 + 3, sl], [], [xw[j]])
                    S.dma(xw[j][d:d + 112, :], xw[3][0:112, :], [xw[3]], [xw[j]])
                    S.dma(xw[j][d + 112:128, :], xw[3][112:128 - d, :], [xw[3]], [xw[j]])
                a = xw[0]
                S.tt(a.ap(), a.ap(), cw[0][:, sl], ALU.mult, [a, cw[0]], [a], e=e)
                for j in range(1, 4):
                    S.tt(xw[j].ap(), xw[j].ap(), cw[j][:, sl], ALU.mult, [xw[j], cw[j]], [xw[j]], e=e)
                    S.tt(a.ap(), a.ap(), xw[j].ap(), ALU.add, [a, xw[j]], [a], e=e)
                S.af(a.ap(), a.ap(), AF.Silu, [a], [a])
                if nl2:
                    S.af(t2.ap(), a.ap(), AF.Square, [a], [t2])
                    S.red(ssq.ap(), t2.ap().rearrange("p (h d) -> p h d", h=8), ALU.add, [t2], [ssq])
                    S.af(ssq.ap(), ssq.ap(), AF.Sqrt, [ssq], [ssq], bias=EPS)
                    S.recip(rs.ap(), ssq.ap(), [ssq], [rs])
                    if qs:
                        S.ts(rs.ap(), rs.ap(), 128.0 ** -0.5, ALU.mult, [rs], [rs])
                    S.tt(a.ap().rearrange("p (h d) -> p h d", h=8), a.ap().rearrange("p (h d) -> p h d", h=8),
                         bc(rs.ap(), 2, [128, 8, 128]), ALU.mult, [a, rs], [a])
                if pi < 2:
                    off = 0 if pi == 0 else 512
                    for g in range(2):
                        S.dma(GD[r0:r0 + 128, g * 3088 + off:g * 3088 + off + 512], a[:, 512 * g:512 * g + 512], [a], [], q="pool")
                elif pi == 2:
                    S.dma(GD[r0:r0 + 128, 1024:1536], a, [a], [], q="pool")
                else:
                    S.dma(GD[r0:r0 + 128, 1536:2048], a[:, 0:512], [a], [], q="pool")
                    S.dma(GD[r0:r0 + 128, 3088 + 1024:3088 + 2048], a[:, 512:1536], [a], [], q="pool")
        S.pop()

    def phase_r1():
        import os as _os
        S.push()
        Ublk = S.sb("Ublk", [128, 128], F32)
        SLblk = S.sb("SLblk", [128, 128], F32)
        rep = S.sb("rep", [128, 3, 64], F32)
        Ibf2 = S.sb("Ibf2", [128, 64], BF16)
        Sg = S.sb("Sg", [128, 16, 128], F32)
        Sgb = [S.sb("Sgb%d" % i, [128, 16, 128], BF16) for i in range(2)]
        gd = [S.sb("gd%d" % i, [128, 3088], F32) for i in range(3)]
        mg = [S.sb("mg%d" % i, [128, 1024], BF16) for i in range(2)]
        eGG = S.sb("eGG", [128, 16], F32)
        gg2 = S.sb("gg2", [128, 16], F32)
        eGlast = [S.sb("eGlast%d" % i, [128, 16], F32) for i in range(2)]
        gU = S.sb("gU", [128, 8, 64], F32)
        E = S.sb("E", [128, 8, 64], F32)
        qkb = S.sb("qkb", [128, 1024], BF16)
        qkT = S.sb("qkT", [128, 8, 2, 64], BF16)
        T1 = S.sb("T1", [128, 8, 64], F32)
        X = [S.sb("X%d" % i, [128, 8, 64], BF16) for i in range(2)]
        Y = [S.sb("Y%d" % i, [128, 8, 64], BF16) for i in range(2)]
        Q = [S.sb("Q%d" % i, [128, 8, 64], BF16) for i in range(2)]
        TT = [S.sb("TT%d" % i, [128, 8, 64], BF16) for i in range(2)]
        attnT = [S.sb("attnT%d" % i, [128, 8, 64], BF16) for i in range(2)]
        nbeta = S.sb("nbeta", [128, 8], F32)
        IeG2 = S.sb("IeG2", [128, 16, 64], F32)
        kTt = [S.sb("kTt%d" % i, [128, 16, 64], BF16) for i in range(2)]
        qTt = [S.sb("qTt%d" % i, [128, 16, 64], BF16) for i in range(2)]
        khat = [S.sb("khat%d" % i, [128, 8, 128], BF16) for i in range(2)]
        yv = S.sb("yv", [128, 8, 128], BF16)
        vnew = S.sb("vnew", [128, 8, 128], BF16)
        osq = S.sb("osq", [128, 8, 128], F32)
        oss = S.sb("oss", [128, 8], F32)
        orstd = S.sb("orstd", [128, 8], F32)
        on = S.sb("on", [128, 8, 128], F32)
        pB = [S.ps("pB%d" % i, [128, 8, 128], F32) for i in range(2)]
        pA = [S.ps("pAa%d" % i, [128, 512], F32) for i in range(4)]
        S.dma(Ublk, c_Ublk, [], [Ublk])
        S.dma(SLblk, c_SLblk, [], [SLblk])
        S.dma(rep, c_rep, [], [rep])
        S.cp(Ibf2.ap(), rep[:, 2, :], [rep], [Ibf2])
        for t_ in gd + [gg2, eGG, gU, E, T1, nbeta, qkb, yv, vnew, osq, oss, orstd, on] + X + Y + Q + TT + attnT + khat + mg + eGlast:
            S.memset(t_, 0.0)
        for t_ in [IeG2] + kTt + qTt + [qkT]:
            S.memset(t_, 0.0, e="pool")
        for t_ in pB + pA:
            S.memset(t_, 0.0)
        U2, SU2, I2 = rep[:, 0, :], rep[:, 1, :], rep[:, 2, :]

        chunks = []
        for si, sq in enumerate(seqs):
            row = sq["start"]
            for ci, c in enumerate(sq["chunks"]):
                chunks.append((si, ci, c, row, ci == len(sq["chunks"]) - 1))
                row += c

        def part_a(n):
            si, ci, c, row, last = chunks[n]
            b = n % 2
            g_ = gd[n % 3]
            for g in range(2):
                S.dma(g_[64 * g:64 * g + c, :], GD[row:row + c, 3088 * g:3088 * g + 3088], [], [g_])
            yield
            beta = g_[:, 3072:3080]
            gg = g_[:, 3080:3088]
            kn4 = g_[:, 512:1024].rearrange("p (j d) -> p j d", j=4)
            a0, a1, a2, a3 = pA
            S.mm(a3[:, 0:8], Ublk.ap(), gg, [Ublk, g_], [a3])
            S.mm(a3[:, 8:16], SLblk.ap(), gg, [SLblk, g_], [a3])
            for g in range(2):
                S.cp(gg2[64 * g:64 * g + 64, 8 * g:8 * g + 8], gg[64 * g:64 * g + 64, :], [g_], [gg2], e="pool")
            S.mm(a3[:, 16:32], ones32.ap(), gg2.ap(), [ones32, gg2], [a3])
            S.tt(gU.ap(), bc(U2, 1, [128, 8, 64]), bc(gg, 2, [128, 8, 64]), ALU.mult, [rep, g_], [gU], e="pool")
            S.ts(nbeta.ap(), beta, -1.0, ALU.mult, [g_], [nbeta], e="pool")
            S.af(eGG.ap(), a3[:, 0:16], AF.Exp, [a3], [eGG])
            S.af(eGlast[b].ap(), a3[:, 16:32], AF.Exp, [a3], [eGlast[b]])
            S.cp(qkb.ap(), g_[:, 0:1024], [g_], [qkb], e="act")
            yield
            if _os.environ.get("R1_NODMT", "") != "1":
                S.mm(a0[:, 0:8 * c], SLblk.ap(), gU[:, :, 0:c], [SLblk, gU], [a0])
            pTv = a1.ap().bitcast(BF16).rearrange("p (j a t) -> p j a t", j=8, a=2)
            for g in range(2):
                ps_ = slice(64 * g, 64 * g + c)
                for jl in range(4):
                    j = 4 * g + jl
                    if g == 0:
                        S.tr(pTv[:, j, 0, 0:c], qkb[ps_, 512 + jl * 128:512 + (jl + 1) * 128], Ibf2[ps_, 0:c], [qkb, Ibf2], [a1])
                        S.tr(pTv[:, j, 1, 0:c], qkb[ps_, jl * 128:(jl + 1) * 128], Ibf2[ps_, 0:c], [qkb, Ibf2], [a1])
                    else:
                        for dh in range(2):
                            do = slice(64 * dh, 64 * dh + 64)
                            S.tr(pTv[do, j, 0, 0:c], qkb[ps_, 512 + jl * 128 + 64 * dh:512 + jl * 128 + 64 * dh + 64], Ibf2[ps_, 0:c], [qkb, Ibf2], [a1])
                            S.tr(pTv[do, j, 1, 0:c], qkb[ps_, jl * 128 + 64 * dh:jl * 128 + 64 * dh + 64], Ibf2[ps_, 0:c], [qkb, Ibf2], [a1])
            S.af(E[:, :, 0:c], a0[:, 0:8 * c].rearrange("p (h t) -> p h t", h=8), AF.Exp, [a0], [E])
            S.cp(qkT[:, :, :, 0:c], pTv[:, :, :, 0:c], [a1], [qkT])
            yield
            for g in range(2):
                ps_ = slice(64 * g, 64 * g + c)
                for jl in range(4):
                    j = 4 * g + jl
                    S.mm(a2[ps_, jl * 128:jl * 128 + 2 * c], qkT[:, j, 0, 0:c], qkT[:, j, :, 0:c], [qkT], [a2])
            S.tt(T1[:, :, 0:c], E[:, :, 0:c], bc(SU2[:, 0:c], 1, [128, 8, c]), ALU.mult, [E, rep], [T1], e="pool")
            S.tt(E[:, :, 0:c], E[:, :, 0:c], bc(U2[:, 0:c], 1, [128, 8, c]), ALU.mult, [E, rep], [E], e="pool")
            for g in range(2):
                S.tt(IeG2[64 * g:64 * g + 64, 8 * g:8 * g + 8, 0:c], bc(I2[64 * g:64 * g + 64, 0:c], 1, [64, 8, c]),
                     bc(eGG[64 * g:64 * g + 64, 0:8], 2, [64, 8, c]), ALU.mult, [rep, eGG], [IeG2], e="pool")
            yield
            pv = a2[:, :].rearrange("p (j x) -> p j x", j=4)[:, :, 0:2 * c].rearrange("p j (a t) -> p j a t", a=2)
            for r_ in range(2):
                hs = slice(r_, 8, 2)
                S.tt(T1[:, hs, 0:c], T1[:, hs, 0:c], pv[:, :, 0, :], ALU.mult, [T1, a2], [T1])
                S.tt(attnT[b][:, hs, 0:c], E[:, hs, 0:c], pv[:, :, 1, :], ALU.mult, [E, a2], [attnT[b]])
            X0, Y0, Q0 = X[0], Y[0], Q[0]
            S.tt(X0[:, :, 0:c], T1[:, :, 0:c], bc(nbeta.ap(), 2, [128, 8, c]), ALU.mult, [T1, nbeta], [X0])
            for hf in range(2):
                pp = pA[hf]
                S.mm(pp[:, 0:8 * c], ones32.ap(), IeG2[:, 8 * hf:8 * hf + 8, 0:c], [ones32, IeG2], [pp])
            yield
            pYv = a3.ap().bitcast(BF16)
            for g in range(2):
                ps_ = slice(64 * g, 64 * g + c)
                for hl in range(8):
                    S.tr(pYv[ps_, hl * c:(hl + 1) * c], X0[ps_, hl, 0:c], Ibf2[ps_, 0:c], [X0, Ibf2], [a3])
            S.cp(Y0[:, :, 0:c], pYv[:, 0:8 * c].rearrange("p (h t) -> p h t", h=8), [a3], [Y0], e="act")
            S.tt(Q0[:, :, 0:c], X0[:, :, 0:c], bc(I2[:, 0:c], 1, [128, 8, c]), ALU.add, [X0, rep], [Q0])
            for hf in range(2):
                pp = pA[hf]
                ev = pp[:, 0:8 * c].rearrange("p (h t) -> p h t", h=8)
                for r_ in range(2):
                    hsl = slice(8 * hf + r_, 8 * hf + 8, 2)
                    jsl = slice(4 * hf, 4 * hf + 4)
                    S.tt(kTt[b][:, hsl, 0:c], ev[:, r_:8:2, :], qkT[:, jsl, 0, 0:c], ALU.mult, [pp, qkT], [kTt[b]])
                    S.tt(qTt[b][:, hsl, 0:c], ev[:, r_:8:2, :], qkT[:, jsl, 1, 0:c], ALU.mult, [pp, qkT], [qTt[b]])
            for r_ in range(2):
                S.tt(khat[b][:, r_:8:2, :], kn4, bc(eGG[:, 8 + r_:16:2], 2, [128, 4, 128]), ALU.mult, [g_, eGG], [khat[b]], e="pool")
            nq = 5 if c == 64 else 3
            for lv in range(nq):
                yield
                Xc, Yc, Qc = X[lv % 2], Y[lv % 2], Q[lv % 2]
                Xn, Yn, Qn = X[(lv + 1) % 2], Y[(lv + 1) % 2], Q[(lv + 1) % 2]
                py, px, pq = a2, a3, (a0 if lv % 2 == 0 else a1)
                for hl in range(8):
                    for g in range(2):
                        ps_ = slice(64 * g, 64 * g + c)
                        S.mm(py[ps_, hl * c:(hl + 1) * c], Xc[ps_, hl, 0:c], Yc[ps_, hl, 0:c], [Xc, Yc], [py])
                S.cp(Yn[:, :, 0:c], py[:, 0:8 * c].rearrange("p (h t) -> p h t", h=8), [py], [Yn], e="act")
                if lv < nq - 1:
                    for hl in range(8):
                        for g in range(2):
                            ps_ = slice(64 * g, 64 * g + c)
                            S.mm(px[ps_, hl * c:(hl + 1) * c], Yc[ps_, hl, 0:c], Xc[ps_, hl, 0:c], [Xc, Yc], [px])
                    S.cp(Xn[:, :, 0:c], px[:, 0:8 * c].rearrange("p (h t) -> p h t", h=8), [px], [Xn], e="dve")
                for hl in range(8):
                    for g in range(2):
                        ps_ = slice(64 * g, 64 * g + c)
                        S.mm(pq[ps_, hl * c:(hl + 1) * c], Yn[ps_, hl, 0:c], Qc[ps_, hl, 0:c], [Yn, Qc], [pq])
                dst = TT[b] if lv == nq - 1 else Qn
                S.tt(dst[:, :, 0:c], Qc[:, :, 0:c], pq[:, 0:8 * c].rearrange("p (h t) -> p h t", h=8), ALU.add, [Qc, pq], [dst])

        sgi = [0]

        def part_b(n):
            si, ci, c, row, last = chunks[n]
            sq = seqs[si]
            b = n % 2
            g_, mgo = gd[n % 3], mg[b]
            if ci == 0:
                if sq["init"] is None:
                    S.memset(Sg, 0.0)
                else:
                    S.dma(Sg, st_gS[sq["init"]].rearrange("h k v -> k h v"), [], [Sg])
                S.cp(Sgb[sgi[0] % 2].ap(), Sg.ap(), [Sg], [Sgb[sgi[0] % 2]], e="act")
            Sb = Sgb[sgi[0] % 2]
            Sbn = Sgb[(sgi[0] + 1) % 2]
            sgi[0] += 1
            v3 = g_[:, 1024:2048].rearrange("p (h v) -> p h v", h=8)
            gz = g_[:, 2048:3072]
            beta = g_[:, 3072:3080]
            for hl in range(8):
                for g in range(2):
                    ps_ = slice(64 * g, 64 * g + c)
                    S.mm(pB[0][ps_, hl, :], kTt[b][:, 8 * g + hl, 0:c], Sb[:, 8 * g + hl, :], [kTt[b], Sb], [pB[0]])
            S.tt(yv.ap(), v3, pB[0].ap(), ALU.subtract, [g_, pB[0]], [yv])
            yield
            for hl in range(8):
                for g in range(2):
                    ps_ = slice(64 * g, 64 * g + c)
                    S.mm(pB[1][ps_, hl, :], TT[b][ps_, hl, 0:c], yv[ps_, hl, :], [TT[b], yv], [pB[1]])
            S.tt(vnew.ap(), pB[1].ap(), bc(beta, 2, [128, 8, 128]), ALU.mult, [pB[1], g_], [vnew])
            yield
            for g in range(2):
                ps_ = slice(64 * g, 64 * g + c)
                for hl in range(8):
                    S.mm(pB[g][:, hl, :], khat[b][ps_, hl, :], vnew[ps_, hl, :], [khat[b], vnew], [pB[g]])
                for hl in range(8):
                    hh = 8 * g + hl
                    S.stt(Sg[:, hh, :], Sg[:, hh, :], eGlast[b][:, hh:hh + 1], pB[g][:, hl, :], ALU.mult, ALU.add, [Sg, eGlast[b], pB[g]], [Sg])
            if not last:
                S.cp(Sbn.ap(), Sg.ap(), [Sg], [Sbn], e="act")
            yield
            for hl in range(8):
                for g in range(2):
                    ps_ = slice(64 * g, 64 * g + c)
                    S.mm(pB[0][ps_, hl, :], qTt[b][:, 8 * g + hl, 0:c], Sb[:, 8 * g + hl, :], [qTt[b], Sb], [pB[0]], start=True, stop=False)
                    S.mm(pB[0][ps_, hl, :], attnT[b][ps_, hl, 0:c], vnew[ps_, hl, :], [attnT[b], vnew], [pB[0]], start=False, stop=True)
            S.af(osq.ap(), pB[0].ap(), AF.Square, [pB[0]], [osq])
            yield
            S.red(oss.ap(), osq.ap(), ALU.add, [osq], [oss])
            S.af(oss.ap(), oss.ap(), AF.Sqrt, [oss], [oss], scale=1.0 / 128.0, bias=EPS)
            S.recip(orstd.ap(), oss.ap(), [oss], [orstd])
            for hl in range(8):
                S.af(on[:, hl, :], pB[0][:, hl, :], AF.Copy, [pB[0], orstd], [on], scale=orstd[:, hl:hl + 1])
            yield
            S.tt(mgo.ap(), on.ap().rearrange("p h v -> p (h v)"), gz, ALU.mult, [on, g_], [mgo])
            for g in range(2):
                S.dma(MG1[row:row + c, 1024 * g:1024 * g + 1024], mgo[64 * g:64 * g + c, :], [mgo], [], q="pool")
            if last:
                S.dma(o_gS[si].rearrange("h k v -> k h v"), Sg, [Sg], [], q="pool")

        def interleave(gens):
            gens = [g for g in gens if g is not None]
            while gens:
                for g in list(gens):
                    try:
                        next(g)
                    except StopIteration:
                        gens.remove(g)

        _mode = _os.environ.get("R1_MODE", "")
        _lim = int(_os.environ.get("R1_LIM", "99"))

        def lim(gen):
            for i, _ in enumerate(gen):
                if i + 1 >= _lim:
                    break
                yield

        if _mode == "":
            interleave([part_a(0)])
        for n in range(len(chunks)):
            if _mode == "":
                interleave([part_b(n), part_a(n + 1) if n + 1 < len(chunks) else None])
            elif _mode == "a_only":
                interleave([lim(part_a(n))])
        S.pop()

    phase_d1_even()
    phase_r0()
    phase_d3ab(0, MG0, 1024, even_w_out, xin)
    phase_d3c(0, X2, False)
    phase_d1_odd()
    phase_c1()
    phase_r1()
    phase_d3ab(1, MG1, 2048, odd_w_out, X2)
    phase_d3c(1, y, True)
    S.finish()
    return nc, seqs, NT, S


def make_consts():
    idx = np.arange(64)
    U = (idx[:, None] <= idx[None, :]).astype(np.float32)
    SU = (idx[:, None] < idx[None, :]).astype(np.float32)
    SL = (idx[:, None] > idx[None, :]).astype(np.float32)
    bm4 = np.zeros((4, 4, 64), np.float32)
    for r in range(4):
        bm4[r, r, :] = 1.0
    Z = np.zeros((64, 64), np.float32)
    Ublk = np.block([[U, Z], [Z, U]])
    SLblk = np.block([[SL, Z], [Z, SL]])
    I64 = np.eye(64, dtype=np.float32)
    rep = np.stack([np.concatenate([U, U], 0), np.concatenate([SU, SU], 0), np.concatenate([I64, I64], 0)], 1)
    return dict(c_ident=np.eye(128, dtype=np.float32), c_U=U, c_SU=SU, c_SL=SL, c_bm4=bm4,
                c_Ublk=np.ascontiguousarray(Ublk), c_SLblk=np.ascontiguousarray(SLblk), c_rep=np.ascontiguousarray(rep))


def make_in_maps(inp, T, NS, n_cores, seqs, NT):
    f = lambda a: np.ascontiguousarray(np.asarray(a, dtype=np.float32))
    xp = f(inp["x_prompt"])
    xs = f(inp["x_sample"])
    meta = f(inp["meta_tokens"])
    consts = make_consts()
    shared = dict(
        norm_mix=f(inp["norm_mix"]), norm_ffn=f(inp["norm_ffn"]), norm_final=f(inp["norm_final"]).reshape(1, D),
        even_w_in=f(inp["even_w_in"])[0], even_w_out=f(inp["even_w_out"])[0], lb_logits=f(inp["hgrn_lb_logits"]),
        hgrn_norm=f(inp["hgrn_norm"])[0:1], mlstm_bif=np.concatenate([f(inp["mlstm_b_i"])[0], f(inp["mlstm_b_f"])[0]]).reshape(1, 8),
        mlstm_norm=f(inp["mlstm_norm"])[0:1], odd_w_in=f(inp["odd_w_in"])[0], odd_conv_w=f(inp["odd_conv_w"])[0],
        gdn_a_log=f(inp["gdn_a_log"])[0:1], gdn_dt_bias=f(inp["gdn_dt_bias"])[0:1], gdn_norm=f(inp["gdn_norm"])[0:1],
        odd_w_out=f(inp["odd_w_out"])[0], ffn_w_in=f(inp["ffn_w_in"]), ffn_w_out=f(inp["ffn_w_out"]), **consts)
    B = xp.shape[0]
    maps = []
    for c in range(n_cores):
        xin = np.zeros((NT, D), np.float32)
        s0 = seqs[0]["start"]
        if c < B:
            xin[s0:s0 + 16] = meta
            xin[s0 + 16:s0 + 16 + T] = xp[c]
        sidx = [c * NS + i for i in range(NS)]
        for i, sj in enumerate(sidx):
            st = seqs[1 + i]["start"]
            if sj < xs.shape[0]:
                xin[st:st + 64] = xs[sj]

        def gather(a):
            a = f(a)[0]
            out = np.zeros((NS,) + a.shape[1:], np.float32)
            for i, sj in enumerate(sidx):
                if sj < a.shape[0]:
                    out[i] = a[sj]
            return out
        m = dict(shared)
        m.update(xin=xin, st_hS=gather(inp["state_hgrn_S"]), st_mC=gather(inp["state_mlstm_C"]), st_mn=gather(inp["state_mlstm_n"]),
                 st_mm=gather(inp["state_mlstm_m"]), st_gS=gather(inp["state_gdn_S"]), st_gc=gather(inp["state_gdn_conv"]))
        maps.append(m)
    return maps


_CACHE = {}


def run(inp, T, NS, n_cores, debug=False, trace=False):
    key = (T, NS, debug)
    if key not in _CACHE:
        _CACHE[key] = build(T, NS, debug)
    nc, seqs, NT, S = _CACHE[key]
    maps = make_in_maps(inp, T, NS, n_cores, seqs, NT)
    res = run_bass_kernel_spmd(nc, maps, core_ids=list(range(n_cores)), **({"trace": True} if trace else {}))
    return res, seqs, NT


def assemble(res, seqs, NT, B, T, NSAMP, NS, n_cores):
    R = res.results
    s0 = seqs[0]["start"]
    y_prompt = np.stack([R[c]["y"][s0 + 16:s0 + 16 + T] for c in range(B)], 0)
    y_sample = np.zeros((NSAMP, 64, D), np.float32)
    names = ["o_hS", "o_mC", "o_mn", "o_mm", "o_gS", "o_gc"]
    pst = {n: np.stack([R[c][n][0] for c in range(B)], 0)[None] for n in names}
    sst = {n: np.zeros((1, NSAMP) + R[0][n].shape[1:], np.float32) for n in names}
    for c in range(n_cores):
        for i in range(NS):
            sj = c * NS + i
            if sj >= NSAMP:
                continue
            st = seqs[1 + i]["start"]
            y_sample[sj] = R[c]["y"][st:st + 64]
            for n in names:
                sst[n][0, sj] = R[c][n][1 + i]
    outs = [y_prompt, y_sample] + [pst[n] for n in names] + [sst[n] for n in names]
    return tuple(np.ascontiguousarray(o.astype(np.float32)) for o in outs)


def kernel(**inputs):
    B = inputs["x_prompt"].shape[0]
    T = inputs["x_prompt"].shape[1]
    NSAMP = inputs["x_sample"].shape[0]
    NS = -(-NSAMP // N_CORES)
    res, seqs, NT = run(inputs, T, NS, N_CORES)
    return assemble(res, seqs, NT, B, T, NSAMP, NS, N_CORES)
```

```python
import numpy as np
from contextlib import ExitStack
import concourse.bass as bass
import concourse.mybir as mybir
from concourse.bass_utils import run_bass_kernel_spmd

F32 = mybir.dt.float32
BF16 = mybir.dt.bfloat16
AF = mybir.ActivationFunctionType
ALU = mybir.AluOpType
AX = mybir.AxisListType

D = 1024
FF = 2816
EVEN_COLS = 3592
ODD_COLS = 6176
EPS = 1e-6
N_CORES = 8
T_FULL = 8192
NS_FULL = 2
DBL_BF16 = True


class TB:
    def __init__(self, t, name):
        self.t = t
        self.name = name
        self.lw = None
        self.rd = {}

    def ap(self):
        return self.t[:]

    def __getitem__(self, idx):
        return self.t[idx]


class Sched:
    def __init__(self, nc, n_dma_slots=8, same_engine_sync=True):
        self.nc = nc
        self.es0 = ExitStack()
        self.stack = [self.es0]
        self.eng = {"pe": nc.tensor, "act": nc.scalar, "dve": nc.vector, "pool": nc.gpsimd, "sp": nc.sync}
        self.sems = {}
        self.cnt = {}
        for e in ("pe", "act", "dve", "pool"):
            self.sems[e] = self.es0.enter_context(nc.semaphore("s_" + e))
            self.cnt[e] = 0
        self.nslots = n_dma_slots
        self.dma_q = {}
        for q in ("sp", "pool"):
            slots = []
            for i in range(n_dma_slots):
                key = ("dma", q, i)
                self.sems[key] = self.es0.enter_context(nc.semaphore("d_%s%d" % (q, i)))
                self.cnt[key] = 0
                slots.append(key)
            self.dma_q[q] = [slots, 0]
        self.waited = {e: {} for e in self.eng}
        self.same = same_engine_sync
        self.noself = {"pe"}
        self.ninst = 0

    def push(self):
        self.phase = getattr(self, "phase", 0) + 1
        self.sb_bytes = getattr(self, "sb_base", 0)
        self.stack.append(ExitStack())

    def pop(self):
        self.barrier()
        self.stack.pop().close()

    def sb(self, name, shape, dtype):
        nb = int(np.prod(shape[1:])) * (2 if dtype == BF16 else 4)
        nb = -(-nb // 32) * 32
        if len(self.stack) == 1:
            self.sb_base = getattr(self, "sb_base", 0) + nb
        self.sb_bytes = getattr(self, "sb_bytes", 0) + nb
        assert self.sb_bytes <= 200 * 1024, ("SBUF budget exceeded", name, self.sb_bytes)
        t = self.stack[-1].enter_context(self.nc.sbuf_tensor("p%d_%s" % (getattr(self, "phase", 0), name), list(shape), dtype))
        return TB(t, name)

    def ps(self, name, shape, dtype):
        t = self.stack[-1].enter_context(self.nc.psum_tensor("p%d_%s" % (getattr(self, "phase", 0), name), list(shape), dtype))
        return TB(t, name)

    def _wait(self, e, k, v):
        if v <= 0 or self.waited[e].get(k, 0) >= v:
            return
        self.eng[e].wait_ge(self.sems[k], v)
        self.waited[e][k] = v
        self.ninst += 1

    def barrier(self):
        for e in self.eng:
            for k, v in self.cnt.items():
                if k == e:
                    continue
                self._wait(e, k, v)

    def _deps(self, e, reads, writes):
        need = {}

        def add(ev):
            if ev is None:
                return
            k, v = ev
            if need.get(k, 0) < v:
                need[k] = v

        for b in reads:
            add(b.lw)
        for b in writes:
            add(b.lw)
            for k, v in b.rd.items():
                add((k, v))
        for k, v in need.items():
            if k == e and (e in self.noself or not self.same):
                continue
            self._wait(e, k, v)

    def _commit(self, ev, reads, writes):
        k, v = ev
        for b in reads:
            if b.rd.get(k, 0) < v:
                b.rd[k] = v
        for b in writes:
            b.lw = ev
            b.rd = {}

    def op(self, e, fn, reads, writes):
        self._deps(e, reads, writes)
        inst = fn()
        self.cnt[e] += 1
        inst.then_inc(self.sems[e], 1)
        self._commit((e, self.cnt[e]), reads, writes)
        self.ninst += 1
        return inst

    def dma(self, out, in_, reads, writes, q="sp", **kw):
        slots, n = self.dma_q[q]
        key = slots[n % self.nslots]
        self.dma_q[q][1] = n + 1
        self._wait(q, key, self.cnt[key])
        self._deps(q, reads, writes)
        o = out.ap() if isinstance(out, TB) else out
        i = in_.ap() if isinstance(in_, TB) else in_
        inst = self.eng[q].dma_start(out=o, in_=i, **kw)
        self.cnt[key] += 16
        inst.then_inc(self.sems[key], 16)
        self._commit((key, self.cnt[key]), reads, writes)
        self.ninst += 1
        return inst

    def finish(self):
        self.barrier()
        while self.stack:
            self.stack.pop().close()

    def mm(self, out, lhsT, rhs, R, W, start=True, stop=True):
        nc = self.nc
        return self.op("pe", lambda: nc.tensor.matmul(out, lhsT=lhsT, rhs=rhs, start=start, stop=stop), R, W)

    def tr(self, out, in_, ident, R, W):
        nc = self.nc
        return self.op("pe", lambda: nc.tensor.transpose(out, in_, ident), R, W)

    def tt(self, out, a, b, op, R, W, e="dve"):
        eo = self.nc.vector if e == "dve" else self.nc.gpsimd
        return self.op(e, lambda: eo.tensor_tensor(out=out, in0=a, in1=b, op=op), R, W)

    def ts(self, out, a, s1, op0, R, W, s2=None, op1=None, e="dve"):
        eo = self.nc.vector if e == "dve" else self.nc.gpsimd
        if op1 is None:
            return self.op(e, lambda: eo.tensor_scalar(out=out, in0=a, scalar1=s1, scalar2=None, op0=op0), R, W)
        return self.op(e, lambda: eo.tensor_scalar(out=out, in0=a, scalar1=s1, scalar2=s2, op0=op0, op1=op1), R, W)

    def stt(self, out, a, s, b, op0, op1, R, W):
        nc = self.nc
        return self.op("dve", lambda: nc.vector.scalar_tensor_tensor(out=out, in0=a, scalar=s, in1=b, op0=op0, op1=op1), R, W)

    def af(self, out, in_, func, R, W, scale=None, bias=None, accum=None):
        nc = self.nc
        kw = {}
        if scale is not None:
            kw["scale"] = scale
        if bias is not None:
            kw["bias"] = bias
        if accum is not None:
            kw["accum_out"] = accum
        return self.op("act", lambda: nc.scalar.activation(out=out, in_=in_, func=func, **kw), R, W)

    def cp(self, out, in_, R, W, e="dve"):
        nc = self.nc
        if e == "act":
            return self.op("act", lambda: nc.scalar.copy(out=out, in_=in_), R, W)
        eo = nc.vector if e == "dve" else nc.gpsimd
        return self.op(e, lambda: eo.tensor_copy(out=out, in_=in_), R, W)

    def red(self, out, in_, op, R, W):
        nc = self.nc
        return self.op("dve", lambda: nc.vector.tensor_reduce(out=out, in_=in_, axis=AX.X, op=op), R, W)

    def recip(self, out, in_, R, W):
        nc = self.nc
        return self.op("dve", lambda: nc.vector.reciprocal(out=out, in_=in_), R, W)

    def memset(self, tb, val, e="dve"):
        eo = self.nc.vector if e == "dve" else self.nc.gpsimd
        return self.op(e, lambda: eo.memset(tb.ap(), val), [], [tb])


def bc(ap, axis, shape):
    return ap.unsqueeze(axis).broadcast_to(list(shape))


def seq_layout(T, NS):
    seqs = []
    r = 3
    seqs.append(dict(start=r, len=16 + T, chunks=[16] + [64] * (T // 64), init=None))
    r += 16 + T
    for i in range(NS):
        r += 3
        seqs.append(dict(start=r, len=64, chunks=[64], init=i))
        r += 64
    NT = -(-r // 128) * 128
    return seqs, NT


def load_weight(S, Wsb, Wd, K):
    for k in range(K // 128):
        S.dma(Wsb[:, k, :], Wd[k * 128:(k + 1) * 128, :], [], [], q="pool")


def rmsnorm_tile(S, xt, wbc, h, junk, ss, rstd):
    S.af(junk.ap(), xt.ap(), AF.Square, [xt], [junk, ss], accum=ss.ap())
    S.af(ss.ap(), ss.ap(), AF.Sqrt, [ss], [ss], scale=1.0 / D, bias=EPS)
    S.recip(rstd.ap(), ss.ap(), [ss], [rstd])
    S.stt(h.ap(), xt.ap(), rstd.ap(), wbc.ap(), ALU.mult, ALU.mult, [xt, rstd, wbc], [h])


def transpose_tile(S, src, nk, ptb, dst, ident, e="dve"):
    for k0 in range(0, nk, 8):
        n = min(8, nk - k0)
        for k in range(n):
            S.tr(ptb[:, k, :], src[:, (k0 + k) * 128:(k0 + k + 1) * 128], ident.ap(), [src, ident], [ptb])
        S.cp(dst[:, k0:k0 + n, :], ptb[:, 0:n, :], [ptb], [dst], e=e)


def run_pipelined(n, A1, B, A2):
    A1(0)
    A2(0)
    for i in range(n):
        if i + 1 < n:
            A1(i + 1)
        B(i)
        if i + 1 < n:
            A2(i + 1)


def load_bcast(S, tb, dram_row, n):
    S.dma(tb, dram_row.broadcast_to([128, n]), [], [tb])


def build(T=T_FULL, NS=NS_FULL, debug=False):
    nc = bass.Bass("TRN2", target_bir_lowering=False)
    seqs, NT = seq_layout(T, NS)
    NSEQ = 1 + NS
    ntile = NT // 128

    def din(name, shape, dt=F32):
        return nc.dram_tensor(name, list(shape), dt, kind="ExternalInput").ap()

    def dout(name, shape, dt=F32):
        return nc.dram_tensor(name, list(shape), dt, kind="ExternalOutput").ap()

    def dscr(name, shape, dt=F32):
        return nc.dram_tensor(name, list(shape), dt, kind="ExternalOutput" if debug else "Internal").ap()

    xin = din("xin", [NT, D])
    st_hS = din("st_hS", [NS, 4, 128, 128])
    st_mC = din("st_mC", [NS, 4, 64, 128])
    st_mn = din("st_mn", [NS, 4, 64])
    st_mm = din("st_mm", [NS, 4])
    st_gS = din("st_gS", [NS, 16, 128, 128])
    st_gc = din("st_gc", [NS, 3, 4096])
    norm_mix = din("norm_mix", [2, D])
    norm_ffn = din("norm_ffn", [2, D])
    norm_final = din("norm_final", [1, D])
    even_w_in = din("even_w_in", [D, EVEN_COLS])
    even_w_out = din("even_w_out", [D, D])
    lb_logits = din("lb_logits", [2, 512])
    hgrn_norm = din("hgrn_norm", [1, 512])
    mlstm_bif = din("mlstm_bif", [1, 8])
    mlstm_norm = din("mlstm_norm", [1, 512])
    odd_w_in = din("odd_w_in", [D, ODD_COLS])
    odd_conv_w = din("odd_conv_w", [4, 4096])
    gdn_a_log = din("gdn_a_log", [1, 16])
    gdn_dt_bias = din("gdn_dt_bias", [1, 16])
    gdn_norm = din("gdn_norm", [1, 128])
    odd_w_out = din("odd_w_out", [2048, D])
    ffn_w_in = din("ffn_w_in", [2, D, 2 * FF])
    ffn_w_out = din("ffn_w_out", [2, FF, D])
    c_ident = din("c_ident", [128, 128])
    c_U = din("c_U", [64, 64])
    c_SU = din("c_SU", [64, 64])
    c_SL = din("c_SL", [64, 64])
    c_bm4 = din("c_bm4", [4, 4, 64])
    c_Ublk = din("c_Ublk", [128, 128])
    c_SLblk = din("c_SLblk", [128, 128])
    c_rep = din("c_rep", [128, 3, 64])

    y = dout("y", [NT, D])
    o_hS = dout("o_hS", [NSEQ, 4, 128, 128])
    o_mC = dout("o_mC", [NSEQ, 4, 64, 128])
    o_mn = dout("o_mn", [NSEQ, 4, 64])
    o_mm = dout("o_mm", [NSEQ, 4])
    o_gS = dout("o_gS", [NSEQ, 16, 128, 128])
    o_gc = dout("o_gc", [NSEQ, 3, 4096])

    HG = dscr("HG", [NT, 2560])
    ML = dscr("ML", [NT, 1544])
    MG0 = dscr("MG0", [NT, 1024], BF16)
    X1 = dscr("X1", [NT, D])
    ACTF = dscr("ACTF", [NT, FF], BF16)
    X2 = dscr("X2", [NT, D])
    PQ = dscr("PQ", [NT + 3, 4096])
    GD = dscr("GD", [NT, 6176])
    MG1 = dscr("MG1", [NT, 2048], BF16)
    X3 = dscr("X3", [NT, D])

    S = Sched(nc)
    gap_rows = []
    _r = 0
    for _sq in seqs:
        if _sq["start"] > _r:
            gap_rows.append((_r, _sq["start"]))
        _r = _sq["start"] + _sq["len"]
    if _r < NT:
        gap_rows.append((_r, NT))

    def zero_gap_rows(dst, ncols, dt):
        z = S.sb("zgap_" + str(getattr(S, "phase", 0)), [128, ncols], dt)
        S.memset(z, 0.0)
        for (a_, b_) in gap_rows:
            S.dma(dst[a_:b_, :], z[0:b_ - a_, :], [z], [], q="pool")

    I32 = S.sb("I32", [128, 128], F32)
    Ibf = S.sb("Ibf", [128, 128], BF16)
    U32 = S.sb("U32", [64, 64], F32)
    SU32 = S.sb("SU32", [64, 64], F32)
    SL32 = S.sb("SL32", [64, 64], F32)
    ones32 = S.sb("ones32", [128, 128], F32)
    bm4 = S.sb("bm4", [4, 4, 64], F32)
    S.dma(I32, c_ident, [], [I32])
    S.dma(Ibf, c_ident, [], [Ibf], q="pool")
    S.dma(U32, c_U, [], [U32])
    S.dma(SU32, c_SU, [], [SU32])
    S.dma(SL32, c_SL, [], [SL32])
    S.dma(bm4, c_bm4, [], [bm4])
    S.memset(ones32, 1.0)
    S.barrier()

    def dense_blocks(hT, nk, W, blocks, pms, pmi):
        for (c0, ncols, epi) in blocks:
            pm = pms[pmi % len(pms)]
            pmi += 1
            for k in range(nk):
                S.mm(pm[:, 0:ncols], hT[:, k, :], W[:, k, c0:c0 + ncols], [hT, W], [pm], start=(k == 0), stop=(k == nk - 1))
            epi(pm)
        return pmi

    def phase_d1_even():
        S.push()
        W = S.sb("W", [128, 8, EVEN_COLS], BF16)
        wn = S.sb("wn", [128, D], F32)
        lb = S.sb("lb", [128, 512], F32)
        oml = S.sb("oml", [128, 512], F32)
        hnw = S.sb("hnw", [128, 512], F32)
        mnw = S.sb("mnw", [128, 512], F32)
        bif = S.sb("bif", [128, 8], F32)
        junk = S.sb("junk", [128, D], F32)
        tmp = [S.sb("tmp%d" % i, [128, 512], F32) for i in range(2)]
        t8 = S.sb("t8", [128, 8], F32)
        t8b = S.sb("t8b", [128, 4], F32)
        xt = [S.sb("xt%d" % i, [128, D], F32) for i in range(2)]
        ss = [S.sb("ss%d" % i, [128, 1], F32) for i in range(2)]
        rstd = [S.sb("rstd%d" % i, [128, 1], F32) for i in range(2)]
        h = [S.sb("h%d" % i, [128, D], BF16) for i in range(2)]
        hT = [S.sb("hT%d" % i, [128, 8, 128], BF16) for i in range(2)]
        HGt = [S.sb("HGt%d" % i, [128, 5, 512], F32) for i in range(2)]
        MLt = [S.sb("MLt%d" % i, [128, 1544], F32) for i in range(2)]
        ptb = [S.ps("ptb%d" % i, [128, 8, 128], BF16) for i in range(2)]
        pms = [S.ps("pm%d" % i, [128, 512], F32) for i in range(6)]

        load_weight(S, W, even_w_in, D)
        load_bcast(S, wn, norm_mix[0:1, :], D)
        load_bcast(S, lb, lb_logits[0:1, :], 512)
        load_bcast(S, oml, lb_logits[1:2, :], 512)
        load_bcast(S, hnw, hgrn_norm[0:1, :], 512)
        load_bcast(S, mnw, mlstm_norm[0:1, :], 512)
        load_bcast(S, bif, mlstm_bif[0:1, :], 8)
        S.tt(lb.ap(), lb.ap(), oml.ap(), ALU.subtract, [lb, oml], [lb])
        S.af(lb.ap(), lb.ap(), AF.Sigmoid, [lb], [lb])
        S.ts(oml.ap(), lb.ap(), -1.0, ALU.mult, [lb], [oml], s2=1.0, op1=ALU.add)
        S.barrier()

        pmi_box = [0]

        def stA1(i):
            b = i % 2
            S.dma(xt[b], xin[i * 128:i * 128 + 128, :], [], [xt[b]])
            rmsnorm_tile(S, xt[b], wn, h[b], junk, ss[b], rstd[b])

        def stA2(i):
            b = i % 2
            transpose_tile(S, h[b], 8, ptb[b], hT[b], Ibf)

        def stB(i):
            pmi = pmi_box[0]
            b = i % 2
            r0 = i * 128
            hg, ml = HGt[b], MLt[b]
            t0, t1 = tmp

            def e_q(pm):
                S.af(hg[:, 0, :], pm.ap(), AF.Silu, [pm], [hg])

            def e_f(pm):
                S.af(t0.ap(), pm.ap(), AF.Sigmoid, [pm], [t0])
                S.tt(t0.ap(), t0.ap(), oml.ap(), ALU.mult, [t0, oml], [t0])
                S.tt(t0.ap(), t0.ap(), lb.ap(), ALU.add, [t0, lb], [t0])
                S.af(hg[:, 1, :], t0.ap(), AF.Ln, [t0], [hg])
                S.ts(hg[:, 2, :], t0.ap(), -1.0, ALU.mult, [t0], [hg], s2=1.0, op1=ALU.add)

            def e_v(pm):
                S.cp(hg[:, 3, :], pm.ap(), [pm], [hg])

            def e_g(pm):
                S.af(t1.ap(), pm.ap(), AF.Silu, [pm], [t1])
                S.tt(hg[:, 4, :], t1.ap(), hnw.ap(), ALU.mult, [t1, hnw], [hg])

            def e_qk(pm):
                S.cp(ml[:, 0:256], pm[:, 0:256], [pm], [ml])
                S.ts(ml[:, 256:512], pm[:, 256:512], 0.125, ALU.mult, [pm], [ml])

            def e_mv(pm):
                S.cp(ml[:, 512:1024], pm.ap(), [pm], [ml])

            def e_o(pm):
                S.af(t1.ap(), pm.ap(), AF.Sigmoid, [pm], [t1])
                S.tt(ml[:, 1024:1536], t1.ap(), mnw.ap(), ALU.mult, [t1, mnw], [ml])

            def e_if(pm):
                S.tt(t8.ap(), pm[:, 0:8], bif.ap(), ALU.add, [pm, bif], [t8])
                S.af(t8.ap(), t8.ap(), AF.Tanh, [t8], [t8], scale=1.0 / 15.0)
                S.ts(ml[:, 1536:1540], t8[:, 0:4], 15.0, ALU.mult, [t8], [ml])
                S.af(t8b.ap(), t8[:, 4:8], AF.Exp, [t8], [t8b], scale=-15.0)
                S.af(t8b.ap(), t8b.ap(), AF.Ln, [t8b], [t8b], bias=1.0)
                S.ts(ml[:, 1540:1544], t8b.ap(), -1.0, ALU.mult, [t8b], [ml])

            blocks = [(0, 512, e_q), (1536, 512, e_g), (512, 512, e_f), (1024, 512, e_v),
                      (2048, 512, e_qk), (2560, 512, e_mv), (3072, 512, e_o), (3584, 8, e_if)]
            pmi_box[0] = dense_blocks(hT[b], 8, W, blocks, pms, pmi)
            S.dma(HG[r0:r0 + 128, :], hg.ap().rearrange("p a b -> p (a b)"), [hg], [], q="pool")
            S.dma(ML[r0:r0 + 128, :], ml, [ml], [], q="pool")

        run_pipelined(ntile, stA1, stB, stA2)
        S.pop()

    def phase_r0():
        S.push()
        Sh = S.sb("Sh", [128, 4, 128], F32)
        Shb = S.sb("Shb", [128, 4, 128], BF16)
        Cn = S.sb("Cn", [64, 4, 129], F32)
        Cnb = S.sb("Cnb", [64, 4, 129], BF16)
        m_row = S.sb("m_row", [4, 1], F32)
        m_bc = S.sb("m_bc", [64, 4], F32)
        hgc = [S.sb("hgc%d" % i, [64, 5, 512], F32) for i in range(2)]
        mlc = [S.sb("mlc%d" % i, [64, 1544], F32) for i in range(2)]
        mg = [S.sb("mg%d" % i, [64, 1024], BF16) for i in range(2)]
        eb = S.sb("eb", [64, 512], F32)
        enb = S.sb("enb", [64, 512], F32)
        qt = S.sb("qt", [64, 512], BF16)
        kt = S.sb("kt", [64, 512], BF16)
        qkT = S.sb("qkT", [128, 8, 64], BF16)
        ebl = S.sb("ebl", [128, 4, 2], F32)
        scm = S.sb("scm", [64, 4, 64], BF16)
        vb = S.sb("vb", [64, 512], BF16)
        stmp = S.sb("stmp", [128, 4, 128], F32)
        osq = S.sb("osq", [64, 512], F32)
        oss = S.sb("oss", [64, 4], F32)
        orstd = S.sb("orstd", [64, 4], F32)
        on = S.sb("on", [64, 4, 128], F32)
        a_t = S.sb("a_t", [64, 4], F32)
        aT = S.sb("aT", [4, 64], F32)
        M_row = S.sb("M_row", [4, 64], F32)
        Mblk = S.sb("Mblk", [4, 4, 64], F32)
        Mlb = S.sb("Mlb", [4, 4], F32)
        M_t = S.sb("M_t", [64, 4], F32)
        F_t = S.sb("F_t", [64, 4], F32)
        DT = S.sb("DT", [64, 4, 64], F32)
        Wbc = S.sb("Wbc", [64, 4, 64], F32)
        mq = S.sb("mq", [64, 512], BF16)
        mqkT = S.sb("mqkT", [64, 8, 64], BF16)
        qTt = S.sb("qTt", [64, 4, 64], BF16)
        qkD = S.sb("qkD", [64, 4, 64], BF16)
        vext = S.sb("vext", [64, 4, 129], BF16)
        den = S.sb("den", [64, 4], F32)
        nden = S.sb("nden", [64, 4], F32)
        emt = S.sb("emt", [64, 4], F32)
        rden = S.sb("rden", [64, 4], F32)
        hn = S.sb("hn", [64, 4, 128], F32)
        w_s = S.sb("w_s", [64, 4], F32)
        khat = S.sb("khat", [64, 4, 64], BF16)
        dec = S.sb("dec", [64, 4], F32)
        Flb = S.sb("Flb", [64, 4], F32)
        Mlbc = S.sb("Mlbc", [64, 4], F32)
        fl_row = S.sb("fl_row", [4, 2], F32)
        ones_c2 = S.sb("ones_c2", [64, 2], F32)
        pA = S.ps("pA", [128, 512], F32)
        pB = S.ps("pB", [128, 8, 64], BF16)
        pC = S.ps("pC", [128, 512], F32)
        pD = S.ps("pD", [128, 512], F32)
        pE = S.ps("pE", [128, 512], F32)
        pF = S.ps("pF", [64, 4, 256], F32)
        pB2 = S.ps("pB2", [128, 8, 64], BF16)
        osq2 = S.sb("osq2", [64, 512], F32)
        oss2 = S.sb("oss2", [64, 4], F32)
        orstd2 = S.sb("orstd2", [64, 4], F32)
        S.memset(ones_c2, 1.0)
        zero_gap_rows(MG0, 1024, BF16)

        def interleave(gens):
            gens = [g for g in gens if g is not None]
            while gens:
                for g in list(gens):
                    try:
                        next(g)
                    except StopIteration:
                        gens.remove(g)

        for si, sq in enumerate(seqs):
            if sq["init"] is None:
                S.memset(Sh, 0.0)
                S.memset(Cn, 0.0)
                S.memset(m_row, 0.0)
                S.memset(m_bc, 0.0)
            else:
                j = sq["init"]
                S.dma(Sh, st_hS[j].rearrange("h k v -> k h v"), [], [Sh])
                S.dma(Cn[:, :, 0:128], st_mC[j].rearrange("h k v -> k h v"), [], [Cn])
                S.dma(Cn[:, :, 128:129], st_mn[j].rearrange("h (k o) -> k h o", o=1), [], [Cn], allow_slow_non_contiguous=True)
                S.dma(m_row, st_mm[j].rearrange("(h o) -> h o", o=1), [], [m_row])
                S.dma(m_bc, st_mm[j:j + 1, :].broadcast_to([64, 4]), [], [m_bc])
            S.cp(Shb.ap(), Sh.ap(), [Sh], [Shb], e="act")
            S.cp(Cnb.ap(), Cn.ap(), [Cn], [Cnb], e="act")
            row = sq["start"]
            for ci, c in enumerate(sq["chunks"]):
                b = ci % 2
                hg, ml, mgo = hgc[b], mlc[b], mg[b]
                S.dma(hg[0:c], HG[row:row + c, :].rearrange("p (a b) -> p a b", a=5), [], [hg])
                S.dma(ml[0:c], ML[row:row + c, :], [], [ml])
                def hgrn_chunk(c=c, hg=hg, mgo=mgo):
                    S.mm(pA[0:c, :], U32[0:c, 0:c], hg[0:c, 1, :], [U32, hg], [pA])
                    S.af(eb[0:c], pA[0:c, :], AF.Exp, [pA], [eb])
                    S.af(enb[0:c], pA[0:c, :], AF.Exp, [pA], [enb], scale=-1.0)
                    yield
                    S.tt(qt[0:c], hg[0:c, 0, :], eb[0:c], ALU.mult, [hg, eb], [qt])
                    S.tt(kt[0:c], hg[0:c, 2, :], enb[0:c], ALU.mult, [hg, enb], [kt])
                    S.cp(vb[0:c], hg[0:c, 3, :], [hg], [vb], e="act")
                    yield
                    for hh in range(4):
                        S.tr(pB[:, hh, 0:c], qt[0:c, hh * 128:(hh + 1) * 128], Ibf[0:c, 0:c], [qt, Ibf], [pB])
                        S.tr(pB[:, 4 + hh, 0:c], kt[0:c, hh * 128:(hh + 1) * 128], Ibf[0:c, 0:c], [kt, Ibf], [pB])
                    S.cp(qkT[:, :, 0:c], pB[:, :, 0:c], [pB], [qkT])
                    yield
                    for hh in range(4):
                        S.mm(pC[:, 2 * hh:2 * hh + 2], eb[0:c, hh * 128:(hh + 1) * 128], I32[0:c, c - 2:c], [eb, I32], [pC])
                    S.cp(ebl.ap().rearrange("p h t -> p (h t)"), pC[:, 0:8], [pC], [ebl], e="act")
                    for hh in range(4):
                        S.mm(pC[0:c, 64 + hh * 64:64 + hh * 64 + c], qkT[:, 4 + hh, 0:c], qkT[:, hh, 0:c], [qkT], [pC])
                    S.tt(scm[0:c, :, 0:c], pC[0:c, 64:320].rearrange("p (h t) -> p h t", h=4)[:, :, 0:c],
                         bc(U32[0:c, 0:c], 1, [c, 4, c]), ALU.mult, [pC, U32], [scm])
                    yield
                    for hh in range(4):
                        S.mm(pA[0:c, hh * 128:(hh + 1) * 128], scm[0:c, hh, 0:c], vb[0:c, hh * 128:(hh + 1) * 128], [scm, vb], [pA], start=True, stop=False)
                        S.mm(pA[0:c, hh * 128:(hh + 1) * 128], qkT[:, hh, 0:c], Shb[:, hh, :], [qkT, Shb], [pA], start=False, stop=True)
                    for hh in range(4):
                        S.mm(pD[:, hh * 128:(hh + 1) * 128], kt[0:c, hh * 128:(hh + 1) * 128], vb[0:c, hh * 128:(hh + 1) * 128], [kt, vb], [pD])
                    yield
                    S.tt(stmp.ap().rearrange("p h v -> p (h v)"), Sh.ap().rearrange("p h v -> p (h v)"), pD.ap(), ALU.add, [Sh, pD], [stmp])
                    S.tt(Sh.ap(), stmp.ap(), bc(ebl[:, :, 1], 2, [128, 4, 128]), ALU.mult, [stmp, ebl], [Sh])
                    S.cp(Shb.ap(), Sh.ap(), [Sh], [Shb], e="act")
                    yield
                    S.af(osq[0:c], pA[0:c, :], AF.Square, [pA], [osq])
                    S.red(oss[0:c], osq[0:c].rearrange("p (h v) -> p h v", h=4), ALU.add, [osq], [oss])
                    S.af(oss[0:c], oss[0:c], AF.Sqrt, [oss], [oss], scale=1.0 / 128.0, bias=EPS)
                    S.recip(orstd[0:c], oss[0:c], [oss], [orstd])
                    S.tt(on[0:c], pA[0:c, :].rearrange("p (h v) -> p h v", h=4), bc(orstd[0:c], 2, [c, 4, 128]), ALU.mult, [pA, orstd], [on])
                    S.tt(mgo[0:c, 0:512], on[0:c].rearrange("p h v -> p (h v)"), hg[0:c, 4, :], ALU.mult, [on, hg], [mgo])

                def mlstm_chunk(c=c, ml=ml, mgo=mgo):
                    S.mm(pE[0:c, 0:4], U32[0:c, 0:c], ml[0:c, 1540:1544], [U32, ml], [pE])
                    S.mm(pE[0:64, 8:12], ones32[0:c, 0:64], ml[0:c, 1540:1544], [ones32, ml], [pE])
                    S.mm(pE[0:4, 16:18], ml[0:c, 1540:1544], ones_c2[0:c, :], [ml, ones_c2], [pE])
                    S.cp(F_t[0:c], pE[0:c, 0:4], [pE], [F_t])
                    S.cp(Flb.ap(), pE[0:64, 8:12], [pE], [Flb])
                    S.cp(fl_row.ap(), pE[0:4, 16:18], [pE], [fl_row])
                    yield
                    S.tt(a_t[0:c], ml[0:c, 1536:1540], F_t[0:c], ALU.subtract, [ml, F_t], [a_t])
                    S.tr(pE[0:4, 32:32 + c], a_t[0:c, :], I32[0:c, 0:c], [a_t, I32], [pE])
                    S.cp(aT[:, 0:c], pE[0:4, 32:32 + c], [pE], [aT])
                    S.op("dve", lambda: nc.vector.tensor_tensor_scan(out=M_row[:, 0:c], data0=aT[:, 0:c], data1=aT[:, 0:c],
                                                                     initial=m_row.ap(), op0=ALU.max, op1=ALU.max),
                         [aT, m_row], [M_row])
                    yield
                    S.tt(Mblk[:, :, 0:c], bc(M_row[:, 0:c], 1, [4, 4, c]), bm4[:, :, 0:c], ALU.mult, [M_row, bm4], [Mblk])
                    S.tt(Mlb.ap(), M_row[:, c - 1:c].broadcast_to([4, 4]), bm4[:, :, 0], ALU.mult, [M_row, bm4], [Mlb])
                    S.mm(pE[0:64, 128:128 + 4 * c], ones32[0:4, 0:64], Mblk[:, :, 0:c], [ones32, Mblk], [pE])
                    S.mm(pE[0:64, 400:404], ones32[0:4, 0:64], Mlb.ap(), [ones32, Mlb], [pE])
                    S.tr(pE[0:c, 416:420], M_row[:, 0:c], I32[0:4, 0:4], [M_row, I32], [pE])
                    S.cp(M_t[0:c], pE[0:c, 416:420], [pE], [M_t])
                    S.cp(Mlbc.ap(), pE[0:64, 400:404], [pE], [Mlbc])
                    yield
                    Mb3 = pE[0:64, 128:128 + 4 * c].rearrange("p (h t) -> p h t", h=4)
                    for hh in range(4):
                        S.af(DT[0:c, hh, 0:c], Mb3[0:c, hh, 0:c], AF.Exp, [pE, a_t], [DT], scale=-1.0, bias=a_t[0:c, hh:hh + 1])
                        S.af(Wbc[:, hh, 0:c], Mb3[:, hh, 0:c], AF.Exp, [pE, m_bc], [Wbc], scale=-1.0, bias=m_bc[:, hh:hh + 1])
                    S.tt(DT[0:c, :, 0:c], DT[0:c, :, 0:c], bc(U32[0:c, 0:c], 1, [c, 4, c]), ALU.mult, [DT, U32], [DT])
                    yield
                    S.cp(mq[0:c], ml[0:c, 0:512], [ml], [mq], e="act")
                    for hh in range(4):
                        S.tr(pB2[0:64, hh, 0:c], mq[0:c, hh * 64:(hh + 1) * 64], Ibf[0:c, 0:c], [mq, Ibf], [pB2])
                        S.tr(pB2[0:64, 4 + hh, 0:c], mq[0:c, 256 + hh * 64:256 + (hh + 1) * 64], Ibf[0:c, 0:c], [mq, Ibf], [pB2])
                    S.cp(mqkT[:, :, 0:c], pB2[0:64, :, 0:c], [pB2], [mqkT])
                    yield
                    S.tt(qTt[:, :, 0:c], mqkT[:, 0:4, 0:c], Wbc[:, :, 0:c], ALU.mult, [mqkT, Wbc], [qTt])
                    for hh in range(4):
                        S.mm(pF[0:c, hh, 192:192 + c], mqkT[:, 4 + hh, 0:c], mqkT[:, hh, 0:c], [mqkT], [pF])
                    S.tt(qkD[0:c, :, 0:c], pF[0:c, :, 192:192 + c], DT[0:c, :, 0:c], ALU.mult, [pF, DT], [qkD])
                    S.cp(vext[0:c, :, 0:128], ml[0:c, 512:1024].rearrange("p (h v) -> p h v", h=4), [ml], [vext], e="act")
                    S.cp(vext[0:c, :, 128:129], bc(ones32[0:c, 0:4], 2, [c, 4, 1]), [ones32], [vext])
                    yield
                    for hh in range(4):
                        S.mm(pF[0:c, hh, 0:129], qkD[0:c, hh, 0:c], vext[0:c, hh, :], [qkD, vext], [pF], start=True, stop=False)
                        S.mm(pF[0:c, hh, 0:129], qTt[:, hh, 0:c], Cnb[:, hh, :], [qTt, Cnb], [pF], start=False, stop=True)
                    yield
                    S.cp(den[0:c], pF[0:c, :, 128], [pF], [den])
                    S.ts(nden[0:c], den[0:c], -1.0, ALU.mult, [den], [nden])
                    S.tt(den[0:c], den[0:c], nden[0:c], ALU.max, [den, nden], [den])
                    S.tt(emt[0:c], F_t[0:c], M_t[0:c], ALU.add, [F_t, M_t], [emt])
                    S.af(emt[0:c], emt[0:c], AF.Exp, [emt], [emt], scale=-1.0)
                    S.tt(den[0:c], den[0:c], emt[0:c], ALU.max, [den, emt], [den])
                    S.recip(rden[0:c], den[0:c], [den], [rden])
                    S.tt(hn[0:c], pF[0:c, :, 0:128], bc(rden[0:c], 2, [c, 4, 128]), ALU.mult, [pF, rden], [hn])
                    yield
                    S.af(osq2[0:c], hn[0:c].rearrange("p h v -> p (h v)"), AF.Square, [hn], [osq2])
                    S.red(oss2[0:c], osq2[0:c].rearrange("p (h v) -> p h v", h=4), ALU.add, [osq2], [oss2])
                    S.af(oss2[0:c], oss2[0:c], AF.Sqrt, [oss2], [oss2], scale=1.0 / 128.0, bias=EPS)
                    S.recip(orstd2[0:c], oss2[0:c], [oss2], [orstd2])
                    S.tt(hn[0:c], hn[0:c], bc(orstd2[0:c], 2, [c, 4, 128]), ALU.mult, [hn, orstd2], [hn])
                    S.tt(mgo[0:c, 512:1024], hn[0:c].rearrange("p h v -> p (h v)"), ml[0:c, 1024:1536], ALU.mult, [hn, ml], [mgo])
                    yield
                    S.tt(w_s[0:c], a_t[0:c], Mlbc[0:c], ALU.subtract, [a_t, Mlbc], [w_s])
                    S.af(w_s[0:c], w_s[0:c], AF.Exp, [w_s], [w_s])
                    S.tt(dec.ap(), m_bc.ap(), Mlbc.ap(), ALU.subtract, [m_bc, Mlbc], [dec])
                    S.af(dec.ap(), dec.ap(), AF.Exp, [dec], [dec])
                    S.tt(khat[0:c], ml[0:c, 256:512].rearrange("p (h d) -> p h d", h=4), bc(w_s[0:c], 2, [c, 4, 64]), ALU.mult, [ml, w_s], [khat])
                    for hh in range(4):
                        S.mm(pF[:, hh, 0:129], khat[0:c, hh, :], vext[0:c, hh, :], [khat, vext], [pF])
                    S.tt(Cn.ap(), Cn.ap(), bc(dec.ap(), 2, [64, 4, 129]), ALU.mult, [Cn, dec], [Cn])
                    S.tt(Cn.ap(), Cn.ap(), pF[:, :, 0:129], ALU.add, [Cn, pF], [Cn])
                    S.cp(Cnb.ap(), Cn.ap(), [Cn], [Cnb], e="act")
                    S.tt(m_bc.ap(), Mlbc.ap(), Flb.ap(), ALU.add, [Mlbc, Flb], [m_bc])
                    S.tt(m_row.ap(), M_row[:, c - 1:c], fl_row[:, 0:1], ALU.add, [M_row, fl_row], [m_row])

                interleave([hgrn_chunk(), mlstm_chunk()])
                S.dma(MG0[row:row + c, :], mgo[0:c], [mgo], [], q="pool")
                row += c
            S.dma(o_hS[si].rearrange("h k v -> k h v"), Sh, [Sh], [], q="pool")
            S.dma(o_mC[si].rearrange("h k v -> k h v"), Cn[:, :, 0:128], [Cn], [], q="pool")
            S.dma(o_mn[si].rearrange("h (k o) -> k h o", o=1), Cn[:, :, 128:129], [Cn], [], q="pool", allow_slow_non_contiguous=True)
            S.dma(o_mm[si].rearrange("(h o) -> h o", o=1), m_row, [m_row], [], q="pool")
        S.pop()

    def phase_d3ab(layer, MG, KM, w_out_d, xsrc):
        S.push()
        nkm = KM // 128
        Wo = S.sb("Wo", [128, nkm, D], BF16)
        W1 = S.sb("W1", [128, 8, 2 * FF], BF16)
        wn = S.sb("wn", [128, D], F32)
        junk = S.sb("junk", [128, D], F32)
        mgt = [S.sb("mgt%d" % i, [128, KM], BF16) for i in range(2)]
        mT = [S.sb("mT%d" % i, [128, nkm, 128], BF16) for i in range(2)]
        xt = [S.sb("xt%d" % i, [128, D], F32) for i in range(2)]
        x1 = [S.sb("x1%d" % i, [128, D], F32) for i in range(2)]
        ss = [S.sb("ss%d" % i, [128, 1], F32) for i in range(2)]
        rstd = [S.sb("rstd%d" % i, [128, 1], F32) for i in range(2)]
        h = [S.sb("h%d" % i, [128, D], BF16) for i in range(2)]
        hT = [S.sb("hT%d" % i, [128, 8, 128], BF16) for i in range(2)]
        sg = [S.sb("sg%d" % i, [128, 512], F32) for i in range(2)]
        act = [S.sb("act%d" % i, [128, FF], BF16) for i in range(2)]
        ptb = [S.ps("ptb%d" % i, [128, 8, 128], BF16) for i in range(2)]
        pms = [S.ps("pm%d" % i, [128, 512], F32) for i in range(6)]
        load_weight(S, Wo, w_out_d, KM)
        load_weight(S, W1, ffn_w_in[layer], D)
        load_bcast(S, wn, norm_ffn[layer:layer + 1, :], D)
        S.barrier()
        box = [0, 0]

        def stA1(i):
            pmi = box[0]
            b = i % 2
            r0 = i * 128
            S.dma(mgt[b], MG[r0:r0 + 128, :], [], [mgt[b]])
            S.dma(xt[b], xsrc[r0:r0 + 128, :], [], [xt[b]])
            transpose_tile(S, mgt[b], nkm, ptb[b], mT[b], Ibf)
            x1b, xb = x1[b], xt[b]
            blocks = []
            for cb in range(2):
                def epi(pm, cb=cb):
                    S.tt(x1b[:, cb * 512:(cb + 1) * 512], xb[:, cb * 512:(cb + 1) * 512], pm.ap(), ALU.add, [xb, pm], [x1b])
                blocks.append((cb * 512, 512, epi))
            box[0] = dense_blocks(mT[b], nkm, Wo, blocks, pms, pmi)
            S.dma(X1[r0:r0 + 128, :], x1b, [x1b], [], q="pool")
            rmsnorm_tile(S, x1b, wn, h[b], junk, ss[b], rstd[b])

        def stA2(i):
            b = i % 2
            transpose_tile(S, h[b], 8, ptb[b], hT[b], Ibf)

        def stB(i):
            pmi, sgi = box
            b = i % 2
            r0 = i * 128
            ab = act[b]
            for o in range(0, FF, 512):
                n = min(512, FF - o)
                pg = pms[pmi % 6]
                pu = pms[(pmi + 1) % 6]
                pmi += 2
                for k in range(8):
                    S.mm(pg[:, 0:n], hT[b][:, k, :], W1[:, k, o:o + n], [hT[b], W1], [pg], start=(k == 0), stop=(k == 7))
                for k in range(8):
                    S.mm(pu[:, 0:n], hT[b][:, k, :], W1[:, k, FF + o:FF + o + n], [hT[b], W1], [pu], start=(k == 0), stop=(k == 7))
                s_ = sg[sgi % 2]
                sgi += 1
                S.af(s_[:, 0:n], pg[:, 0:n], AF.Silu, [pg], [s_])
                S.tt(ab[:, o:o + n], s_[:, 0:n], pu[:, 0:n], ALU.mult, [s_, pu], [ab])
            box[0], box[1] = pmi, sgi
            S.dma(ACTF[r0:r0 + 128, :], ab, [ab], [], q="pool")

        run_pipelined(ntile, stA1, stB, stA2)
        S.pop()

    def phase_d3c(layer, xdst, final):
        S.push()
        nk = FF // 128
        W2 = S.sb("W2", [128, nk, D], BF16)
        wn = S.sb("wn", [128, D], F32)
        junk = S.sb("junk", [128, D], F32)
        at = [S.sb("at%d" % i, [128, FF], BF16) for i in range(2)]
        aT = [S.sb("aT%d" % i, [128, nk, 128], BF16) for i in range(2)]
        xt = [S.sb("xt%d" % i, [128, D], F32) for i in range(2)]
        x2 = [S.sb("x2%d" % i, [128, D], F32) for i in range(2)]
        yt = [S.sb("yt%d" % i, [128, D], F32) for i in range(2)]
        ss = [S.sb("ss%d" % i, [128, 1], F32) for i in range(2)]
        rstd = [S.sb("rstd%d" % i, [128, 1], F32) for i in range(2)]
        ptb = [S.ps("ptb%d" % i, [128, 8, 128], BF16) for i in range(2)]
        pms = [S.ps("pm%d" % i, [128, 512], F32) for i in range(4)]
        load_weight(S, W2, ffn_w_out[layer], FF)
        if final:
            load_bcast(S, wn, norm_final[0:1, :], D)
        S.barrier()
        box = [0]

        def stA1(i):
            b = i % 2
            r0 = i * 128
            S.dma(at[b], ACTF[r0:r0 + 128, :], [], [at[b]])
            S.dma(xt[b], X1[r0:r0 + 128, :], [], [xt[b]])

        def stA2(i):
            b = i % 2
            transpose_tile(S, at[b], nk, ptb[b], aT[b], Ibf)

        def stB(i):
            pmi = box[0]
            b = i % 2
            r0 = i * 128
            x2b, xb = x2[b], xt[b]
            blocks = []
            for cb in range(2):
                def epi(pm, cb=cb):
                    S.tt(x2b[:, cb * 512:(cb + 1) * 512], xb[:, cb * 512:(cb + 1) * 512], pm.ap(), ALU.add, [xb, pm], [x2b])
                blocks.append((cb * 512, 512, epi))
            box[0] = dense_blocks(aT[b], nk, W2, blocks, pms, pmi)
            if not final:
                S.dma(xdst[r0:r0 + 128, :], x2b, [x2b], [], q="pool")
            else:
                S.af(junk.ap(), x2b.ap(), AF.Square, [x2b], [junk, ss[b]], accum=ss[b].ap())
                S.af(ss[b].ap(), ss[b].ap(), AF.Sqrt, [ss[b]], [ss[b]], scale=1.0 / D, bias=EPS)
                S.recip(rstd[b].ap(), ss[b].ap(), [ss[b]], [rstd[b]])
                S.stt(yt[b].ap(), x2b.ap(), rstd[b].ap(), wn.ap(), ALU.mult, ALU.mult, [x2b, rstd[b], wn], [yt[b]])
                S.dma(xdst[r0:r0 + 128, :], yt[b], [yt[b]], [], q="pool")

        run_pipelined(ntile, stA1, stB, stA2)
        S.pop()

    def phase_d1_odd():
        S.push()
        W = S.sb("W", [128, 8, ODD_COLS], BF16)
        wn = S.sb("wn", [128, D], F32)
        gnw = S.sb("gnw", [128, 128], F32)
        negA = S.sb("negA", [128, 16], F32)
        dtb = S.sb("dtb", [128, 16], F32)
        junk = S.sb("junk", [128, D], F32)
        tmp = [S.sb("tmp%d" % i, [128, 512], F32) for i in range(2)]
        t16 = S.sb("t16", [128, 16], F32)
        xt = [S.sb("xt%d" % i, [128, D], F32) for i in range(2)]
        ss = [S.sb("ss%d" % i, [128, 1], F32) for i in range(2)]
        rstd = [S.sb("rstd%d" % i, [128, 1], F32) for i in range(2)]
        h = [S.sb("h%d" % i, [128, D], BF16) for i in range(2)]
        hT = [S.sb("hT%d" % i, [128, 8, 128], BF16) for i in range(2)]
        pq = [S.sb("pq%d" % i, [128, 4096], F32) for i in range(2)]
        gzt = [S.sb("gzt%d" % i, [128, 2, 1040], F32) for i in range(2)]
        ptb = [S.ps("ptb%d" % i, [128, 8, 128], BF16) for i in range(2)]
        pms = [S.ps("pm%d" % i, [128, 512], F32) for i in range(6)]
        load_weight(S, W, odd_w_in, D)
        load_bcast(S, wn, norm_mix[1:2, :], D)
        load_bcast(S, gnw, gdn_norm[0:1, :], 128)
        load_bcast(S, negA, gdn_a_log[0:1, :], 16)
        load_bcast(S, dtb, gdn_dt_bias[0:1, :], 16)
        S.af(negA.ap(), negA.ap(), AF.Exp, [negA], [negA])
        S.ts(negA.ap(), negA.ap(), -1.0, ALU.mult, [negA], [negA])
        S.barrier()
        box = [0]

        def stA1(i):
            b = i % 2
            S.dma(xt[b], X2[i * 128:i * 128 + 128, :], [], [xt[b]])
            rmsnorm_tile(S, xt[b], wn, h[b], junk, ss[b], rstd[b])

        def stA2(i):
            b = i % 2
            transpose_tile(S, h[b], 8, ptb[b], hT[b], Ibf)

        def stB(i):
            pmi = box[0]
            b = i % 2
            r0 = i * 128
            pqb, gz = pq[b], gzt[b]
            blocks = []
            for cb in range(8):
                def epi(pm, cb=cb):
                    S.cp(pqb[:, cb * 512:(cb + 1) * 512], pm.ap(), [pm], [pqb], e=("act" if cb % 2 else "dve"))
                blocks.append((cb * 512, 512, epi))
            for cb in range(4):
                def epi(pm, cb=cb):
                    t_ = tmp[cb % 2]
                    S.af(t_.ap(), pm.ap(), AF.Silu, [pm], [t_])
                    S.tt(gz[:, cb // 2, (cb % 2) * 512:(cb % 2) * 512 + 512].rearrange("p (h v) -> p h v", h=4), t_.ap().rearrange("p (h v) -> p h v", h=4),
                         bc(gnw.ap(), 1, [128, 4, 128]), ALU.mult, [t_, gnw], [gz])
                blocks.append((4096 + cb * 512, 512, epi))

            def epi_ba(pm):
                S.af(gz[:, :, 1024:1032], pm[:, 0:16].rearrange("p (g h) -> p g h", g=2), AF.Sigmoid, [pm], [gz])
                S.tt(t16.ap(), pm[:, 16:32], dtb.ap(), ALU.add, [pm, dtb], [t16])
                S.af(t16.ap(), t16.ap(), AF.Exp, [t16], [t16])
                S.af(t16.ap(), t16.ap(), AF.Ln, [t16], [t16], bias=1.0)
                S.tt(gz[:, :, 1032:1040], t16.ap().rearrange("p (g h) -> p g h", g=2), negA.ap().rearrange("p (g h) -> p g h", g=2), ALU.mult, [t16, negA], [gz])
            blocks.append((6144, 32, epi_ba))
            box[0] = dense_blocks(hT[b], 8, W, blocks, pms, pmi)
            S.dma(PQ[3 + r0:3 + r0 + 128, :], pqb, [pqb], [], q="pool")
            for g in range(2):
                S.dma(GD[r0:r0 + 128, g * 3088 + 2048:g * 3088 + 3088], gz[:, g, :], [gz], [], q="pool")

        run_pipelined(ntile, stA1, stB, stA2)
        S.pop()
        S.push()
        zr = S.sb("zr", [3, 4096], F32)
        cs = [S.sb("cs%d" % i, [3, 4096], F32) for i in range(NSEQ)]
        S.memset(zr, 0.0)
        S.dma(PQ[0:3, :], zr, [zr], [])
        for si, sq in enumerate(seqs):
            st = sq["start"]
            if sq["init"] is None:
                S.dma(PQ[3 + st - 3:3 + st, :], zr, [zr], [])
            else:
                S.dma(cs[si], st_gc[sq["init"]], [], [cs[si]])
                S.dma(PQ[3 + st - 3:3 + st, :], cs[si], [cs[si]], [])
        for si, sq in enumerate(seqs):
            e = sq["start"] + sq["len"]
            t_ = S.sb("co%d" % si, [3, 4096], F32)
            S.dma(t_, PQ[3 + e - 3:3 + e, :], [], [t_])
            S.dma(o_gc[si], t_, [t_], [], q="pool")
        S.pop()

    def phase_c1():
        S.push()
        cw = [S.sb("cw%d" % j, [128, 4096], F32) for j in range(4)]
        pieces = [(0, 1024, "dve", 8, True), (1024, 1024, "dve", 8, False), (2048, 512, "dve", 0, False), (2560, 1536, "pool", 0, False)]
        bufA = [[S.sb("xa%d_%d" % (j, i), [128, 1024], F32) for j in range(4)] for i in range(3)]
        bufB = [[S.sb("xb%d_%d" % (j, i), [128, 512], F32) for j in range(4)] for i in range(2)]
        bufC = [[S.sb("xc%d_%d" % (j, i), [128, 1536], F32) for j in range(4)] for i in range(2)]
        t2 = S.sb("t2", [128, 1024], F32)
        ssq = S.sb("ssq", [128, 8], F32)
        rs = S.sb("rs", [128, 8], F32)
        for j in range(4):
            load_bcast(S, cw[j], odd_conv_w[j:j + 1, :], 4096)
        S.barrier()
        ia = 0
        for i in range(ntile):
            r0 = i * 128
            for pi, (c0, ncol, e, nl2, qs) in enumerate(pieces):
                if pi < 2:
                    xw = bufA[ia % 3]
                    ia += 1
                elif pi == 2:
                    xw = bufB[i % 2]
                else:
                    xw = bufC[i % 2]
                sl = slice(c0, c0 + ncol)
                for j in range(4):
                    S.dma(xw[j], PQ[r0 + j:r0 + j + 128, sl], [], [xw[j]])
                a = xw[0]
                S.tt(a.ap(), a.ap(), cw[0][:, sl], ALU.mult, [a, cw[0]], [a], e=e)
                for j in range(1, 4):
                    S.tt(xw[j].ap(), xw[j].ap(), cw[j][:, sl], ALU.mult, [xw[j], cw[j]], [xw[j]], e=e)
                    S.tt(a.ap(), a.ap(), xw[j].ap(), ALU.add, [a, xw[j]], [a], e=e)
                S.af(a.ap(), a.ap(), AF.Silu, [a], [a])
                if nl2:
                    S.af(t2.ap(), a.ap(), AF.Square, [a], [t2])
                    S.red(ssq.ap(), t2.ap().rearrange("p (h d) -> p h d", h=8), ALU.add, [t2], [ssq])
                    S.af(ssq.ap(), ssq.ap(), AF.Sqrt, [ssq], [ssq], bias=EPS)
                    S.recip(rs.ap(), ssq.ap(), [ssq], [rs])
                    if qs:
                        S.ts(rs.ap(), rs.ap(), 128.0 ** -0.5, ALU.mult, [rs], [rs])
                    S.tt(a.ap().rearrange("p (h d) -> p h d", h=8), a.ap().rearrange("p (h d) -> p h d", h=8),
                         bc(rs.ap(), 2, [128, 8, 128]), ALU.mult, [a, rs], [a])
                if pi < 2:
                    off = 0 if pi == 0 else 512
                    for g in range(2):
                        S.dma(GD[r0:r0 + 128, g * 3088 + off:g * 3088 + off + 512], a[:, 512 * g:512 * g + 512], [a], [], q="pool")
                elif pi == 2:
                    S.dma(GD[r0:r0 + 128, 1024:1536], a, [a], [], q="pool")
                else:
                    S.dma(GD[r0:r0 + 128, 1536:2048], a[:, 0:512], [a], [], q="pool")
                    S.dma(GD[r0:r0 + 128, 3088 + 1024:3088 + 2048], a[:, 512:1536], [a], [], q="pool")
        S.pop()

    def phase_r1():
        import os as _os
        S.push()
        Ublk = S.sb("Ublk", [128, 128], F32)
        SLblk = S.sb("SLblk", [128, 128], F32)
        rep = S.sb("rep", [128, 3, 64], F32)
        Ibf2 = S.sb("Ibf2", [128, 64], BF16)
        Sg = S.sb("Sg", [128, 16, 128], F32)
        Sgb = [S.sb("Sgb%d" % i, [128, 16, 128], BF16) for i in range(2)]
        gd = [S.sb("gd%d" % i, [128, 3088], F32) for i in range(3)]
        mg = [S.sb("mg%d" % i, [128, 1024], BF16) for i in range(2)]
        eGG = S.sb("eGG", [128, 16], F32)
        gg2 = S.sb("gg2", [128, 16], F32)
        eGlast = [S.sb("eGlast%d" % i, [128, 16], F32) for i in range(2)]
        gU = S.sb("gU", [128, 8, 64], F32)
        E = S.sb("E", [128, 8, 64], F32)
        qkb = S.sb("qkb", [128, 1024], BF16)
        qkT = S.sb("qkT", [128, 8, 2, 64], BF16)
        T1 = S.sb("T1", [128, 8, 64], F32)
        X = [S.sb("X%d" % i, [128, 8, 64], BF16) for i in range(2)]
        Y = [S.sb("Y%d" % i, [128, 8, 64], BF16) for i in range(2)]
        Q = [S.sb("Q%d" % i, [128, 8, 64], BF16) for i in range(2)]
        TT = [S.sb("TT%d" % i, [128, 8, 64], BF16) for i in range(2)]
        attnT = [S.sb("attnT%d" % i, [128, 8, 64], BF16) for i in range(2)]
        nbeta = S.sb("nbeta", [128, 8], F32)
        IeG2 = S.sb("IeG2", [128, 16, 64], F32)
        kTt = [S.sb("kTt%d" % i, [128, 16, 64], BF16) for i in range(2)]
        qTt = [S.sb("qTt%d" % i, [128, 16, 64], BF16) for i in range(2)]
        khat = [S.sb("khat%d" % i, [128, 8, 128], BF16) for i in range(2)]
        yv = S.sb("yv", [128, 8, 128], BF16)
        vnew = S.sb("vnew", [128, 8, 128], BF16)
        osq = S.sb("osq", [128, 8, 128], F32)
        oss = S.sb("oss", [128, 8], F32)
        orstd = S.sb("orstd", [128, 8], F32)
        on = S.sb("on", [128, 8, 128], F32)
        pB = [S.ps("pB%d" % i, [128, 8, 128], F32) for i in range(2)]
        pA = [S.ps("pAa%d" % i, [128, 512], F32) for i in range(4)]
        S.dma(Ublk, c_Ublk, [], [Ublk])
        S.dma(SLblk, c_SLblk, [], [SLblk])
        S.dma(rep, c_rep, [], [rep])
        S.cp(Ibf2.ap(), rep[:, 2, :], [rep], [Ibf2])
        for t_ in gd + [gg2, eGG, gU, E, T1, nbeta, qkb, yv, vnew, osq, oss, orstd, on] + X + Y + Q + TT + attnT + khat + mg + eGlast:
            S.memset(t_, 0.0)
        for t_ in [IeG2] + kTt + qTt + [qkT]:
            S.memset(t_, 0.0, e="pool")
        for t_ in pB + pA:
            S.memset(t_, 0.0)
        U2, SU2, I2 = rep[:, 0, :], rep[:, 1, :], rep[:, 2, :]
        zero_gap_rows(MG1, 2048, BF16)

        chunks = []
        for si, sq in enumerate(seqs):
            row = sq["start"]
            for ci, c in enumerate(sq["chunks"]):
                chunks.append((si, ci, c, row, ci == len(sq["chunks"]) - 1))
                row += c

        def part_a(n):
            si, ci, c, row, last = chunks[n]
            b = n % 2
            g_ = gd[n % 3]
            for g in range(2):
                S.dma(g_[64 * g:64 * g + c, :], GD[row:row + c, 3088 * g:3088 * g + 3088], [], [g_])
            yield
            beta = g_[:, 3072:3080]
            gg = g_[:, 3080:3088]
            kn4 = g_[:, 512:1024].rearrange("p (j d) -> p j d", j=4)
            a0, a1, a2, a3 = pA
            S.mm(a3[:, 0:8], Ublk.ap(), gg, [Ublk, g_], [a3])
            S.mm(a3[:, 8:16], SLblk.ap(), gg, [SLblk, g_], [a3])
            for g in range(2):
                S.cp(gg2[64 * g:64 * g + 64, 8 * g:8 * g + 8], gg[64 * g:64 * g + 64, :], [g_], [gg2], e="pool")
            S.mm(a3[:, 16:32], ones32.ap(), gg2.ap(), [ones32, gg2], [a3])
            S.tt(gU.ap(), bc(U2, 1, [128, 8, 64]), bc(gg, 2, [128, 8, 64]), ALU.mult, [rep, g_], [gU], e="pool")
            S.ts(nbeta.ap(), beta, -1.0, ALU.mult, [g_], [nbeta], e="pool")
            S.af(eGG.ap(), a3[:, 0:16], AF.Exp, [a3], [eGG])
            S.af(eGlast[b].ap(), a3[:, 16:32], AF.Exp, [a3], [eGlast[b]])
            S.cp(qkb.ap(), g_[:, 0:1024], [g_], [qkb], e="act")
            yield
            if _os.environ.get("R1_NODMT", "") != "1":
                S.mm(a0[:, 0:8 * c], SLblk.ap(), gU[:, :, 0:c], [SLblk, gU], [a0])
            pTv = a1.ap().bitcast(BF16).rearrange("p (j a t) -> p j a t", j=8, a=2)
            for g in range(2):
                ps_ = slice(64 * g, 64 * g + c)
                for jl in range(4):
                    j = 4 * g + jl
                    if g == 0:
                        S.tr(pTv[:, j, 0, 0:c], qkb[ps_, 512 + jl * 128:512 + (jl + 1) * 128], Ibf2[ps_, 0:c], [qkb, Ibf2], [a1])
                        S.tr(pTv[:, j, 1, 0:c], qkb[ps_, jl * 128:(jl + 1) * 128], Ibf2[ps_, 0:c], [qkb, Ibf2], [a1])
                    else:
                        for dh in range(2):
                            do = slice(64 * dh, 64 * dh + 64)
                            S.tr(pTv[do, j, 0, 0:c], qkb[ps_, 512 + jl * 128 + 64 * dh:512 + jl * 128 + 64 * dh + 64], Ibf2[ps_, 0:c], [qkb, Ibf2], [a1])
                            S.tr(pTv[do, j, 1, 0:c], qkb[ps_, jl * 128 + 64 * dh:jl * 128 + 64 * dh + 64], Ibf2[ps_, 0:c], [qkb, Ibf2], [a1])
            S.af(E[:, :, 0:c], a0[:, 0:8 * c].rearrange("p (h t) -> p h t", h=8), AF.Exp, [a0], [E])
            S.cp(qkT[:, :, :, 0:c], pTv[:, :, :, 0:c], [a1], [qkT])
            yield
            for g in range(2):
                ps_ = slice(64 * g, 64 * g + c)
                for jl in range(4):
                    j = 4 * g + jl
                    S.mm(a2[ps_, jl * 128:jl * 128 + 2 * c], qkT[:, j, 0, 0:c], qkT[:, j, :, 0:c], [qkT], [a2])
            S.tt(T1[:, :, 0:c], E[:, :, 0:c], bc(SU2[:, 0:c], 1, [128, 8, c]), ALU.mult, [E, rep], [T1], e="pool")
            S.tt(E[:, :, 0:c], E[:, :, 0:c], bc(U2[:, 0:c], 1, [128, 8, c]), ALU.mult, [E, rep], [E], e="pool")
            for g in range(2):
                S.tt(IeG2[64 * g:64 * g + 64, 8 * g:8 * g + 8, 0:c], bc(I2[64 * g:64 * g + 64, 0:c], 1, [64, 8, c]),
                     bc(eGG[64 * g:64 * g + 64, 0:8], 2, [64, 8, c]), ALU.mult, [rep, eGG], [IeG2], e="pool")
            yield
            pv = a2[:, :].rearrange("p (j x) -> p j x", j=4)[:, :, 0:2 * c].rearrange("p j (a t) -> p j a t", a=2)
            for r_ in range(2):
                hs = slice(r_, 8, 2)
                S.tt(T1[:, hs, 0:c], T1[:, hs, 0:c], pv[:, :, 0, :], ALU.mult, [T1, a2], [T1])
                S.tt(attnT[b][:, hs, 0:c], E[:, hs, 0:c], pv[:, :, 1, :], ALU.mult, [E, a2], [attnT[b]])
            X0, Y0, Q0 = X[0], Y[0], Q[0]
            S.tt(X0[:, :, 0:c], T1[:, :, 0:c], bc(nbeta.ap(), 2, [128, 8, c]), ALU.mult, [T1, nbeta], [X0])
            for hf in range(2):
                pp = pA[hf]
                S.mm(pp[:, 0:8 * c], ones32.ap(), IeG2[:, 8 * hf:8 * hf + 8, 0:c], [ones32, IeG2], [pp])
            yield
            pYv = a3.ap().bitcast(BF16)
            for g in range(2):
                ps_ = slice(64 * g, 64 * g + c)
                for hl in range(8):
                    S.tr(pYv[ps_, hl * c:(hl + 1) * c], X0[ps_, hl, 0:c], Ibf2[ps_, 0:c], [X0, Ibf2], [a3])
            S.cp(Y0[:, :, 0:c], pYv[:, 0:8 * c].rearrange("p (h t) -> p h t", h=8), [a3], [Y0], e="act")
            S.tt(Q0[:, :, 0:c], X0[:, :, 0:c], bc(I2[:, 0:c], 1, [128, 8, c]), ALU.add, [X0, rep], [Q0])
            for hf in range(2):
                pp = pA[hf]
                ev = pp[:, 0:8 * c].rearrange("p (h t) -> p h t", h=8)
                for r_ in range(2):
                    hsl = slice(8 * hf + r_, 8 * hf + 8, 2)
                    jsl = slice(4 * hf, 4 * hf + 4)
                    S.tt(kTt[b][:, hsl, 0:c], ev[:, r_:8:2, :], qkT[:, jsl, 0, 0:c], ALU.mult, [pp, qkT], [kTt[b]])
                    S.tt(qTt[b][:, hsl, 0:c], ev[:, r_:8:2, :], qkT[:, jsl, 1, 0:c], ALU.mult, [pp, qkT], [qTt[b]])
            for r_ in range(2):
                S.tt(khat[b][:, r_:8:2, :], kn4, bc(eGG[:, 8 + r_:16:2], 2, [128, 4, 128]), ALU.mult, [g_, eGG], [khat[b]], e="pool")
            nq = 5 if c == 64 else 3
            for lv in range(nq):
                yield
                Xc, Yc, Qc = X[lv % 2], Y[lv % 2], Q[lv % 2]
                Xn, Yn, Qn = X[(lv + 1) % 2], Y[(lv + 1) % 2], Q[(lv + 1) % 2]
                py, px, pq = a2, a3, (a0 if lv % 2 == 0 else a1)
                for hl in range(8):
                    for g in range(2):
                        ps_ = slice(64 * g, 64 * g + c)
                        S.mm(py[ps_, hl * c:(hl + 1) * c], Xc[ps_, hl, 0:c], Yc[ps_, hl, 0:c], [Xc, Yc], [py])
                S.cp(Yn[:, :, 0:c], py[:, 0:8 * c].rearrange("p (h t) -> p h t", h=8), [py], [Yn], e="act")
                if lv < nq - 1:
                    for hl in range(8):
                        for g in range(2):
                            ps_ = slice(64 * g, 64 * g + c)
                            S.mm(px[ps_, hl * c:(hl + 1) * c], Yc[ps_, hl, 0:c], Xc[ps_, hl, 0:c], [Xc, Yc], [px])
                    S.cp(Xn[:, :, 0:c], px[:, 0:8 * c].rearrange("p (h t) -> p h t", h=8), [px], [Xn], e="dve")
                for hl in range(8):
                    for g in range(2):
                        ps_ = slice(64 * g, 64 * g + c)
                        S.mm(pq[ps_, hl * c:(hl + 1) * c], Yn[ps_, hl, 0:c], Qc[ps_, hl, 0:c], [Yn, Qc], [pq])
                dst = TT[b] if lv == nq - 1 else Qn
                S.tt(dst[:, :, 0:c], Qc[:, :, 0:c], pq[:, 0:8 * c].rearrange("p (h t) -> p h t", h=8), ALU.add, [Qc, pq], [dst])

        sgi = [0]

        def part_b(n):
            si, ci, c, row, last = chunks[n]
            sq = seqs[si]
            b = n % 2
            g_, mgo = gd[n % 3], mg[b]
            if ci == 0:
                if sq["init"] is None:
                    S.memset(Sg, 0.0)
                else:
                    S.dma(Sg, st_gS[sq["init"]].rearrange("h k v -> k h v"), [], [Sg])
                S.cp(Sgb[sgi[0] % 2].ap(), Sg.ap(), [Sg], [Sgb[sgi[0] % 2]], e="act")
            Sb = Sgb[sgi[0] % 2]
            Sbn = Sgb[(sgi[0] + 1) % 2]
            sgi[0] += 1
            v3 = g_[:, 1024:2048].rearrange("p (h v) -> p h v", h=8)
            gz = g_[:, 2048:3072]
            beta = g_[:, 3072:3080]
            for hl in range(8):
                for g in range(2):
                    ps_ = slice(64 * g, 64 * g + c)
                    S.mm(pB[0][ps_, hl, :], kTt[b][:, 8 * g + hl, 0:c], Sb[:, 8 * g + hl, :], [kTt[b], Sb], [pB[0]])
            S.tt(yv.ap(), v3, pB[0].ap(), ALU.subtract, [g_, pB[0]], [yv])
            yield
            for hl in range(8):
                for g in range(2):
                    ps_ = slice(64 * g, 64 * g + c)
                    S.mm(pB[1][ps_, hl, :], TT[b][ps_, hl, 0:c], yv[ps_, hl, :], [TT[b], yv], [pB[1]])
            S.tt(vnew.ap(), pB[1].ap(), bc(beta, 2, [128, 8, 128]), ALU.mult, [pB[1], g_], [vnew])
            yield
            for g in range(2):
                ps_ = slice(64 * g, 64 * g + c)
                for hl in range(8):
                    S.mm(pB[g][:, hl, :], khat[b][ps_, hl, :], vnew[ps_, hl, :], [khat[b], vnew], [pB[g]])
                for hl in range(8):
                    hh = 8 * g + hl
                    S.stt(Sg[:, hh, :], Sg[:, hh, :], eGlast[b][:, hh:hh + 1], pB[g][:, hl, :], ALU.mult, ALU.add, [Sg, eGlast[b], pB[g]], [Sg])
            if not last:
                S.cp(Sbn.ap(), Sg.ap(), [Sg], [Sbn], e="act")
            yield
            for hl in range(8):
                for g in range(2):
                    ps_ = slice(64 * g, 64 * g + c)
                    S.mm(pB[0][ps_, hl, :], qTt[b][:, 8 * g + hl, 0:c], Sb[:, 8 * g + hl, :], [qTt[b], Sb], [pB[0]], start=True, stop=False)
                    S.mm(pB[0][ps_, hl, :], attnT[b][ps_, hl, 0:c], vnew[ps_, hl, :], [attnT[b], vnew], [pB[0]], start=False, stop=True)
            S.af(osq.ap(), pB[0].ap(), AF.Square, [pB[0]], [osq])
            yield
            S.red(oss.ap(), osq.ap(), ALU.add, [osq], [oss])
            S.af(oss.ap(), oss.ap(), AF.Sqrt, [oss], [oss], scale=1.0 / 128.0, bias=EPS)
            S.recip(orstd.ap(), oss.ap(), [oss], [orstd])
            for hl in range(8):
                S.af(on[:, hl, :], pB[0][:, hl, :], AF.Copy, [pB[0], orstd], [on], scale=orstd[:, hl:hl + 1])
            yield
            S.tt(mgo.ap(), on.ap().rearrange("p h v -> p (h v)"), gz, ALU.mult, [on, g_], [mgo])
            for g in range(2):
                S.dma(MG1[row:row + c, 1024 * g:1024 * g + 1024], mgo[64 * g:64 * g + c, :], [mgo], [], q="pool")
            if last:
                S.dma(o_gS[si].rearrange("h k v -> k h v"), Sg, [Sg], [], q="pool")

        def interleave(gens):
            gens = [g for g in gens if g is not None]
            while gens:
                for g in list(gens):
                    try:
                        next(g)
                    except StopIteration:
                        gens.remove(g)

        _mode = _os.environ.get("R1_MODE", "")
        _lim = int(_os.environ.get("R1_LIM", "99"))

        def lim(gen):
            for i, _ in enumerate(gen):
                if i + 1 >= _lim:
                    break
                yield

        if _mode == "":
            interleave([part_a(0)])
        for n in range(len(chunks)):
            if _mode == "":
                interleave([part_b(n), part_a(n + 1) if n + 1 < len(chunks) else None])
            elif _mode == "a_only":
                interleave([lim(part_a(n))])
        S.pop()

    phase_d1_even()
    phase_r0()
    phase_d3ab(0, MG0, 1024, even_w_out, xin)
    phase_d3c(0, X2, False)
    phase_d1_odd()
    phase_c1()
    phase_r1()
    phase_d3ab(1, MG1, 2048, odd_w_out, X2)
    phase_d3c(1, y, True)
    S.finish()
    return nc, seqs, NT, S


def make_consts():
    idx = np.arange(64)
    U = (idx[:, None] <= idx[None, :]).astype(np.float32)
    SU = (idx[:, None] < idx[None, :]).astype(np.float32)
    SL = (idx[:, None] > idx[None, :]).astype(np.float32)
    bm4 = np.zeros((4, 4, 64), np.float32)
    for r in range(4):
        bm4[r, r, :] = 1.0
    Z = np.zeros((64, 64), np.float32)
    Ublk = np.block([[U, Z], [Z, U]])
    SLblk = np.block([[SL, Z], [Z, SL]])
    I64 = np.eye(64, dtype=np.float32)
    rep = np.stack([np.concatenate([U, U], 0), np.concatenate([SU, SU], 0), np.concatenate([I64, I64], 0)], 1)
    return dict(c_ident=np.eye(128, dtype=np.float32), c_U=U, c_SU=SU, c_SL=SL, c_bm4=bm4,
                c_Ublk=np.ascontiguousarray(Ublk), c_SLblk=np.ascontiguousarray(SLblk), c_rep=np.ascontiguousarray(rep))


def make_in_maps(inp, T, NS, n_cores, seqs, NT):
    f = lambda a: np.ascontiguousarray(np.asarray(a, dtype=np.float32))
    xp = f(inp["x_prompt"])
    xs = f(inp["x_sample"])
    meta = f(inp["meta_tokens"])
    consts = make_consts()
    shared = dict(
        norm_mix=f(inp["norm_mix"]), norm_ffn=f(inp["norm_ffn"]), norm_final=f(inp["norm_final"]).reshape(1, D),
        even_w_in=f(inp["even_w_in"])[0], even_w_out=f(inp["even_w_out"])[0], lb_logits=f(inp["hgrn_lb_logits"]),
        hgrn_norm=f(inp["hgrn_norm"])[0:1], mlstm_bif=np.concatenate([f(inp["mlstm_b_i"])[0], f(inp["mlstm_b_f"])[0]]).reshape(1, 8),
        mlstm_norm=f(inp["mlstm_norm"])[0:1], odd_w_in=f(inp["odd_w_in"])[0], odd_conv_w=f(inp["odd_conv_w"])[0],
        gdn_a_log=f(inp["gdn_a_log"])[0:1], gdn_dt_bias=f(inp["gdn_dt_bias"])[0:1], gdn_norm=f(inp["gdn_norm"])[0:1],
        odd_w_out=f(inp["odd_w_out"])[0], ffn_w_in=f(inp["ffn_w_in"]), ffn_w_out=f(inp["ffn_w_out"]), **consts)
    B = xp.shape[0]
    maps = []
    for c in range(n_cores):
        xin = np.zeros((NT, D), np.float32)
        s0 = seqs[0]["start"]
        if c < B:
            xin[s0:s0 + 16] = meta
            xin[s0 + 16:s0 + 16 + T] = xp[c]
        sidx = [c * NS + i for i in range(NS)]
        for i, sj in enumerate(sidx):
            st = seqs[1 + i]["start"]
            if sj < xs.shape[0]:
                xin[st:st + 64] = xs[sj]

        def gather(a):
            a = f(a)[0]
            out = np.zeros((NS,) + a.shape[1:], np.float32)
            for i, sj in enumerate(sidx):
                if sj < a.shape[0]:
                    out[i] = a[sj]
            return out
        m = dict(shared)
        m.update(xin=xin, st_hS=gather(inp["state_hgrn_S"]), st_mC=gather(inp["state_mlstm_C"]), st_mn=gather(inp["state_mlstm_n"]),
                 st_mm=gather(inp["state_mlstm_m"]), st_gS=gather(inp["state_gdn_S"]), st_gc=gather(inp["state_gdn_conv"]))
        maps.append(m)
    return maps


_CACHE = {}


def run(inp, T, NS, n_cores, debug=False, trace=False):
    key = (T, NS, debug)
    if key not in _CACHE:
        _CACHE[key] = build(T, NS, debug)
    nc, seqs, NT, S = _CACHE[key]
    maps = make_in_maps(inp, T, NS, n_cores, seqs, NT)
    res = run_bass_kernel_spmd(nc, maps, core_ids=list(range(n_cores)), **({"trace": True} if trace else {}))
    return res, seqs, NT


def assemble(res, seqs, NT, B, T, NSAMP, NS, n_cores):
    R = res.results
    s0 = seqs[0]["start"]
    y_prompt = np.stack([R[c]["y"][s0 + 16:s0 + 16 + T] for c in range(B)], 0)
    y_sample = np.zeros((NSAMP, 64, D), np.float32)
    names = ["o_hS", "o_mC", "o_mn", "o_mm", "o_gS", "o_gc"]
    pst = {n: np.stack([R[c][n][0] for c in range(B)], 0)[None] for n in names}
    sst = {n: np.zeros((1, NSAMP) + R[0][n].shape[1:], np.float32) for n in names}
    for c in range(n_cores):
        for i in range(NS):
            sj = c * NS + i
            if sj >= NSAMP:
                continue
            st = seqs[1 + i]["start"]
            y_sample[sj] = R[c]["y"][st:st + 64]
            for n in names:
                sst[n][0, sj] = R[c][n][1 + i]
    outs = [y_prompt, y_sample] + [pst[n] for n in names] + [sst[n] for n in names]
    return tuple(np.ascontiguousarray(o.astype(np.float32)) for o in outs)


def kernel(**inputs):
    B = inputs["x_prompt"].shape[0]
    T = inputs["x_prompt"].shape[1]
    NSAMP = inputs["x_sample"].shape[0]
    NS = -(-NSAMP // N_CORES)
    res, seqs, NT = run(inputs, T, NS, N_CORES)
    return assemble(res, seqs, NT, B, T, NSAMP, NS, N_CORES)
```

```python
import numpy as np
from contextlib import ExitStack
import concourse.bass as bass
import concourse.mybir as mybir
from concourse.bass_utils import run_bass_kernel_spmd

F32 = mybir.dt.float32
BF16 = mybir.dt.bfloat16
AF = mybir.ActivationFunctionType
ALU = mybir.AluOpType
AX = mybir.AxisListType

D = 1024
FF = 2816
EVEN_COLS = 3592
ODD_COLS = 6176
EPS = 1e-6
N_CORES = 8
T_FULL = 8192
NS_FULL = 2
DBL_BF16 = True


class TB:
    def __init__(self, t, name):
        self.t = t
        self.name = name
        self.lw = None
        self.rd = {}

    def ap(self):
        return self.t[:]

    def __getitem__(self, idx):
        return self.t[idx]


class Sched:
    def __init__(self, nc, n_dma_slots=8, same_engine_sync=True):
        self.nc = nc
        self.es0 = ExitStack()
        self.stack = [self.es0]
        self.eng = {"pe": nc.tensor, "act": nc.scalar, "dve": nc.vector, "pool": nc.gpsimd, "sp": nc.sync}
        self.sems = {}
        self.cnt = {}
        for e in ("pe", "act", "dve", "pool"):
            self.sems[e] = self.es0.enter_context(nc.semaphore("s_" + e))
            self.cnt[e] = 0
        self.nslots = n_dma_slots
        self.dma_q = {}
        for q in ("sp", "pool"):
            slots = []
            for i in range(n_dma_slots):
                key = ("dma", q, i)
                self.sems[key] = self.es0.enter_context(nc.semaphore("d_%s%d" % (q, i)))
                self.cnt[key] = 0
                slots.append(key)
            self.dma_q[q] = [slots, 0]
        self.waited = {e: {} for e in self.eng}
        self.same = same_engine_sync
        self.noself = {"pe"}
        self.ninst = 0

    def push(self):
        self.phase = getattr(self, "phase", 0) + 1
        self.sb_bytes = getattr(self, "sb_base", 0)
        self.stack.append(ExitStack())

    def pop(self):
        self.barrier()
        self.stack.pop().close()

    def sb(self, name, shape, dtype):
        nb = int(np.prod(shape[1:])) * (2 if dtype == BF16 else 4)
        nb = -(-nb // 32) * 32
        if len(self.stack) == 1:
            self.sb_base = getattr(self, "sb_base", 0) + nb
        self.sb_bytes = getattr(self, "sb_bytes", 0) + nb
        assert self.sb_bytes <= 200 * 1024, ("SBUF budget exceeded", name, self.sb_bytes)
        t = self.stack[-1].enter_context(self.nc.sbuf_tensor("p%d_%s" % (getattr(self, "phase", 0), name), list(shape), dtype))
        return TB(t, name)

    def ps(self, name, shape, dtype):
        t = self.stack[-1].enter_context(self.nc.psum_tensor("p%d_%s" % (getattr(self, "phase", 0), name), list(shape), dtype))
        return TB(t, name)

    def _wait(self, e, k, v):
        if v <= 0 or self.waited[e].get(k, 0) >= v:
            return
        self.eng[e].wait_ge(self.sems[k], v)
        self.waited[e][k] = v
        self.ninst += 1

    def barrier(self):
        for e in self.eng:
            for k, v in self.cnt.items():
                if k == e:
                    continue
                self._wait(e, k, v)

    def _deps(self, e, reads, writes):
        need = {}

        def add(ev):
            if ev is None:
                return
            k, v = ev
            if need.get(k, 0) < v:
                need[k] = v

        for b in reads:
            add(b.lw)
        for b in writes:
            add(b.lw)
            for k, v in b.rd.items():
                add((k, v))
        for k, v in need.items():
            if k == e and (e in self.noself or not self.same):
                continue
            self._wait(e, k, v)

    def _commit(self, ev, reads, writes):
        k, v = ev
        for b in reads:
            if b.rd.get(k, 0) < v:
                b.rd[k] = v
        for b in writes:
            b.lw = ev
            b.rd = {}

    def op(self, e, fn, reads, writes):
        self._deps(e, reads, writes)
        inst = fn()
        self.cnt[e] += 1
        inst.then_inc(self.sems[e], 1)
        self._commit((e, self.cnt[e]), reads, writes)
        self.ninst += 1
        return inst

    def dma(self, out, in_, reads, writes, q="sp", **kw):
        slots, n = self.dma_q[q]
        key = slots[n % self.nslots]
        self.dma_q[q][1] = n + 1
        self._wait(q, key, self.cnt[key])
        self._deps(q, reads, writes)
        o = out.ap() if isinstance(out, TB) else out
        i = in_.ap() if isinstance(in_, TB) else in_
        inst = self.eng[q].dma_start(out=o, in_=i, **kw)
        self.cnt[key] += 16
        inst.then_inc(self.sems[key], 16)
        self._commit((key, self.cnt[key]), reads, writes)
        self.ninst += 1
        return inst

    def finish(self):
        self.barrier()
        while self.stack:
            self.stack.pop().close()

    def mm(self, out, lhsT, rhs, R, W, start=True, stop=True):
        nc = self.nc
        return self.op("pe", lambda: nc.tensor.matmul(out, lhsT=lhsT, rhs=rhs, start=start, stop=stop), R, W)

    def tr(self, out, in_, ident, R, W):
        nc = self.nc
        return self.op("pe", lambda: nc.tensor.transpose(out, in_, ident), R, W)

    def tt(self, out, a, b, op, R, W, e="dve"):
        eo = self.nc.vector if e == "dve" else self.nc.gpsimd
        return self.op(e, lambda: eo.tensor_tensor(out=out, in0=a, in1=b, op=op), R, W)

    def ts(self, out, a, s1, op0, R, W, s2=None, op1=None, e="dve"):
        eo = self.nc.vector if e == "dve" else self.nc.gpsimd
        if op1 is None:
            return self.op(e, lambda: eo.tensor_scalar(out=out, in0=a, scalar1=s1, scalar2=None, op0=op0), R, W)
        return self.op(e, lambda: eo.tensor_scalar(out=out, in0=a, scalar1=s1, scalar2=s2, op0=op0, op1=op1), R, W)

    def stt(self, out, a, s, b, op0, op1, R, W):
        nc = self.nc
        return self.op("dve", lambda: nc.vector.scalar_tensor_tensor(out=out, in0=a, scalar=s, in1=b, op0=op0, op1=op1), R, W)

    def af(self, out, in_, func, R, W, scale=None, bias=None, accum=None):
        nc = self.nc
        kw = {}
        if scale is not None:
            kw["scale"] = scale
        if bias is not None:
            kw["bias"] = bias
        if accum is not None:
            kw["accum_out"] = accum
        return self.op("act", lambda: nc.scalar.activation(out=out, in_=in_, func=func, **kw), R, W)

    def cp(self, out, in_, R, W, e="dve"):
        nc = self.nc
        if e == "act":
            return self.op("act", lambda: nc.scalar.copy(out=out, in_=in_), R, W)
        eo = nc.vector if e == "dve" else nc.gpsimd
        return self.op(e, lambda: eo.tensor_copy(out=out, in_=in_), R, W)

    def red(self, out, in_, op, R, W):
        nc = self.nc
        return self.op("dve", lambda: nc.vector.tensor_reduce(out=out, in_=in_, axis=AX.X, op=op), R, W)

    def recip(self, out, in_, R, W):
        nc = self.nc
        return self.op("dve", lambda: nc.vector.reciprocal(out=out, in_=in_), R, W)

    def memset(self, tb, val, e="dve"):
        eo = self.nc.vector if e == "dve" else self.nc.gpsimd
        return self.op(e, lambda: eo.memset(tb.ap(), val), [], [tb])


def bc(ap, axis, shape):
    return ap.unsqueeze(axis).broadcast_to(list(shape))


def seq_layout(T, NS):
    seqs = []
    r = 3
    seqs.append(dict(start=r, len=16 + T, chunks=[16] + [64] * (T // 64), init=None))
    r += 16 + T
    for i in range(NS):
        r += 3
        seqs.append(dict(start=r, len=64, chunks=[64], init=i))
        r += 64
    NT = -(-r // 128) * 128
    return seqs, NT


def load_weight(S, Wsb, Wd, K):
    for k in range(K // 128):
        S.dma(Wsb[:, k, :], Wd[k * 128:(k + 1) * 128, :], [], [], q="pool")


def rmsnorm_tile(S, xt, wbc, h, junk, ss, rstd):
    S.af(junk.ap(), xt.ap(), AF.Square, [xt], [junk, ss], accum=ss.ap())
    S.af(ss.ap(), ss.ap(), AF.Sqrt, [ss], [ss], scale=1.0 / D, bias=EPS)
    S.recip(rstd.ap(), ss.ap(), [ss], [rstd])
    S.stt(h.ap(), xt.ap(), rstd.ap(), wbc.ap(), ALU.mult, ALU.mult, [xt, rstd, wbc], [h])


def transpose_tile(S, src, nk, ptb, dst, ident, e="dve"):
    for k0 in range(0, nk, 8):
        n = min(8, nk - k0)
        for k in range(n):
            S.tr(ptb[:, k, :], src[:, (k0 + k) * 128:(k0 + k + 1) * 128], ident.ap(), [src, ident], [ptb])
        S.cp(dst[:, k0:k0 + n, :], ptb[:, 0:n, :], [ptb], [dst], e=e)


def run_pipelined(n, A1, B, A2):
    A1(0)
    A2(0)
    for i in range(n):
        if i + 1 < n:
            A1(i + 1)
        B(i)
        if i + 1 < n:
            A2(i + 1)


def load_bcast(S, tb, dram_row, n):
    S.dma(tb, dram_row.broadcast_to([128, n]), [], [tb])


def build(T=T_FULL, NS=NS_FULL, debug=False):
    nc = bass.Bass("TRN2", target_bir_lowering=False)
    seqs, NT = seq_layout(T, NS)
    NSEQ = 1 + NS
    ntile = NT // 128

    def din(name, shape, dt=F32):
        return nc.dram_tensor(name, list(shape), dt, kind="ExternalInput").ap()

    def dout(name, shape, dt=F32):
        return nc.dram_tensor(name, list(shape), dt, kind="ExternalOutput").ap()

    def dscr(name, shape, dt=F32):
        return nc.dram_tensor(name, list(shape), dt, kind="ExternalOutput" if debug else "Internal").ap()

    xin = din("xin", [NT, D])
    st_hS = din("st_hS", [NS, 4, 128, 128])
    st_mC = din("st_mC", [NS, 4, 64, 128])
    st_mn = din("st_mn", [NS, 4, 64])
    st_mm = din("st_mm", [NS, 4])
    st_gS = din("st_gS", [NS, 16, 128, 128])
    st_gc = din("st_gc", [NS, 3, 4096])
    norm_mix = din("norm_mix", [2, D])
    norm_ffn = din("norm_ffn", [2, D])
    norm_final = din("norm_final", [1, D])
    even_w_in = din("even_w_in", [D, EVEN_COLS])
    even_w_out = din("even_w_out", [D, D])
    lb_logits = din("lb_logits", [2, 512])
    hgrn_norm = din("hgrn_norm", [1, 512])
    mlstm_bif = din("mlstm_bif", [1, 8])
    mlstm_norm = din("mlstm_norm", [1, 512])
    odd_w_in = din("odd_w_in", [D, ODD_COLS])
    odd_conv_w = din("odd_conv_w", [4, 4096])
    gdn_a_log = din("gdn_a_log", [1, 16])
    gdn_dt_bias = din("gdn_dt_bias", [1, 16])
    gdn_norm = din("gdn_norm", [1, 128])
    odd_w_out = din("odd_w_out", [2048, D])
    ffn_w_in = din("ffn_w_in", [2, D, 2 * FF])
    ffn_w_out = din("ffn_w_out", [2, FF, D])
    c_ident = din("c_ident", [128, 128])
    c_U = din("c_U", [64, 64])
    c_SU = din("c_SU", [64, 64])
    c_SL = din("c_SL", [64, 64])
    c_bm4 = din("c_bm4", [4, 4, 64])
    c_Ublk = din("c_Ublk", [128, 128])
    c_SLblk = din("c_SLblk", [128, 128])
    c_rep = din("c_rep", [128, 3, 64])

    y = dout("y", [NT, D])
    o_hS = dout("o_hS", [NSEQ, 4, 128, 128])
    o_mC = dout("o_mC", [NSEQ, 4, 64, 128])
    o_mn = dout("o_mn", [NSEQ, 4, 64])
    o_mm = dout("o_mm", [NSEQ, 4])
    o_gS = dout("o_gS", [NSEQ, 16, 128, 128])
    o_gc = dout("o_gc", [NSEQ, 3, 4096])

    HG = dscr("HG", [NT, 2560])
    ML = dscr("ML", [NT, 1544])
    MG0 = dscr("MG0", [NT, 1024], BF16)
    X1 = dscr("X1", [NT, D])
    ACTF = dscr("ACTF", [NT, FF], BF16)
    X2 = dscr("X2", [NT, D])
    PQ = dscr("PQ", [NT + 3, 4096])
    GD = dscr("GD", [NT, 6176])
    MG1 = dscr("MG1", [NT, 2048], BF16)
    X3 = dscr("X3", [NT, D])

    S = Sched(nc)
    gap_rows = []
    _r = 0
    for _sq in seqs:
        if _sq["start"] > _r:
            gap_rows.append((_r, _sq["start"]))
        _r = _sq["start"] + _sq["len"]
    if _r < NT:
        gap_rows.append((_r, NT))

    def zero_gap_rows(dst, ncols, dt):
        z = S.sb("zgap_" + str(getattr(S, "phase", 0)), [128, ncols], dt)
        S.memset(z, 0.0)
        for (a_, b_) in gap_rows:
            S.dma(dst[a_:b_, :], z[0:b_ - a_, :], [z], [], q="pool")

    I32 = S.sb("I32", [128, 128], F32)
    Ibf = S.sb("Ibf", [128, 128], BF16)
    U32 = S.sb("U32", [64, 64], F32)
    SU32 = S.sb("SU32", [64, 64], F32)
    SL32 = S.sb("SL32", [64, 64], F32)
    ones32 = S.sb("ones32", [128, 128], F32)
    bm4 = S.sb("bm4", [4, 4, 64], F32)
    S.dma(I32, c_ident, [], [I32])
    S.dma(Ibf, c_ident, [], [Ibf], q="pool")
    S.dma(U32, c_U, [], [U32])
    S.dma(SU32, c_SU, [], [SU32])
    S.dma(SL32, c_SL, [], [SL32])
    S.dma(bm4, c_bm4, [], [bm4])
    S.memset(ones32, 1.0)
    S.barrier()

    def dense_blocks(hT, nk, W, blocks, pms, pmi):
        for (c0, ncols, epi) in blocks:
            pm = pms[pmi % len(pms)]
            pmi += 1
            for k in range(nk):
                S.mm(pm[:, 0:ncols], hT[:, k, :], W[:, k, c0:c0 + ncols], [hT, W], [pm], start=(k == 0), stop=(k == nk - 1))
            epi(pm)
        return pmi

    def phase_d1_even():
        S.push()
        W = S.sb("W", [128, 8, EVEN_COLS], BF16)
        wn = S.sb("wn", [128, D], F32)
        lb = S.sb("lb", [128, 512], F32)
        oml = S.sb("oml", [128, 512], F32)
        hnw = S.sb("hnw", [128, 512], F32)
        mnw = S.sb("mnw", [128, 512], F32)
        bif = S.sb("bif", [128, 8], F32)
        junk = S.sb("junk", [128, D], F32)
        tmp = [S.sb("tmp%d" % i, [128, 512], F32) for i in range(2)]
        t8 = S.sb("t8", [128, 8], F32)
        t8b = S.sb("t8b", [128, 4], F32)
        xt = [S.sb("xt%d" % i, [128, D], F32) for i in range(2)]
        ss = [S.sb("ss%d" % i, [128, 1], F32) for i in range(2)]
        rstd = [S.sb("rstd%d" % i, [128, 1], F32) for i in range(2)]
        h = [S.sb("h%d" % i, [128, D], BF16) for i in range(2)]
        hT = [S.sb("hT%d" % i, [128, 8, 128], BF16) for i in range(2)]
        HGt = [S.sb("HGt%d" % i, [128, 5, 512], F32) for i in range(2)]
        MLt = [S.sb("MLt%d" % i, [128, 1544], F32) for i in range(2)]
        ptb = [S.ps("ptb%d" % i, [128, 8, 128], BF16) for i in range(2)]
        pms = [S.ps("pm%d" % i, [128, 512], F32) for i in range(6)]

        load_weight(S, W, even_w_in, D)
        load_bcast(S, wn, norm_mix[0:1, :], D)
        load_bcast(S, lb, lb_logits[0:1, :], 512)
        load_bcast(S, oml, lb_logits[1:2, :], 512)
        load_bcast(S, hnw, hgrn_norm[0:1, :], 512)
        load_bcast(S, mnw, mlstm_norm[0:1, :], 512)
        load_bcast(S, bif, mlstm_bif[0:1, :], 8)
        S.tt(lb.ap(), lb.ap(), oml.ap(), ALU.subtract, [lb, oml], [lb])
        S.af(lb.ap(), lb.ap(), AF.Sigmoid, [lb], [lb])
        S.ts(oml.ap(), lb.ap(), -1.0, ALU.mult, [lb], [oml], s2=1.0, op1=ALU.add)
        S.barrier()

        pmi_box = [0]

        def stA1(i):
            b = i % 2
            S.dma(xt[b], xin[i * 128:i * 128 + 128, :], [], [xt[b]])
            rmsnorm_tile(S, xt[b], wn, h[b], junk, ss[b], rstd[b])

        def stA2(i):
            b = i % 2
            transpose_tile(S, h[b], 8, ptb[b], hT[b], Ibf)

        def stB(i):
            pmi = pmi_box[0]
            b = i % 2
            r0 = i * 128
            hg, ml = HGt[b], MLt[b]
            t0, t1 = tmp

            def e_q(pm):
                S.af(hg[:, 0, :], pm.ap(), AF.Silu, [pm], [hg])

            def e_f(pm):
                S.af(t0.ap(), pm.ap(), AF.Sigmoid, [pm], [t0])
                S.tt(t0.ap(), t0.ap(), oml.ap(), ALU.mult, [t0, oml], [t0])
                S.tt(t0.ap(), t0.ap(), lb.ap(), ALU.add, [t0, lb], [t0])
                S.af(hg[:, 1, :], t0.ap(), AF.Ln, [t0], [hg])
                S.ts(hg[:, 2, :], t0.ap(), -1.0, ALU.mult, [t0], [hg], s2=1.0, op1=ALU.add)

            def e_v(pm):
                S.cp(hg[:, 3, :], pm.ap(), [pm], [hg])

            def e_g(pm):
                S.af(t1.ap(), pm.ap(), AF.Silu, [pm], [t1])
                S.tt(hg[:, 4, :], t1.ap(), hnw.ap(), ALU.mult, [t1, hnw], [hg])

            def e_qk(pm):
                S.cp(ml[:, 0:256], pm[:, 0:256], [pm], [ml])
                S.ts(ml[:, 256:512], pm[:, 256:512], 0.125, ALU.mult, [pm], [ml])

            def e_mv(pm):
                S.cp(ml[:, 512:1024], pm.ap(), [pm], [ml])

            def e_o(pm):
                S.af(t1.ap(), pm.ap(), AF.Sigmoid, [pm], [t1])
                S.tt(ml[:, 1024:1536], t1.ap(), mnw.ap(), ALU.mult, [t1, mnw], [ml])

            def e_if(pm):
                S.tt(t8.ap(), pm[:, 0:8], bif.ap(), ALU.add, [pm, bif], [t8])
                S.af(t8.ap(), t8.ap(), AF.Tanh, [t8], [t8], scale=1.0 / 15.0)
                S.ts(ml[:, 1536:1540], t8[:, 0:4], 15.0, ALU.mult, [t8], [ml])
                S.af(t8b.ap(), t8[:, 4:8], AF.Exp, [t8], [t8b], scale=-15.0)
                S.af(t8b.ap(), t8b.ap(), AF.Ln, [t8b], [t8b], bias=1.0)
                S.ts(ml[:, 1540:1544], t8b.ap(), -1.0, ALU.mult, [t8b], [ml])

            blocks = [(0, 512, e_q), (1536, 512, e_g), (512, 512, e_f), (1024, 512, e_v),
                      (2048, 512, e_qk), (2560, 512, e_mv), (3072, 512, e_o), (3584, 8, e_if)]
            pmi_box[0] = dense_blocks(hT[b], 8, W, blocks, pms, pmi)
            S.dma(HG[r0:r0 + 128, :], hg.ap().rearrange("p a b -> p (a b)"), [hg], [], q="pool")
            S.dma(ML[r0:r0 + 128, :], ml, [ml], [], q="pool")

        run_pipelined(ntile, stA1, stB, stA2)
        S.pop()

    def phase_r0():
        S.push()
        Sh = S.sb("Sh", [128, 4, 128], F32)
        Shb = S.sb("Shb", [128, 4, 128], BF16)
        Cn = S.sb("Cn", [64, 4, 129], F32)
        Cnb = S.sb("Cnb", [64, 4, 129], BF16)
        m_row = S.sb("m_row", [4, 1], F32)
        m_bc = S.sb("m_bc", [64, 4], F32)
        hgc = [S.sb("hgc%d" % i, [64, 5, 512], F32) for i in range(2)]
        mlc = [S.sb("mlc%d" % i, [64, 1544], F32) for i in range(2)]
        mg = [S.sb("mg%d" % i, [64, 1024], BF16) for i in range(2)]
        eb = S.sb("eb", [64, 512], F32)
        enb = S.sb("enb", [64, 512], F32)
        qt = S.sb("qt", [64, 512], BF16)
        kt = S.sb("kt", [64, 512], BF16)
        qkT = S.sb("qkT", [128, 8, 64], BF16)
        ebl = S.sb("ebl", [128, 4, 2], F32)
        scm = S.sb("scm", [64, 4, 64], BF16)
        vb = S.sb("vb", [64, 512], BF16)
        stmp = S.sb("stmp", [128, 4, 128], F32)
        osq = S.sb("osq", [64, 512], F32)
        oss = S.sb("oss", [64, 4], F32)
        orstd = S.sb("orstd", [64, 4], F32)
        on = S.sb("on", [64, 4, 128], F32)
        a_t = S.sb("a_t", [64, 4], F32)
        aT = S.sb("aT", [4, 64], F32)
        M_row = S.sb("M_row", [4, 64], F32)
        Mblk = S.sb("Mblk", [4, 4, 64], F32)
        Mlb = S.sb("Mlb", [4, 4], F32)
        M_t = S.sb("M_t", [64, 4], F32)
        F_t = S.sb("F_t", [64, 4], F32)
        DT = S.sb("DT", [64, 4, 64], F32)
        Wbc = S.sb("Wbc", [64, 4, 64], F32)
        mq = S.sb("mq", [64, 512], BF16)
        mqkT = S.sb("mqkT", [64, 8, 64], BF16)
        qTt = S.sb("qTt", [64, 4, 64], BF16)
        qkD = S.sb("qkD", [64, 4, 64], BF16)
        vext = S.sb("vext", [64, 4, 129], BF16)
        den = S.sb("den", [64, 4], F32)
        nden = S.sb("nden", [64, 4], F32)
        emt = S.sb("emt", [64, 4], F32)
        rden = S.sb("rden", [64, 4], F32)
        hn = S.sb("hn", [64, 4, 128], F32)
        w_s = S.sb("w_s", [64, 4], F32)
        khat = S.sb("khat", [64, 4, 64], BF16)
        dec = S.sb("dec", [64, 4], F32)
        Flb = S.sb("Flb", [64, 4], F32)
        Mlbc = S.sb("Mlbc", [64, 4], F32)
        fl_row = S.sb("fl_row", [4, 2], F32)
        ones_c2 = S.sb("ones_c2", [64, 2], F32)
        pA = S.ps("pA", [128, 512], F32)
        pB = S.ps("pB", [128, 8, 64], BF16)
        pC = S.ps("pC", [128, 512], F32)
        pD = S.ps("pD", [128, 512], F32)
        pE = S.ps("pE", [128, 512], F32)
        pF = S.ps("pF", [64, 4, 256], F32)
        pB2 = S.ps("pB2", [128, 8, 64], BF16)
        osq2 = S.sb("osq2", [64, 512], F32)
        oss2 = S.sb("oss2", [64, 4], F32)
        orstd2 = S.sb("orstd2", [64, 4], F32)
        S.memset(ones_c2, 1.0)
        zero_gap_rows(MG0, 1024, BF16)

        def interleave(gens):
            gens = [g for g in gens if g is not None]
            while gens:
                for g in list(gens):
                    try:
                        next(g)
                    except StopIteration:
                        gens.remove(g)

        for si, sq in enumerate(seqs):
            if sq["init"] is None:
                S.memset(Sh, 0.0)
                S.memset(Cn, 0.0)
                S.memset(m_row, 0.0)
                S.memset(m_bc, 0.0)
            else:
                j = sq["init"]
                S.dma(Sh, st_hS[j].rearrange("h k v -> k h v"), [], [Sh])
                S.dma(Cn[:, :, 0:128], st_mC[j].rearrange("h k v -> k h v"), [], [Cn])
                S.dma(Cn[:, :, 128:129], st_mn[j].rearrange("h (k o) -> k h o", o=1), [], [Cn], allow_slow_non_contiguous=True)
                S.dma(m_row, st_mm[j].rearrange("(h o) -> h o", o=1), [], [m_row])
                S.dma(m_bc, st_mm[j:j + 1, :].broadcast_to([64, 4]), [], [m_bc])
            S.cp(Shb.ap(), Sh.ap(), [Sh], [Shb], e="act")
            S.cp(Cnb.ap(), Cn.ap(), [Cn], [Cnb], e="act")
            row = sq["start"]
            for ci, c in enumerate(sq["chunks"]):
                b = ci % 2
                hg, ml, mgo = hgc[b], mlc[b], mg[b]
                S.dma(hg[0:c], HG[row:row + c, :].rearrange("p (a b) -> p a b", a=5), [], [hg])
                S.dma(ml[0:c], ML[row:row + c, :], [], [ml])
                def hgrn_chunk(c=c, hg=hg, mgo=mgo):
                    S.mm(pA[0:c, :], U32[0:c, 0:c], hg[0:c, 1, :], [U32, hg], [pA])
                    S.af(eb[0:c], pA[0:c, :], AF.Exp, [pA], [eb])
                    S.af(enb[0:c], pA[0:c, :], AF.Exp, [pA], [enb], scale=-1.0)
                    yield
                    S.tt(qt[0:c], hg[0:c, 0, :], eb[0:c], ALU.mult, [hg, eb], [qt])
                    S.tt(kt[0:c], hg[0:c, 2, :], enb[0:c], ALU.mult, [hg, enb], [kt])
                    S.cp(vb[0:c], hg[0:c, 3, :], [hg], [vb], e="act")
                    yield
                    for hh in range(4):
                        S.tr(pB[:, hh, 0:c], qt[0:c, hh * 128:(hh + 1) * 128], Ibf[0:c, 0:c], [qt, Ibf], [pB])
                        S.tr(pB[:, 4 + hh, 0:c], kt[0:c, hh * 128:(hh + 1) * 128], Ibf[0:c, 0:c], [kt, Ibf], [pB])
                    S.cp(qkT[:, :, 0:c], pB[:, :, 0:c], [pB], [qkT])
                    yield
                    for hh in range(4):
                        S.mm(pC[:, 2 * hh:2 * hh + 2], eb[0:c, hh * 128:(hh + 1) * 128], I32[0:c, c - 2:c], [eb, I32], [pC])
                    S.cp(ebl.ap().rearrange("p h t -> p (h t)"), pC[:, 0:8], [pC], [ebl], e="act")
                    for hh in range(4):
                        S.mm(pC[0:c, 64 + hh * 64:64 + hh * 64 + c], qkT[:, 4 + hh, 0:c], qkT[:, hh, 0:c], [qkT], [pC])
                    S.tt(scm[0:c, :, 0:c], pC[0:c, 64:320].rearrange("p (h t) -> p h t", h=4)[:, :, 0:c],
                         bc(U32[0:c, 0:c], 1, [c, 4, c]), ALU.mult, [pC, U32], [scm])
                    yield
                    for hh in range(4):
                        S.mm(pA[0:c, hh * 128:(hh + 1) * 128], scm[0:c, hh, 0:c], vb[0:c, hh * 128:(hh + 1) * 128], [scm, vb], [pA], start=True, stop=False)
                        S.mm(pA[0:c, hh * 128:(hh + 1) * 128], qkT[:, hh, 0:c], Shb[:, hh, :], [qkT, Shb], [pA], start=False, stop=True)
                    for hh in range(4):
                        S.mm(pD[:, hh * 128:(hh + 1) * 128], kt[0:c, hh * 128:(hh + 1) * 128], vb[0:c, hh * 128:(hh + 1) * 128], [kt, vb], [pD])
                    yield
                    S.tt(stmp.ap().rearrange("p h v -> p (h v)"), Sh.ap().rearrange("p h v -> p (h v)"), pD.ap(), ALU.add, [Sh, pD], [stmp])
                    S.tt(Sh.ap(), stmp.ap(), bc(ebl[:, :, 1], 2, [128, 4, 128]), ALU.mult, [stmp, ebl], [Sh])
                    S.cp(Shb.ap(), Sh.ap(), [Sh], [Shb], e="act")
                    yield
                    S.af(osq[0:c], pA[0:c, :], AF.Square, [pA], [osq])
                    S.red(oss[0:c], osq[0:c].rearrange("p (h v) -> p h v", h=4), ALU.add, [osq], [oss])
                    S.af(oss[0:c], oss[0:c], AF.Ln, [oss], [oss], scale=1.0 / 128.0, bias=EPS)
                    S.af(orstd[0:c], oss[0:c], AF.Exp, [oss], [orstd], scale=-0.5)
                    S.tt(on[0:c], pA[0:c, :].rearrange("p (h v) -> p h v", h=4), bc(orstd[0:c], 2, [c, 4, 128]), ALU.mult, [pA, orstd], [on])
                    S.tt(mgo[0:c, 0:512], on[0:c].rearrange("p h v -> p (h v)"), hg[0:c, 4, :], ALU.mult, [on, hg], [mgo])

                def mlstm_chunk(c=c, ml=ml, mgo=mgo):
                    S.mm(pE[0:c, 0:4], U32[0:c, 0:c], ml[0:c, 1540:1544], [U32, ml], [pE])
                    S.mm(pE[0:64, 8:12], ones32[0:c, 0:64], ml[0:c, 1540:1544], [ones32, ml], [pE])
                    S.mm(pE[0:4, 16:18], ml[0:c, 1540:1544], ones_c2[0:c, :], [ml, ones_c2], [pE])
                    S.cp(F_t[0:c], pE[0:c, 0:4], [pE], [F_t])
                    S.cp(Flb.ap(), pE[0:64, 8:12], [pE], [Flb])
                    S.cp(fl_row.ap(), pE[0:4, 16:18], [pE], [fl_row])
                    yield
                    S.tt(a_t[0:c], ml[0:c, 1536:1540], F_t[0:c], ALU.subtract, [ml, F_t], [a_t])
                    S.tr(pE[0:4, 32:32 + c], a_t[0:c, :], I32[0:c, 0:c], [a_t, I32], [pE])
                    S.cp(aT[:, 0:c], pE[0:4, 32:32 + c], [pE], [aT])
                    S.op("dve", lambda: nc.vector.tensor_tensor_scan(out=M_row[:, 0:c], data0=aT[:, 0:c], data1=aT[:, 0:c],
                                                                     initial=m_row.ap(), op0=ALU.max, op1=ALU.max),
                         [aT, m_row], [M_row])
                    yield
                    S.tt(Mblk[:, :, 0:c], bc(M_row[:, 0:c], 1, [4, 4, c]), bm4[:, :, 0:c], ALU.mult, [M_row, bm4], [Mblk])
                    S.tt(Mlb.ap(), M_row[:, c - 1:c].broadcast_to([4, 4]), bm4[:, :, 0], ALU.mult, [M_row, bm4], [Mlb])
                    S.mm(pE[0:64, 128:128 + 4 * c], ones32[0:4, 0:64], Mblk[:, :, 0:c], [ones32, Mblk], [pE])
                    S.mm(pE[0:64, 400:404], ones32[0:4, 0:64], Mlb.ap(), [ones32, Mlb], [pE])
                    S.tr(pE[0:c, 416:420], M_row[:, 0:c], I32[0:4, 0:4], [M_row, I32], [pE])
                    S.cp(M_t[0:c], pE[0:c, 416:420], [pE], [M_t])
                    S.cp(Mlbc.ap(), pE[0:64, 400:404], [pE], [Mlbc])
                    yield
                    Mb3 = pE[0:64, 128:128 + 4 * c].rearrange("p (h t) -> p h t", h=4)
                    for hh in range(4):
                        S.af(DT[0:c, hh, 0:c], Mb3[0:c, hh, 0:c], AF.Exp, [pE, a_t], [DT], scale=-1.0, bias=a_t[0:c, hh:hh + 1])
                        S.af(Wbc[:, hh, 0:c], Mb3[:, hh, 0:c], AF.Exp, [pE, m_bc], [Wbc], scale=-1.0, bias=m_bc[:, hh:hh + 1])
                    S.tt(DT[0:c, :, 0:c], DT[0:c, :, 0:c], bc(U32[0:c, 0:c], 1, [c, 4, c]), ALU.mult, [DT, U32], [DT])
                    yield
                    S.cp(mq[0:c], ml[0:c, 0:512], [ml], [mq], e="act")
                    for hh in range(4):
                        S.tr(pB2[0:64, hh, 0:c], mq[0:c, hh * 64:(hh + 1) * 64], Ibf[0:c, 0:c], [mq, Ibf], [pB2])
                        S.tr(pB2[0:64, 4 + hh, 0:c], mq[0:c, 256 + hh * 64:256 + (hh + 1) * 64], Ibf[0:c, 0:c], [mq, Ibf], [pB2])
                    S.cp(mqkT[:, :, 0:c], pB2[0:64, :, 0:c], [pB2], [mqkT])
                    yield
                    S.tt(qTt[:, :, 0:c], mqkT[:, 0:4, 0:c], Wbc[:, :, 0:c], ALU.mult, [mqkT, Wbc], [qTt])
                    for hh in range(4):
                        S.mm(pF[0:c, hh, 192:192 + c], mqkT[:, 4 + hh, 0:c], mqkT[:, hh, 0:c], [mqkT], [pF])
                    S.tt(qkD[0:c, :, 0:c], pF[0:c, :, 192:192 + c], DT[0:c, :, 0:c], ALU.mult, [pF, DT], [qkD])
                    S.cp(vext[0:c, :, 0:128], ml[0:c, 512:1024].rearrange("p (h v) -> p h v", h=4), [ml], [vext], e="act")
                    S.cp(vext[0:c, :, 128:129], bc(ones32[0:c, 0:4], 2, [c, 4, 1]), [ones32], [vext])
                    yield
                    for hh in range(4):
                        S.mm(pF[0:c, hh, 0:129], qkD[0:c, hh, 0:c], vext[0:c, hh, :], [qkD, vext], [pF], start=True, stop=False)
                        S.mm(pF[0:c, hh, 0:129], qTt[:, hh, 0:c], Cnb[:, hh, :], [qTt, Cnb], [pF], start=False, stop=True)
                    yield
                    S.cp(den[0:c], pF[0:c, :, 128], [pF], [den])
                    S.ts(nden[0:c], den[0:c], -1.0, ALU.mult, [den], [nden])
                    S.tt(den[0:c], den[0:c], nden[0:c], ALU.max, [den, nden], [den])
                    S.tt(emt[0:c], F_t[0:c], M_t[0:c], ALU.add, [F_t, M_t], [emt])
                    S.af(emt[0:c], emt[0:c], AF.Exp, [emt], [emt], scale=-1.0)
                    S.tt(den[0:c], den[0:c], emt[0:c], ALU.max, [den, emt], [den])
                    S.recip(rden[0:c], den[0:c], [den], [rden])
                    S.tt(hn[0:c], pF[0:c, :, 0:128], bc(rden[0:c], 2, [c, 4, 128]), ALU.mult, [pF, rden], [hn])
                    yield
                    S.af(osq2[0:c], hn[0:c].rearrange("p h v -> p (h v)"), AF.Square, [hn], [osq2])
                    S.red(oss2[0:c], osq2[0:c].rearrange("p (h v) -> p h v", h=4), ALU.add, [osq2], [oss2])
                    S.af(oss2[0:c], oss2[0:c], AF.Ln, [oss2], [oss2], scale=1.0 / 128.0, bias=EPS)
                    S.af(orstd2[0:c], oss2[0:c], AF.Exp, [oss2], [orstd2], scale=-0.5)
                    S.tt(hn[0:c], hn[0:c], bc(orstd2[0:c], 2, [c, 4, 128]), ALU.mult, [hn, orstd2], [hn])
                    S.tt(mgo[0:c, 512:1024], hn[0:c].rearrange("p h v -> p (h v)"), ml[0:c, 1024:1536], ALU.mult, [hn, ml], [mgo])
                    yield
                    S.tt(w_s[0:c], a_t[0:c], Mlbc[0:c], ALU.subtract, [a_t, Mlbc], [w_s])
                    S.af(w_s[0:c], w_s[0:c], AF.Exp, [w_s], [w_s])
                    S.tt(dec.ap(), m_bc.ap(), Mlbc.ap(), ALU.subtract, [m_bc, Mlbc], [dec])
                    S.af(dec.ap(), dec.ap(), AF.Exp, [dec], [dec])
                    S.tt(khat[0:c], ml[0:c, 256:512].rearrange("p (h d) -> p h d", h=4), bc(w_s[0:c], 2, [c, 4, 64]), ALU.mult, [ml, w_s], [khat])
                    for hh in range(4):
                        S.mm(pF[:, hh, 0:129], khat[0:c, hh, :], vext[0:c, hh, :], [khat, vext], [pF])
                    S.tt(Cn.ap(), Cn.ap(), bc(dec.ap(), 2, [64, 4, 129]), ALU.mult, [Cn, dec], [Cn])
                    S.tt(Cn.ap(), Cn.ap(), pF[:, :, 0:129], ALU.add, [Cn, pF], [Cn])
                    S.cp(Cnb.ap(), Cn.ap(), [Cn], [Cnb], e="act")
                    S.tt(m_bc.ap(), Mlbc.ap(), Flb.ap(), ALU.add, [Mlbc, Flb], [m_bc])
                    S.tt(m_row.ap(), M_row[:, c - 1:c], fl_row[:, 0:1], ALU.add, [M_row, fl_row], [m_row])

                interleave([hgrn_chunk(), mlstm_chunk()])
                S.dma(MG0[row:row + c, :], mgo[0:c], [mgo], [], q="pool")
                row += c
            S.dma(o_hS[si].rearrange("h k v -> k h v"), Sh, [Sh], [], q="pool")
            S.dma(o_mC[si].rearrange("h k v -> k h v"), Cn[:, :, 0:128], [Cn], [], q="pool")
            S.dma(o_mn[si].rearrange("h (k o) -> k h o", o=1), Cn[:, :, 128:129], [Cn], [], q="pool", allow_slow_non_contiguous=True)
            S.dma(o_mm[si].rearrange("(h o) -> h o", o=1), m_row, [m_row], [], q="pool")
        S.pop()

    def phase_d3ab(layer, MG, KM, w_out_d, xsrc):
        S.push()
        nkm = KM // 128
        Wo = S.sb("Wo", [128, nkm, D], BF16)
        W1 = S.sb("W1", [128, 8, 2 * FF], BF16)
        wn = S.sb("wn", [128, D], F32)
        junk = S.sb("junk", [128, D], F32)
        mgt = [S.sb("mgt%d" % i, [128, KM], BF16) for i in range(2)]
        mT = [S.sb("mT%d" % i, [128, nkm, 128], BF16) for i in range(2)]
        xt = [S.sb("xt%d" % i, [128, D], F32) for i in range(2)]
        x1 = [S.sb("x1%d" % i, [128, D], F32) for i in range(2)]
        ss = [S.sb("ss%d" % i, [128, 1], F32) for i in range(2)]
        rstd = [S.sb("rstd%d" % i, [128, 1], F32) for i in range(2)]
        h = [S.sb("h%d" % i, [128, D], BF16) for i in range(2)]
        hT = [S.sb("hT%d" % i, [128, 8, 128], BF16) for i in range(2)]
        sg = [S.sb("sg%d" % i, [128, 512], F32) for i in range(2)]
        act = [S.sb("act%d" % i, [128, FF], BF16) for i in range(2)]
        ptb = [S.ps("ptb%d" % i, [128, 8, 128], BF16) for i in range(2)]
        pms = [S.ps("pm%d" % i, [128, 512], F32) for i in range(6)]
        load_weight(S, Wo, w_out_d, KM)
        load_weight(S, W1, ffn_w_in[layer], D)
        load_bcast(S, wn, norm_ffn[layer:layer + 1, :], D)
        S.barrier()
        box = [0, 0]

        def stA1(i):
            pmi = box[0]
            b = i % 2
            r0 = i * 128
            S.dma(mgt[b], MG[r0:r0 + 128, :], [], [mgt[b]])
            S.dma(xt[b], xsrc[r0:r0 + 128, :], [], [xt[b]])
            transpose_tile(S, mgt[b], nkm, ptb[b], mT[b], Ibf)
            x1b, xb = x1[b], xt[b]
            blocks = []
            for cb in range(2):
                def epi(pm, cb=cb):
                    S.tt(x1b[:, cb * 512:(cb + 1) * 512], xb[:, cb * 512:(cb + 1) * 512], pm.ap(), ALU.add, [xb, pm], [x1b])
                blocks.append((cb * 512, 512, epi))
            box[0] = dense_blocks(mT[b], nkm, Wo, blocks, pms, pmi)
            S.dma(X1[r0:r0 + 128, :], x1b, [x1b], [], q="pool")
            rmsnorm_tile(S, x1b, wn, h[b], junk, ss[b], rstd[b])

        def stA2(i):
            b = i % 2
            transpose_tile(S, h[b], 8, ptb[b], hT[b], Ibf)

        def stB(i):
            pmi, sgi = box
            b = i % 2
            r0 = i * 128
            ab = act[b]
            for o in range(0, FF, 512):
                n = min(512, FF - o)
                pg = pms[pmi % 6]
                pu = pms[(pmi + 1) % 6]
                pmi += 2
                for k in range(8):
                    S.mm(pg[:, 0:n], hT[b][:, k, :], W1[:, k, o:o + n], [hT[b], W1], [pg], start=(k == 0), stop=(k == 7))
                for k in range(8):
                    S.mm(pu[:, 0:n], hT[b][:, k, :], W1[:, k, FF + o:FF + o + n], [hT[b], W1], [pu], start=(k == 0), stop=(k == 7))
                s_ = sg[sgi % 2]
                sgi += 1
                S.af(s_[:, 0:n], pg[:, 0:n], AF.Silu, [pg], [s_])
                S.tt(ab[:, o:o + n], s_[:, 0:n], pu[:, 0:n], ALU.mult, [s_, pu], [ab])
            box[0], box[1] = pmi, sgi
            S.dma(ACTF[r0:r0 + 128, :], ab, [ab], [], q="pool")

        run_pipelined(ntile, stA1, stB, stA2)
        S.pop()

    def phase_d3c(layer, xdst, final):
        S.push()
        nk = FF // 128
        W2 = S.sb("W2", [128, nk, D], BF16)
        wn = S.sb("wn", [128, D], F32)
        junk = S.sb("junk", [128, D], F32)
        at = [S.sb("at%d" % i, [128, FF], BF16) for i in range(2)]
        aT = [S.sb("aT%d" % i, [128, nk, 128], BF16) for i in range(2)]
        xt = [S.sb("xt%d" % i, [128, D], F32) for i in range(2)]
        x2 = [S.sb("x2%d" % i, [128, D], F32) for i in range(2)]
        yt = [S.sb("yt%d" % i, [128, D], F32) for i in range(2)]
        ss = [S.sb("ss%d" % i, [128, 1], F32) for i in range(2)]
        rstd = [S.sb("rstd%d" % i, [128, 1], F32) for i in range(2)]
        ptb = [S.ps("ptb%d" % i, [128, 8, 128], BF16) for i in range(2)]
        pms = [S.ps("pm%d" % i, [128, 512], F32) for i in range(4)]
        load_weight(S, W2, ffn_w_out[layer], FF)
        if final:
            load_bcast(S, wn, norm_final[0:1, :], D)
        S.barrier()
        box = [0]

        def stA1(i):
            b = i % 2
            r0 = i * 128
            S.dma(at[b], ACTF[r0:r0 + 128, :], [], [at[b]])
            S.dma(xt[b], X1[r0:r0 + 128, :], [], [xt[b]])

        def stA2(i):
            b = i % 2
            transpose_tile(S, at[b], nk, ptb[b], aT[b], Ibf)

        def stB(i):
            pmi = box[0]
            b = i % 2
            r0 = i * 128
            x2b, xb = x2[b], xt[b]
            blocks = []
            for cb in range(2):
                def epi(pm, cb=cb):
                    S.tt(x2b[:, cb * 512:(cb + 1) * 512], xb[:, cb * 512:(cb + 1) * 512], pm.ap(), ALU.add, [xb, pm], [x2b])
                blocks.append((cb * 512, 512, epi))
            box[0] = dense_blocks(aT[b], nk, W2, blocks, pms, pmi)
            if not final:
                S.dma(xdst[r0:r0 + 128, :], x2b, [x2b], [], q="pool")
            else:
                S.af(junk.ap(), x2b.ap(), AF.Square, [x2b], [junk, ss[b]], accum=ss[b].ap())
                S.af(ss[b].ap(), ss[b].ap(), AF.Sqrt, [ss[b]], [ss[b]], scale=1.0 / D, bias=EPS)
                S.recip(rstd[b].ap(), ss[b].ap(), [ss[b]], [rstd[b]])
                S.stt(yt[b].ap(), x2b.ap(), rstd[b].ap(), wn.ap(), ALU.mult, ALU.mult, [x2b, rstd[b], wn], [yt[b]])
                S.dma(xdst[r0:r0 + 128, :], yt[b], [yt[b]], [], q="pool")

        run_pipelined(ntile, stA1, stB, stA2)
        S.pop()

    def phase_d1_odd():
        S.push()
        W = S.sb("W", [128, 8, ODD_COLS], BF16)
        wn = S.sb("wn", [128, D], F32)
        gnw = S.sb("gnw", [128, 128], F32)
        negA = S.sb("negA", [128, 16], F32)
        dtb = S.sb("dtb", [128, 16], F32)
        junk = S.sb("junk", [128, D], F32)
        tmp = [S.sb("tmp%d" % i, [128, 512], F32) for i in range(2)]
        t16 = S.sb("t16", [128, 16], F32)
        xt = [S.sb("xt%d" % i, [128, D], F32) for i in range(2)]
        ss = [S.sb("ss%d" % i, [128, 1], F32) for i in range(2)]
        rstd = [S.sb("rstd%d" % i, [128, 1], F32) for i in range(2)]
        h = [S.sb("h%d" % i, [128, D], BF16) for i in range(2)]
        hT = [S.sb("hT%d" % i, [128, 8, 128], BF16) for i in range(2)]
        pq = [S.sb("pq%d" % i, [128, 4096], F32) for i in range(2)]
        gzt = [S.sb("gzt%d" % i, [128, 2, 1040], F32) for i in range(2)]
        ptb = [S.ps("ptb%d" % i, [128, 8, 128], BF16) for i in range(2)]
        pms = [S.ps("pm%d" % i, [128, 512], F32) for i in range(6)]
        load_weight(S, W, odd_w_in, D)
        load_bcast(S, wn, norm_mix[1:2, :], D)
        load_bcast(S, gnw, gdn_norm[0:1, :], 128)
        load_bcast(S, negA, gdn_a_log[0:1, :], 16)
        load_bcast(S, dtb, gdn_dt_bias[0:1, :], 16)
        S.af(negA.ap(), negA.ap(), AF.Exp, [negA], [negA])
        S.ts(negA.ap(), negA.ap(), -1.0, ALU.mult, [negA], [negA])
        S.barrier()
        box = [0]

        def stA1(i):
            b = i % 2
            S.dma(xt[b], X2[i * 128:i * 128 + 128, :], [], [xt[b]])
            rmsnorm_tile(S, xt[b], wn, h[b], junk, ss[b], rstd[b])

        def stA2(i):
            b = i % 2
            transpose_tile(S, h[b], 8, ptb[b], hT[b], Ibf)

        def stB(i):
            pmi = box[0]
            b = i % 2
            r0 = i * 128
            pqb, gz = pq[b], gzt[b]
            blocks = []
            for cb in range(8):
                def epi(pm, cb=cb):
                    S.cp(pqb[:, cb * 512:(cb + 1) * 512], pm.ap(), [pm], [pqb], e=("act" if cb % 2 else "dve"))
                blocks.append((cb * 512, 512, epi))
            for cb in range(4):
                def epi(pm, cb=cb):
                    t_ = tmp[cb % 2]
                    S.af(t_.ap(), pm.ap(), AF.Silu, [pm], [t_])
                    S.tt(gz[:, cb // 2, (cb % 2) * 512:(cb % 2) * 512 + 512].rearrange("p (h v) -> p h v", h=4), t_.ap().rearrange("p (h v) -> p h v", h=4),
                         bc(gnw.ap(), 1, [128, 4, 128]), ALU.mult, [t_, gnw], [gz])
                blocks.append((4096 + cb * 512, 512, epi))

            def epi_ba(pm):
                S.af(gz[:, :, 1024:1032], pm[:, 0:16].rearrange("p (g h) -> p g h", g=2), AF.Sigmoid, [pm], [gz])
                S.tt(t16.ap(), pm[:, 16:32], dtb.ap(), ALU.add, [pm, dtb], [t16])
                S.af(t16.ap(), t16.ap(), AF.Exp, [t16], [t16])
                S.af(t16.ap(), t16.ap(), AF.Ln, [t16], [t16], bias=1.0)
                S.tt(gz[:, :, 1032:1040], t16.ap().rearrange("p (g h) -> p g h", g=2), negA.ap().rearrange("p (g h) -> p g h", g=2), ALU.mult, [t16, negA], [gz])
            blocks.append((6144, 32, epi_ba))
            box[0] = dense_blocks(hT[b], 8, W, blocks, pms, pmi)
            S.dma(PQ[3 + r0:3 + r0 + 128, :], pqb, [pqb], [], q="pool")
            for g in range(2):
                S.dma(GD[r0:r0 + 128, g * 3088 + 2048:g * 3088 + 3088], gz[:, g, :], [gz], [], q="pool")

        run_pipelined(ntile, stA1, stB, stA2)
        S.pop()
        S.push()
        zr = S.sb("zr", [3, 4096], F32)
        cs = [S.sb("cs%d" % i, [3, 4096], F32) for i in range(NSEQ)]
        S.memset(zr, 0.0)
        S.dma(PQ[0:3, :], zr, [zr], [])
        for si, sq in enumerate(seqs):
            st = sq["start"]
            if sq["init"] is None:
                S.dma(PQ[3 + st - 3:3 + st, :], zr, [zr], [])
            else:
                S.dma(cs[si], st_gc[sq["init"]], [], [cs[si]])
                S.dma(PQ[3 + st - 3:3 + st, :], cs[si], [cs[si]], [])
        for si, sq in enumerate(seqs):
            e = sq["start"] + sq["len"]
            t_ = S.sb("co%d" % si, [3, 4096], F32)
            S.dma(t_, PQ[3 + e - 3:3 + e, :], [], [t_])
            S.dma(o_gc[si], t_, [t_], [], q="pool")
        S.pop()

    def phase_c1():
        S.push()
        cw = [S.sb("cw%d" % j, [128, 4096], F32) for j in range(4)]
        pieces = [(0, 1024, "dve", 8, True), (1024, 1024, "dve", 8, False), (2048, 512, "dve", 0, False), (2560, 1536, "pool", 0, False)]
        bufA = [[S.sb("xa%d_%d" % (j, i), [128, 1024], F32) for j in range(4)] for i in range(3)]
        bufB = [[S.sb("xb%d_%d" % (j, i), [128, 512], F32) for j in range(4)] for i in range(2)]
        bufC = [[S.sb("xc%d_%d" % (j, i), [128, 1536], F32) for j in range(4)] for i in range(2)]
        t2 = S.sb("t2", [128, 1024], F32)
        ssq = S.sb("ssq", [128, 8], F32)
        rs = S.sb("rs", [128, 8], F32)
        for j in range(4):
            load_bcast(S, cw[j], odd_conv_w[j:j + 1, :], 4096)
        S.barrier()
        ia = 0
        for i in range(ntile):
            r0 = i * 128
            for pi, (c0, ncol, e, nl2, qs) in enumerate(pieces):
                if pi < 2:
                    xw = bufA[ia % 3]
                    ia += 1
                elif pi == 2:
                    xw = bufB[i % 2]
                else:
                    xw = bufC[i % 2]
                sl = slice(c0, c0 + ncol)
                for j in range(4):
                    S.dma(xw[j], PQ[r0 + j:r0 + j + 128, sl], [], [xw[j]])
                a = xw[0]
                S.tt(a.ap(), a.ap(), cw[0][:, sl], ALU.mult, [a, cw[0]], [a], e=e)
                for j in range(1, 4):
                    S.tt(xw[j].ap(), xw[j].ap(), cw[j][:, sl], ALU.mult, [xw[j], cw[j]], [xw[j]], e=e)
                    S.tt(a.ap(), a.ap(), xw[j].ap(), ALU.add, [a, xw[j]], [a], e=e)
                S.af(a.ap(), a.ap(), AF.Silu, [a], [a])
                if nl2:
                    S.af(t2.ap(), a.ap(), AF.Square, [a], [t2])
                    S.red(ssq.ap(), t2.ap().rearrange("p (h d) -> p h d", h=8), ALU.add, [t2], [ssq])
                    S.af(ssq.ap(), ssq.ap(), AF.Sqrt, [ssq], [ssq], bias=EPS)
                    S.recip(rs.ap(), ssq.ap(), [ssq], [rs])
                    if qs:
                        S.ts(rs.ap(), rs.ap(), 128.0 ** -0.5, ALU.mult, [rs], [rs])
                    S.tt(a.ap().rearrange("p (h d) -> p h d", h=8), a.ap().rearrange("p (h d) -> p h d", h=8),
                         bc(rs.ap(), 2, [128, 8, 128]), ALU.mult, [a, rs], [a])
                if pi < 2:
                    off = 0 if pi == 0 else 512
                    for g in range(2):
                        S.dma(GD[r0:r0 + 128, g * 3088 + off:g * 3088 + off + 512], a[:, 512 * g:512 * g + 512], [a], [], q="pool")
                elif pi == 2:
                    S.dma(GD[r0:r0 + 128, 1024:1536], a, [a], [], q="pool")
                else:
                    S.dma(GD[r0:r0 + 128, 1536:2048], a[:, 0:512], [a], [], q="pool")
                    S.dma(GD[r0:r0 + 128, 3088 + 1024:3088 + 2048], a[:, 512:1536], [a], [], q="pool")
        S.pop()

    def phase_r1():
        import os as _os
        S.push()
        Ublk = S.sb("Ublk", [128, 128], F32)
        SLblk = S.sb("SLblk", [128, 128], F32)
        rep = S.sb("rep", [128, 3, 64], F32)
        Ibf2 = S.sb("Ibf2", [128, 64], BF16)
        Sg = S.sb("Sg", [128, 16, 128], F32)
        Sgb = [S.sb("Sgb%d" % i, [128, 16, 128], BF16) for i in range(2)]
        gd = [S.sb("gd%d" % i, [128, 3088], F32) for i in range(3)]
        mg = [S.sb("mg%d" % i, [128, 1024], BF16) for i in range(2)]
        eGG = S.sb("eGG", [128, 16], F32)
        gg2 = S.sb("gg2", [128, 16], F32)
        eGlast = [S.sb("eGlast%d" % i, [128, 16], F32) for i in range(2)]
        gU = S.sb("gU", [128, 8, 64], F32)
        E = S.sb("E", [128, 8, 64], F32)
        qkb = S.sb("qkb", [128, 1024], BF16)
        qkT = S.sb("qkT", [128, 8, 2, 64], BF16)
        T1 = S.sb("T1", [128, 8, 64], F32)
        X = [S.sb("X%d" % i, [128, 8, 64], BF16) for i in range(2)]
        Y = [S.sb("Y%d" % i, [128, 8, 64], BF16) for i in range(2)]
        Q = [S.sb("Q%d" % i, [128, 8, 64], BF16) for i in range(2)]
        TT = [S.sb("TT%d" % i, [128, 8, 64], BF16) for i in range(2)]
        attnT = [S.sb("attnT%d" % i, [128, 8, 64], BF16) for i in range(2)]
        nbeta = S.sb("nbeta", [128, 8], F32)
        IeG2 = S.sb("IeG2", [128, 16, 64], F32)
        kTt = [S.sb("kTt%d" % i, [128, 16, 64], BF16) for i in range(2)]
        qTt = [S.sb("qTt%d" % i, [128, 16, 64], BF16) for i in range(2)]
        khat = [S.sb("khat%d" % i, [128, 8, 128], BF16) for i in range(2)]
        yv = S.sb("yv", [128, 8, 128], BF16)
        vnew = S.sb("vnew", [128, 8, 128], BF16)
        osq = S.sb("osq", [128, 8, 128], F32)
        oss = S.sb("oss", [128, 8], F32)
        orstd = S.sb("orstd", [128, 8], F32)
        on = S.sb("on", [128, 8, 128], F32)
        pB = [S.ps("pB%d" % i, [128, 8, 128], F32) for i in range(2)]
        pA = [S.ps("pAa%d" % i, [128, 512], F32) for i in range(4)]
        S.dma(Ublk, c_Ublk, [], [Ublk])
        S.dma(SLblk, c_SLblk, [], [SLblk])
        S.dma(rep, c_rep, [], [rep])
        S.cp(Ibf2.ap(), rep[:, 2, :], [rep], [Ibf2])
        for t_ in gd + [gg2, eGG, gU, E, T1, nbeta, qkb, yv, vnew, osq, oss, orstd, on] + X + Y + Q + TT + attnT + khat + mg + eGlast:
            S.memset(t_, 0.0)
        for t_ in [IeG2] + kTt + qTt + [qkT]:
            S.memset(t_, 0.0, e="pool")
        for t_ in pB + pA:
            S.memset(t_, 0.0)
        U2, SU2, I2 = rep[:, 0, :], rep[:, 1, :], rep[:, 2, :]
        zero_gap_rows(MG1, 2048, BF16)

        chunks = []
        for si, sq in enumerate(seqs):
            row = sq["start"]
            for ci, c in enumerate(sq["chunks"]):
                chunks.append((si, ci, c, row, ci == len(sq["chunks"]) - 1))
                row += c

        def part_a(n):
            si, ci, c, row, last = chunks[n]
            b = n % 2
            g_ = gd[n % 3]
            for g in range(2):
                S.dma(g_[64 * g:64 * g + c, :], GD[row:row + c, 3088 * g:3088 * g + 3088], [], [g_])
            yield
            beta = g_[:, 3072:3080]
            gg = g_[:, 3080:3088]
            kn4 = g_[:, 512:1024].rearrange("p (j d) -> p j d", j=4)
            a0, a1, a2, a3 = pA
            S.mm(a3[:, 0:8], Ublk.ap(), gg, [Ublk, g_], [a3])
            S.mm(a3[:, 8:16], SLblk.ap(), gg, [SLblk, g_], [a3])
            for g in range(2):
                S.cp(gg2[64 * g:64 * g + 64, 8 * g:8 * g + 8], gg[64 * g:64 * g + 64, :], [g_], [gg2], e="pool")
            S.mm(a3[:, 16:32], ones32.ap(), gg2.ap(), [ones32, gg2], [a3])
            S.tt(gU.ap(), bc(U2, 1, [128, 8, 64]), bc(gg, 2, [128, 8, 64]), ALU.mult, [rep, g_], [gU], e="pool")
            S.ts(nbeta.ap(), beta, -1.0, ALU.mult, [g_], [nbeta], e="pool")
            S.af(eGG.ap(), a3[:, 0:16], AF.Exp, [a3], [eGG])
            S.af(eGlast[b].ap(), a3[:, 16:32], AF.Exp, [a3], [eGlast[b]])
            S.cp(qkb.ap(), g_[:, 0:1024], [g_], [qkb], e="act")
            yield
            if _os.environ.get("R1_NODMT", "") != "1":
                S.mm(a0[:, 0:8 * c], SLblk.ap(), gU[:, :, 0:c], [SLblk, gU], [a0])
            pTv = a1.ap().bitcast(BF16).rearrange("p (j a t) -> p j a t", j=8, a=2)
            for g in range(2):
                ps_ = slice(64 * g, 64 * g + c)
                for jl in range(4):
                    j = 4 * g + jl
                    if g == 0:
                        S.tr(pTv[:, j, 0, 0:c], qkb[ps_, 512 + jl * 128:512 + (jl + 1) * 128], Ibf2[ps_, 0:c], [qkb, Ibf2], [a1])
                        S.tr(pTv[:, j, 1, 0:c], qkb[ps_, jl * 128:(jl + 1) * 128], Ibf2[ps_, 0:c], [qkb, Ibf2], [a1])
                    else:
                        for dh in range(2):
                            do = slice(64 * dh, 64 * dh + 64)
                            S.tr(pTv[do, j, 0, 0:c], qkb[ps_, 512 + jl * 128 + 64 * dh:512 + jl * 128 + 64 * dh + 64], Ibf2[ps_, 0:c], [qkb, Ibf2], [a1])
                            S.tr(pTv[do, j, 1, 0:c], qkb[ps_, jl * 128 + 64 * dh:jl * 128 + 64 * dh + 64], Ibf2[ps_, 0:c], [qkb, Ibf2], [a1])
            S.af(E[:, :, 0:c], a0[:, 0:8 * c].rearrange("p (h t) -> p h t", h=8), AF.Exp, [a0], [E])
            S.cp(qkT[:, :, :, 0:c], pTv[:, :, :, 0:c], [a1], [qkT])
            yield
            for g in range(2):
                ps_ = slice(64 * g, 64 * g + c)
                for jl in range(4):
                    j = 4 * g + jl
                    S.mm(a2[ps_, jl * 128:jl * 128 + 2 * c], qkT[:, j, 0, 0:c], qkT[:, j, :, 0:c], [qkT], [a2])
            S.tt(T1[:, :, 0:c], E[:, :, 0:c], bc(SU2[:, 0:c], 1, [128, 8, c]), ALU.mult, [E, rep], [T1], e="pool")
            S.tt(E[:, :, 0:c], E[:, :, 0:c], bc(U2[:, 0:c], 1, [128, 8, c]), ALU.mult, [E, rep], [E], e="pool")
            for g in range(2):
                S.tt(IeG2[64 * g:64 * g + 64, 8 * g:8 * g + 8, 0:c], bc(I2[64 * g:64 * g + 64, 0:c], 1, [64, 8, c]),
                     bc(eGG[64 * g:64 * g + 64, 0:8], 2, [64, 8, c]), ALU.mult, [rep, eGG], [IeG2], e="pool")
            yield
            pv = a2[:, :].rearrange("p (j x) -> p j x", j=4)[:, :, 0:2 * c].rearrange("p j (a t) -> p j a t", a=2)
            for r_ in range(2):
                hs = slice(r_, 8, 2)
                S.tt(T1[:, hs, 0:c], T1[:, hs, 0:c], pv[:, :, 0, :], ALU.mult, [T1, a2], [T1])
                S.tt(attnT[b][:, hs, 0:c], E[:, hs, 0:c], pv[:, :, 1, :], ALU.mult, [E, a2], [attnT[b]])
            X0, Y0, Q0 = X[0], Y[0], Q[0]
            S.tt(X0[:, :, 0:c], T1[:, :, 0:c], bc(nbeta.ap(), 2, [128, 8, c]), ALU.mult, [T1, nbeta], [X0])
            for hf in range(2):
                pp = pA[hf]
                S.mm(pp[:, 0:8 * c], ones32.ap(), IeG2[:, 8 * hf:8 * hf + 8, 0:c], [ones32, IeG2], [pp])
            yield
            pYv = a3.ap().bitcast(BF16)
            for g in range(2):
                ps_ = slice(64 * g, 64 * g + c)
                for hl in range(8):
                    S.tr(pYv[ps_, hl * c:(hl + 1) * c], X0[ps_, hl, 0:c], Ibf2[ps_, 0:c], [X0, Ibf2], [a3])
            S.cp(Y0[:, :, 0:c], pYv[:, 0:8 * c].rearrange("p (h t) -> p h t", h=8), [a3], [Y0], e="act")
            S.tt(Q0[:, :, 0:c], X0[:, :, 0:c], bc(I2[:, 0:c], 1, [128, 8, c]), ALU.add, [X0, rep], [Q0])
            for hf in range(2):
                pp = pA[hf]
                ev = pp[:, 0:8 * c].rearrange("p (h t) -> p h t", h=8)
                for r_ in range(2):
                    hsl = slice(8 * hf + r_, 8 * hf + 8, 2)
                    jsl = slice(4 * hf, 4 * hf + 4)
                    S.tt(kTt[b][:, hsl, 0:c], ev[:, r_:8:2, :], qkT[:, jsl, 0, 0:c], ALU.mult, [pp, qkT], [kTt[b]])
                    S.tt(qTt[b][:, hsl, 0:c], ev[:, r_:8:2, :], qkT[:, jsl, 1, 0:c], ALU.mult, [pp, qkT], [qTt[b]])
            for r_ in range(2):
                S.tt(khat[b][:, r_:8:2, :], kn4, bc(eGG[:, 8 + r_:16:2], 2, [128, 4, 128]), ALU.mult, [g_, eGG], [khat[b]], e="pool")
            nq = 5 if c == 64 else 3
            for lv in range(nq):
                yield
                Xc, Yc, Qc = X[lv % 2], Y[lv % 2], Q[lv % 2]
                Xn, Yn, Qn = X[(lv + 1) % 2], Y[(lv + 1) % 2], Q[(lv + 1) % 2]
                py, px, pq = a2, a3, (a0 if lv % 2 == 0 else a1)
                for hl in range(8):
                    for g in range(2):
                        ps_ = slice(64 * g, 64 * g + c)
                        S.mm(py[ps_, hl * c:(hl + 1) * c], Xc[ps_, hl, 0:c], Yc[ps_, hl, 0:c], [Xc, Yc], [py])
                S.cp(Yn[:, :, 0:c], py[:, 0:8 * c].rearrange("p (h t) -> p h t", h=8), [py], [Yn], e="act")
                if lv < nq - 1:
                    for hl in range(8):
                        for g in range(2):
                            ps_ = slice(64 * g, 64 * g + c)
                            S.mm(px[ps_, hl * c:(hl + 1) * c], Yc[ps_, hl, 0:c], Xc[ps_, hl, 0:c], [Xc, Yc], [px])
                    S.cp(Xn[:, :, 0:c], px[:, 0:8 * c].rearrange("p (h t) -> p h t", h=8), [px], [Xn], e="dve")
                for hl in range(8):
                    for g in range(2):
                        ps_ = slice(64 * g, 64 * g + c)
                        S.mm(pq[ps_, hl * c:(hl + 1) * c], Yn[ps_, hl, 0:c], Qc[ps_, hl, 0:c], [Yn, Qc], [pq])
                dst = TT[b] if lv == nq - 1 else Qn
                S.tt(dst[:, :, 0:c], Qc[:, :, 0:c], pq[:, 0:8 * c].rearrange("p (h t) -> p h t", h=8), ALU.add, [Qc, pq], [dst])

        sgi = [0]

        def part_b(n):
            si, ci, c, row, last = chunks[n]
            sq = seqs[si]
            b = n % 2
            g_, mgo = gd[n % 3], mg[b]
            if ci == 0:
                if sq["init"] is None:
                    S.memset(Sg, 0.0)
                else:
                    S.dma(Sg, st_gS[sq["init"]].rearrange("h k v -> k h v"), [], [Sg])
                S.cp(Sgb[sgi[0] % 2].ap(), Sg.ap(), [Sg], [Sgb[sgi[0] % 2]], e="act")
            Sb = Sgb[sgi[0] % 2]
            Sbn = Sgb[(sgi[0] + 1) % 2]
            sgi[0] += 1
            v3 = g_[:, 1024:2048].rearrange("p (h v) -> p h v", h=8)
            gz = g_[:, 2048:3072]
            beta = g_[:, 3072:3080]
            for hl in range(8):
                for g in range(2):
                    ps_ = slice(64 * g, 64 * g + c)
                    S.mm(pB[0][ps_, hl, :], kTt[b][:, 8 * g + hl, 0:c], Sb[:, 8 * g + hl, :], [kTt[b], Sb], [pB[0]])
            S.tt(yv.ap(), v3, pB[0].ap(), ALU.subtract, [g_, pB[0]], [yv])
            yield
            for hl in range(8):
                for g in range(2):
                    ps_ = slice(64 * g, 64 * g + c)
                    S.mm(pB[1][ps_, hl, :], TT[b][ps_, hl, 0:c], yv[ps_, hl, :], [TT[b], yv], [pB[1]])
            S.tt(vnew.ap(), pB[1].ap(), bc(beta, 2, [128, 8, 128]), ALU.mult, [pB[1], g_], [vnew])
            yield
            for g in range(2):
                ps_ = slice(64 * g, 64 * g + c)
                for hl in range(8):
                    S.mm(pB[g][:, hl, :], khat[b][ps_, hl, :], vnew[ps_, hl, :], [khat[b], vnew], [pB[g]])
                for hl in range(8):
                    hh = 8 * g + hl
                    S.stt(Sg[:, hh, :], Sg[:, hh, :], eGlast[b][:, hh:hh + 1], pB[g][:, hl, :], ALU.mult, ALU.add, [Sg, eGlast[b], pB[g]], [Sg])
            if not last:
                S.cp(Sbn.ap(), Sg.ap(), [Sg], [Sbn], e="act")
            yield
            for hl in range(8):
                for g in range(2):
                    ps_ = slice(64 * g, 64 * g + c)
                    S.mm(pB[0][ps_, hl, :], qTt[b][:, 8 * g + hl, 0:c], Sb[:, 8 * g + hl, :], [qTt[b], Sb], [pB[0]], start=True, stop=False)
                    S.mm(pB[0][ps_, hl, :], attnT[b][ps_, hl, 0:c], vnew[ps_, hl, :], [attnT[b], vnew], [pB[0]], start=False, stop=True)
            S.af(osq.ap(), pB[0].ap(), AF.Square, [pB[0]], [osq])
            yield
            S.red(oss.ap(), osq.ap(), ALU.add, [osq], [oss])
            S.af(oss.ap(), oss.ap(), AF.Ln, [oss], [oss], scale=1.0 / 128.0, bias=EPS)
            S.af(orstd.ap(), oss.ap(), AF.Exp, [oss], [orstd], scale=-0.5)
            for hl in range(8):
                S.af(on[:, hl, :], pB[0][:, hl, :], AF.Copy, [pB[0], orstd], [on], scale=orstd[:, hl:hl + 1])
            yield
            S.tt(mgo.ap(), on.ap().rearrange("p h v -> p (h v)"), gz, ALU.mult, [on, g_], [mgo])
            for g in range(2):
                S.dma(MG1[row:row + c, 1024 * g:1024 * g + 1024], mgo[64 * g:64 * g + c, :], [mgo], [], q="pool")
            if last:
                S.dma(o_gS[si].rearrange("h k v -> k h v"), Sg, [Sg], [], q="pool")

        def interleave(gens):
            gens = [g for g in gens if g is not None]
            while gens:
                for g in list(gens):
                    try:
                        next(g)
                    except StopIteration:
                        gens.remove(g)

        _mode = _os.environ.get("R1_MODE", "")
        _lim = int(_os.environ.get("R1_LIM", "99"))

        def lim(gen):
            for i, _ in enumerate(gen):
                if i + 1 >= _lim:
                    break
                yield

        if _mode == "":
            interleave([part_a(0)])
        for n in range(len(chunks)):
            if _mode == "":
                interleave([part_b(n), part_a(n + 1) if n + 1 < len(chunks) else None])
            elif _mode == "a_only":
                interleave([lim(part_a(n))])
        S.pop()

    phase_d1_even()
    phase_r0()
    phase_d3ab(0, MG0, 1024, even_w_out, xin)
    phase_d3c(0, X2, False)
    phase_d1_odd()
    phase_c1()
    phase_r1()
    phase_d3ab(1, MG1, 2048, odd_w_out, X2)
    phase_d3c(1, y, True)
    S.finish()
    return nc, seqs, NT, S


def make_consts():
    idx = np.arange(64)
    U = (idx[:, None] <= idx[None, :]).astype(np.float32)
    SU = (idx[:, None] < idx[None, :]).astype(np.float32)
    SL = (idx[:, None] > idx[None, :]).astype(np.float32)
    bm4 = np.zeros((4, 4, 64), np.float32)
    for r in range(4):
        bm4[r, r, :] = 1.0
    Z = np.zeros((64, 64), np.float32)
    Ublk = np.block([[U, Z], [Z, U]])
    SLblk = np.block([[SL, Z], [Z, SL]])
    I64 = np.eye(64, dtype=np.float32)
    rep = np.stack([np.concatenate([U, U], 0), np.concatenate([SU, SU], 0), np.concatenate([I64, I64], 0)], 1)
    return dict(c_ident=np.eye(128, dtype=np.float32), c_U=U, c_SU=SU, c_SL=SL, c_bm4=bm4,
                c_Ublk=np.ascontiguousarray(Ublk), c_SLblk=np.ascontiguousarray(SLblk), c_rep=np.ascontiguousarray(rep))


def make_in_maps(inp, T, NS, n_cores, seqs, NT):
    f = lambda a: np.ascontiguousarray(np.asarray(a, dtype=np.float32))
    xp = f(inp["x_prompt"])
    xs = f(inp["x_sample"])
    meta = f(inp["meta_tokens"])
    consts = make_consts()
    shared = dict(
        norm_mix=f(inp["norm_mix"]), norm_ffn=f(inp["norm_ffn"]), norm_final=f(inp["norm_final"]).reshape(1, D),
        even_w_in=f(inp["even_w_in"])[0], even_w_out=f(inp["even_w_out"])[0], lb_logits=f(inp["hgrn_lb_logits"]),
        hgrn_norm=f(inp["hgrn_norm"])[0:1], mlstm_bif=np.concatenate([f(inp["mlstm_b_i"])[0], f(inp["mlstm_b_f"])[0]]).reshape(1, 8),
        mlstm_norm=f(inp["mlstm_norm"])[0:1], odd_w_in=f(inp["odd_w_in"])[0], odd_conv_w=f(inp["odd_conv_w"])[0],
        gdn_a_log=f(inp["gdn_a_log"])[0:1], gdn_dt_bias=f(inp["gdn_dt_bias"])[0:1], gdn_norm=f(inp["gdn_norm"])[0:1],
        odd_w_out=f(inp["odd_w_out"])[0], ffn_w_in=f(inp["ffn_w_in"]), ffn_w_out=f(inp["ffn_w_out"]), **consts)
    B = xp.shape[0]
    maps = []
    for c in range(n_cores):
        xin = np.zeros((NT, D), np.float32)
        s0 = seqs[0]["start"]
        if c < B:
            xin[s0:s0 + 16] = meta
            xin[s0 + 16:s0 + 16 + T] = xp[c]
        sidx = [c * NS + i for i in range(NS)]
        for i, sj in enumerate(sidx):
            st = seqs[1 + i]["start"]
            if sj < xs.shape[0]:
                xin[st:st + 64] = xs[sj]

        def gather(a):
            a = f(a)[0]
            out = np.zeros((NS,) + a.shape[1:], np.float32)
            for i, sj in enumerate(sidx):
                if sj < a.shape[0]:
                    out[i] = a[sj]
            return out
        m = dict(shared)
        m.update(xin=xin, st_hS=gather(inp["state_hgrn_S"]), st_mC=gather(inp["state_mlstm_C"]), st_mn=gather(inp["state_mlstm_n"]),
                 st_mm=gather(inp["state_mlstm_m"]), st_gS=gather(inp["state_gdn_S"]), st_gc=gather(inp["state_gdn_conv"]))
        maps.append(m)
    return maps


_CACHE = {}


def run(inp, T, NS, n_cores, debug=False, trace=False):
    key = (T, NS, debug)
    if key not in _CACHE:
        _CACHE[key] = build(T, NS, debug)
    nc, seqs, NT, S = _CACHE[key]
    maps = make_in_maps(inp, T, NS, n_cores, seqs, NT)
    res = run_bass_kernel_spmd(nc, maps, core_ids=list(range(n_cores)), **({"trace": True} if trace else {}))
    return res, seqs, NT


def assemble(res, seqs, NT, B, T, NSAMP, NS, n_cores):
    R = res.results
    s0 = seqs[0]["start"]
    y_prompt = np.stack([R[c]["y"][s0 + 16:s0 + 16 + T] for c in range(B)], 0)
    y_sample = np.zeros((NSAMP, 64, D), np.float32)
    names = ["o_hS", "o_mC", "o_mn", "o_mm", "o_gS", "o_gc"]
    pst = {n: np.stack([R[c][n][0] for c in range(B)], 0)[None] for n in names}
    sst = {n: np.zeros((1, NSAMP) + R[0][n].shape[1:], np.float32) for n in names}
    for c in range(n_cores):
        for i in range(NS):
            sj = c * NS + i
            if sj >= NSAMP:
                continue
            st = seqs[1 + i]["start"]
            y_sample[sj] = R[c]["y"][st:st + 64]
            for n in names:
                sst[n][0, sj] = R[c][n][1 + i]
    outs = [y_prompt, y_sample] + [pst[n] for n in names] + [sst[n] for n in names]
    return tuple(np.ascontiguousarray(o.astype(np.float32)) for o in outs)


def kernel(**inputs):
    B = inputs["x_prompt"].shape[0]
    T = inputs["x_prompt"].shape[1]
    NSAMP = inputs["x_sample"].shape[0]
    NS = -(-NSAMP // N_CORES)
    res, seqs, NT = run(inputs, T, NS, N_CORES)
    return assemble(res, seqs, NT, B, T, NSAMP, NS, N_CORES)
```
